# Optimizing a Trainium2 kernel written in Bass

```python
import math
import jax, jax.numpy as jnp
from jax import lax
import numpy as np

D_MODEL = 1024
BATCH = 16
SEQ = 4096
DEPTH = 4

MEM_LEN = 256
A_HEADS = 4
A_QK_DIM = 64
A_V_DIM = 2 * A_QK_DIM
B_HEADS = 4
B_HEAD_DIM = 64
DILATED_PATTERNS = ((128, 1), (512, 4), (2048, 16))
DILATED_BLK = 128
C_HEADS = 4
C_HEAD_DIM = 64
IDX_HEADS = 8
IDX_DIM = 64
TOPK_MAX = 256
IN_SPLITS = (
    A_HEADS * 2 * A_QK_DIM,
    A_HEADS * 2 * A_QK_DIM,
    A_HEADS * A_V_DIM,
    B_HEADS * B_HEAD_DIM,
    B_HEADS * B_HEAD_DIM,
    B_HEADS * B_HEAD_DIM,
    C_HEADS * C_HEAD_DIM,
    C_HEADS * C_HEAD_DIM,
    C_HEADS * C_HEAD_DIM,
    IDX_HEADS * IDX_DIM,
    IDX_DIM,
    IDX_HEADS,
)
N_IN = 3656
MIX_WIDTH = A_HEADS * A_V_DIM + B_HEADS * B_HEAD_DIM + C_HEADS * C_HEAD_DIM
REL_BUCKETS = 32
REL_MAX_DIST = 2048
N_BIAS_HEADS = A_HEADS + B_HEADS + C_HEADS
MEM_HEADS = 4
MEM_HEAD_DIM = D_MODEL // MEM_HEADS
D_FF = 2816
CONV_WIDTH = 3

Q_BLK = 128
EPS = 1e-6
NEG = -1e30

kernel_name = "hybrid_diff_dilated_dsa_trunk"


def rmsnorm(x, g):
    x32 = x.astype(jnp.float32)
    y = x32 * lax.rsqrt(jnp.mean(x32 * x32, axis=-1, keepdims=True) + EPS)
    return (y * g.astype(jnp.float32)).astype(x.dtype)


def rel_bucket(dist):
    n = jnp.maximum(dist, 0)
    max_exact = REL_BUCKETS // 2
    nf = jnp.maximum(n, 1).astype(jnp.float32)
    large = max_exact + (jnp.log(nf / max_exact) / math.log(REL_MAX_DIST / max_exact)
                         * (REL_BUCKETS - max_exact)).astype(jnp.int32)
    large = jnp.minimum(large, REL_BUCKETS - 1)
    return jnp.where(n < max_exact, n, large)


def diff_attention(q, k, v, lam, lam_init, subln_g, table):
    Bn, S, H, _, d = q.shape
    nb = S // Q_BLK
    scale = d ** -0.5
    kpos = jnp.arange(S, dtype=jnp.int32)
    v32 = v.astype(jnp.float32)

    def block(args):
        qb, start = args
        qpos = start + jnp.arange(Q_BLK, dtype=jnp.int32)
        s = jnp.einsum('bqhmd,bkhmd->bhmqk', qb, k).astype(jnp.float32) * scale
        dist = qpos[:, None] - kpos[None, :]
        bias = jnp.transpose(table[rel_bucket(dist)].astype(jnp.float32), (2, 0, 1))
        s = jnp.where(dist >= 0, s + bias[None, :, None], NEG)
        p = jax.nn.softmax(s, axis=-1)
        attn = p[:, :, 0] - lam * p[:, :, 1]
        return jnp.einsum('bhqk,bkhd->bqhd', attn, v32)

    qb = jnp.moveaxis(q.reshape(Bn, nb, Q_BLK, H, 2, d), 1, 0)
    starts = jnp.arange(nb, dtype=jnp.int32) * Q_BLK
    o = lax.map(block, (qb, starts))
    o = jnp.moveaxis(o, 0, 1).reshape(Bn, S, H, v.shape[-1])
    return rmsnorm(o, subln_g) * (1.0 - lam_init)


def dilated_branch(q, k, v, window, r, table, scale):
    Bn, S, H, d = q.shape
    n_back = window // r
    unit = r * DILATED_BLK
    S_pad = -(-S // unit) * unit
    nb = S_pad // unit
    pad = ((0, 0), (0, S_pad - S), (0, 0), (0, 0))
    qb = jnp.pad(q, pad).reshape(Bn, nb, DILATED_BLK, r, H, d)
    kb = jnp.pad(k, pad).reshape(Bn, nb, DILATED_BLK, r, H, d)
    vb = jnp.pad(v, pad).reshape(Bn, nb, DILATED_BLK, r, H, d)

    def with_prev(t):
        prev = jnp.pad(t, ((0, 0), (1, 0), (0, 0), (0, 0), (0, 0), (0, 0)))[:, :nb]
        return jnp.concatenate([prev, t], axis=2)

    kk = with_prev(kb)
    vv = with_prev(vb).astype(jnp.float32)
    s = jnp.einsum('bnqchd,bnkchd->bnchqk', qb, kk).astype(jnp.float32) * scale
    qi = jnp.arange(DILATED_BLK, dtype=jnp.int32)[:, None]
    ki = jnp.arange(2 * DILATED_BLK, dtype=jnp.int32)[None, :]
    dist_m = qi + DILATED_BLK - ki
    m_q = jnp.arange(nb, dtype=jnp.int32)[:, None, None] * DILATED_BLK + qi[None]
    mask = (dist_m >= 0) & (dist_m <= n_back) & (m_q - dist_m >= 0)
    bias = jnp.transpose(table[rel_bucket(dist_m * r)].astype(jnp.float32), (2, 0, 1))
    s = jnp.where(mask[None, :, None, None], s + bias, NEG)
    lse = jax.nn.logsumexp(s, axis=-1)
    p = jnp.exp(s - lse[..., None])
    o = jnp.einsum('bnchqk,bnkchd->bnqchd', p, vv).reshape(Bn, S_pad, H, d)[:, :S]
    lse = jnp.transpose(lse, (0, 1, 4, 2, 3)).reshape(Bn, S_pad, H)[:, :S]
    return o, lse


def dilated_attention(q, k, v, table):
    scale = q.shape[-1] ** -0.5
    outs, lses = [], []
    for window, r in DILATED_PATTERNS:
        o, lse = dilated_branch(q, k, v, window, r, table, scale)
        outs.append(o)
        lses.append(lse)
    wts = jax.nn.softmax(jnp.stack(lses, 0), axis=0)
    return jnp.einsum('pbsh,pbshd->bshd', wts, jnp.stack(outs, 0))


def dsa_attention(q, k, v, iq, ik, iw, table):
    Bn, S, H, d = q.shape
    nb = S // Q_BLK
    topk = min(TOPK_MAX, S // 4)
    scale = d ** -0.5
    kpos = jnp.arange(S, dtype=jnp.int32)
    iw = iw.astype(jnp.float32) * (IDX_HEADS ** -0.5)

    def block(args):
        qb, iqb, iwb, start = args
        qpos = start + jnp.arange(Q_BLK, dtype=jnp.int32)
        logits = jnp.einsum('bqhd,bkd->bqhk', iqb, ik).astype(jnp.float32) * (IDX_DIM ** -0.5)
        score = jnp.einsum('bqhk,bqh->bqk', jax.nn.relu(logits), iwb)
        score = jnp.where(kpos[None, None, :] <= qpos[None, :, None], score, NEG)
        _, idx = lax.top_k(score, topk)
        valid = idx <= qpos[None, :, None]
        flat = idx.reshape(Bn, Q_BLK * topk)
        ks = jax.vmap(lambda kb, ib: kb[ib])(k, flat).reshape(Bn, Q_BLK, topk, H, d)
        vs = jax.vmap(lambda vb, ib: vb[ib])(v, flat).reshape(Bn, Q_BLK, topk, H, d)
        s = jnp.einsum('bqhd,bqkhd->bhqk', qb, ks).astype(jnp.float32) * scale
        bias = jnp.transpose(table[rel_bucket(qpos[None, :, None] - idx)].astype(jnp.float32), (0, 3, 1, 2))
        s = jnp.where(valid[:, None], s + bias, NEG)
        p = jax.nn.softmax(s, axis=-1)
        return jnp.einsum('bhqk,bqkhd->bqhd', p, vs.astype(jnp.float32))

    qb = jnp.moveaxis(q.reshape(Bn, nb, Q_BLK, H, d), 1, 0)
    iqb = jnp.moveaxis(iq.reshape(Bn, nb, Q_BLK, IDX_HEADS, IDX_DIM), 1, 0)
    iwb = jnp.moveaxis(iw.reshape(Bn, nb, Q_BLK, IDX_HEADS), 1, 0)
    starts = jnp.arange(nb, dtype=jnp.int32) * Q_BLK
    o = lax.map(block, (qb, iqb, iwb, starts))
    return jnp.moveaxis(o, 0, 1).reshape(Bn, S, H, d)


def hybrid_mixer(h, w_in, lam_q1, lam_k1, lam_q2, lam_k2, subln_g, w_out, rel_bias, layer_idx):
    Bn, S, _ = h.shape
    proj = h @ w_in
    (aq, ak, av, bq, bk, bv, cq, ck, cv, iq, ik, iw) = jnp.split(
        proj, np.cumsum(IN_SPLITS)[:-1].tolist(), axis=-1)
    lam_init = 0.8 - 0.6 * math.exp(-0.3 * layer_idx)
    lam = (jnp.exp(jnp.sum(lam_q1.astype(jnp.float32) * lam_k1.astype(jnp.float32)))
           - jnp.exp(jnp.sum(lam_q2.astype(jnp.float32) * lam_k2.astype(jnp.float32))) + lam_init)
    o_a = diff_attention(aq.reshape(Bn, S, A_HEADS, 2, A_QK_DIM), ak.reshape(Bn, S, A_HEADS, 2, A_QK_DIM),
                         av.reshape(Bn, S, A_HEADS, A_V_DIM), lam, lam_init, subln_g,
                         rel_bias[:, :A_HEADS])
    o_b = dilated_attention(bq.reshape(Bn, S, B_HEADS, B_HEAD_DIM), bk.reshape(Bn, S, B_HEADS, B_HEAD_DIM),
                            bv.reshape(Bn, S, B_HEADS, B_HEAD_DIM),
                            rel_bias[:, A_HEADS:A_HEADS + B_HEADS])
    o_c = dsa_attention(cq.reshape(Bn, S, C_HEADS, C_HEAD_DIM), ck.reshape(Bn, S, C_HEADS, C_HEAD_DIM),
                        cv.reshape(Bn, S, C_HEADS, C_HEAD_DIM), iq.reshape(Bn, S, IDX_HEADS, IDX_DIM),
                        ik, iw, rel_bias[:, A_HEADS + B_HEADS:])
    o = jnp.concatenate([o_a.reshape(Bn, S, -1), o_b.reshape(Bn, S, -1), o_c.reshape(Bn, S, -1)],
                        axis=-1).astype(h.dtype)
    return o @ w_out


def memory_attention(h, mem_n, w_mq, w_mkv, w_mo):
    Bn, S, D = h.shape
    q = (h @ w_mq).reshape(Bn, S, MEM_HEADS, MEM_HEAD_DIM)
    k, v = jnp.split(mem_n @ w_mkv, 2, axis=-1)
    k = k.reshape(Bn, -1, MEM_HEADS, MEM_HEAD_DIM)
    v = v.reshape(Bn, -1, MEM_HEADS, MEM_HEAD_DIM)
    s = jnp.einsum('bqhd,bkhd->bhqk', q, k).astype(jnp.float32) * (MEM_HEAD_DIM ** -0.5)
    p = jax.nn.softmax(s, axis=-1)
    o = jnp.einsum('bhqk,bkhd->bqhd', p, v.astype(jnp.float32)).astype(h.dtype).reshape(Bn, S, D)
    return o @ w_mo


def conv_ffn(h, w_up, conv_w, conv_b, w_down):
    S = h.shape[1]
    u = h @ w_up
    u_p = jnp.pad(u, ((0, 0), (CONV_WIDTH - 1, 0), (0, 0)))
    c = conv_b
    for j in range(CONV_WIDTH):
        c = c + conv_w[j] * u_p[:, j:j + S]
    gate, val = jnp.split(c, 2, axis=-1)
    return (jax.nn.silu(gate) * val) @ w_down


def setup_inputs(seed: int = 0) -> dict:
    key = jax.random.key(seed)
    ks = jax.random.split(key, 24)

    def nrm(k, shape, scale):
        return jax.random.normal(k, shape, jnp.float32) * scale

    L, D, F = DEPTH, D_MODEL, D_FF
    return {
        "x": nrm(ks[0], (BATCH, SEQ, D), 1.0),
        "mem": nrm(ks[1], (BATCH, MEM_LEN, D), 1.0),
        "rel_bias": nrm(ks[2], (REL_BUCKETS, N_BIAS_HEADS), 0.2),
        "norm_mix": 1.0 + nrm(ks[3], (L, D), 0.1),
        "w_in": nrm(ks[4], (L, D, N_IN), D ** -0.5),
        "lam_q1": nrm(ks[5], (L, A_QK_DIM), 0.1),
        "lam_k1": nrm(ks[6], (L, A_QK_DIM), 0.1),
        "lam_q2": nrm(ks[7], (L, A_QK_DIM), 0.1),
        "lam_k2": nrm(ks[8], (L, A_QK_DIM), 0.1),
        "subln": 1.0 + nrm(ks[9], (L, A_V_DIM), 0.1),
        "w_out": nrm(ks[10], (L, MIX_WIDTH, D), MIX_WIDTH ** -0.5),
        "norm_mem": 1.0 + nrm(ks[11], (L, D), 0.1),
        "norm_memkv": 1.0 + nrm(ks[12], (L, D), 0.1),
        "w_mq": nrm(ks[13], (L, D, D), D ** -0.5),
        "w_mkv": nrm(ks[14], (L, D, 2 * D), D ** -0.5),
        "w_mo": nrm(ks[15], (L, D, D), D ** -0.5),
        "norm_ffn": 1.0 + nrm(ks[16], (L, D), 0.1),
        "w_up": nrm(ks[17], (L, D, 2 * F), D ** -0.5),
        "conv_w": nrm(ks[18], (L, CONV_WIDTH, 2 * F), CONV_WIDTH ** -0.5),
        "conv_b": nrm(ks[19], (L, 2 * F), 0.02),
        "w_down": nrm(ks[20], (L, F, D), F ** -0.5),
        "norm_final": 1.0 + nrm(ks[21], (D,), 0.1),
    }


def reference(x, mem, rel_bias, norm_mix, w_in, lam_q1, lam_k1, lam_q2, lam_k2, subln, w_out,
              norm_mem, norm_memkv, w_mq, w_mkv, w_mo, norm_ffn, w_up, conv_w, conv_b, w_down,
              norm_final):
    for l in range(DEPTH):
        h = rmsnorm(x, norm_mix[l])
        x = x + hybrid_mixer(h, w_in[l], lam_q1[l], lam_k1[l], lam_q2[l], lam_k2[l], subln[l],
                             w_out[l], rel_bias, l)
        h = rmsnorm(x, norm_mem[l])
        x = x + memory_attention(h, rmsnorm(mem, norm_memkv[l]), w_mq[l], w_mkv[l], w_mo[l])
        h = rmsnorm(x, norm_ffn[l])
        x = x + conv_ffn(h, w_up[l], conv_w[l], conv_b[l], w_down[l])
    return rmsnorm(x, norm_final)
```

```python
import math
from contextlib import ExitStack
import numpy as np
import concourse.bass as bass
import concourse.mybir as mybir
from concourse.bass_utils import run_bass_kernel_spmd

F32 = mybir.dt.float32
BF16 = mybir.dt.bfloat16
AF = mybir.ActivationFunctionType
ALU = mybir.AluOpType
AX = mybir.AxisListType

D = 1024
KC = 8
NIN = 3656
FF = 2816
NFB = 22
MEM = 256
NG = 3200
MS = 3072
DOFF = 511
DFAR = 2176
EPS = 1e-6
NEGM = -30000.0
NBIS = 14


class Buf:
    __slots__ = ("name", "w", "r")

    def __init__(self, name=""):
        self.name = name
        self.w = {}
        self.r = {}


class Stream:
    def __init__(self, name):
        self.name = name
        self.ops = []
        self.cnt = 0
        self.dcnt = 0
        self.slot = [0] * NSLOT
        self.known = {}


NSLOT = 12


class Prog:
    STREAMS = ("pe", "act", "dve", "pool", "sp")

    def __init__(self, nc, es):
        self.nc = nc
        self.streams = {n: Stream(n) for n in self.STREAMS}
        self.sems = {}
        for n in self.STREAMS:
            self.sems[n] = es.enter_context(nc.semaphore("s_" + n))
        for n in ("pool", "sp"):
            for k in range(NSLOT):
                self.sems["%s.d%d" % (n, k)] = es.enter_context(nc.semaphore("d_%s%d" % (n, k)))
        self.nops = 0
        self.nwaits = 0

    def _deps(self, st, reads, writes, dma=False, extra=None):
        deps = {}
        own_d = st.name + ".d"

        def add(k, v):
            if k == st.name and k == "pe":
                return
            if deps.get(k, 0) < v:
                deps[k] = v
        for b in reads:
            for k, v in b.w.items():
                add(k, v)
        for b in writes:
            for k, v in b.w.items():
                if dma and k.startswith(own_d):
                    continue
                add(k, v)
            for k, v in b.r.items():
                add(k, v)
        if extra is not None:
            add(*extra)
        out = {}
        for k, v in deps.items():
            if st.known.get(k, 0) < v:
                st.known[k] = v
                out[k] = v
        return out

    def op(self, stream, fn, reads=(), writes=(), dma=False):
        st = self.streams[stream]
        if dma:
            slot = st.dcnt % NSLOT
            st.dcnt += 1
            kname = "%s.d%d" % (stream, slot)
            extra = (kname, st.slot[slot]) if st.slot[slot] else None
            waits = self._deps(st, reads, writes, True, extra)
            st.slot[slot] += 1
            key = (kname, st.slot[slot])
        else:
            waits = self._deps(st, reads, writes, False)
            st.cnt += 1
            key = (stream, st.cnt)
        st.ops.append((waits, fn, key[0] if dma else None))
        for b in reads:
            if b.r.get(key[0], 0) < key[1]:
                b.r[key[0]] = key[1]
        for b in writes:
            if dma:
                b.w = {k: v for k, v in b.w.items() if k.startswith(stream + ".d")}
                b.w[key[0]] = key[1]
            else:
                b.w = {key[0]: key[1]}
            b.r = {}
        self.nops += 1
        self.nwaits += len(waits)
        return key

    def flush(self):
        nc = self.nc
        sems = self.sems
        fin = {}
        for n, st in self.streams.items():
            if st.cnt:
                fin[n] = st.cnt
            for k in range(NSLOT):
                if st.slot[k]:
                    fin["%s.d%d" % (n, k)] = st.slot[k]

        def val(k, v):
            return v * 16 if ".d" in k else v

        def run(stname):
            st = self.streams[stname]
            ops = st.ops
            st.ops = []

            def body(e):
                for waits, fn, dkey in ops:
                    for k, v in waits.items():
                        e.wait_ge(sems[k], val(k, v))
                    ins = fn(e)
                    if dkey is not None:
                        ins.then_inc(sems[dkey], 16)
                    else:
                        ins.then_inc(sems[stname], 1)
                for k, v in fin.items():
                    if k == stname:
                        continue
                    if st.known.get(k, 0) < v:
                        e.wait_ge(sems[k], val(k, v))
                        st.known[k] = v
            return body

        with nc.Block() as block:
            block.tensor(run("pe"))
            block.scalar(run("act"))
            block.vector(run("dve"))
            block.gpsimd(run("pool"))
            block.sync(run("sp"))


class Rot:
    def __init__(self, tiles):
        self.t = [(t, Buf()) for t in tiles]
        self.i = 0

    def next(self):
        r = self.t[self.i % len(self.t)]
        self.i += 1
        return r


def rel_bucket_np(n):
    n = np.maximum(n, 0)
    nf = np.maximum(n, 1).astype(np.float32)
    large = 16 + (np.log(nf / np.float32(16)) / np.float32(math.log(2048 / 16)) * np.float32(16)).astype(np.int32)
    large = np.minimum(large, 31)
    return np.where(n < 16, n, large)


def host_consts():
    d = np.arange(NG, dtype=np.int64) - DOFF
    oh = np.zeros((34, NG), np.float32)
    bk = rel_bucket_np(d.astype(np.int32))
    valid = d >= 0
    oh[bk[valid], np.nonzero(valid)[0]] = 1.0
    oh[32] = np.where(valid, 0.0, NEGM)
    mult = ((d <= 128).astype(np.int64) + ((d % 4 == 0) & (d <= 512)) + ((d % 16 == 0) & (d <= 2048)))
    mult = np.where(valid, mult, 0)
    oh[33] = np.where(mult > 0, 8.0 * np.log(np.maximum(mult, 1)).astype(np.float32), NEGM)
    sel = np.zeros((2, 12), np.float32)
    sel[0, 0:4] = 1.0
    sel[0, 8:12] = 1.0
    sel[1, 4:8] = 1.0
    ident = np.eye(128, dtype=np.float32)
    J = np.ascontiguousarray(ident[::-1])
    tri = np.where(np.arange(128)[None, :] <= np.arange(128)[:, None], 0.0, -1e30).astype(np.float32)
    return {"c_onehot": oh, "c_sel": sel, "c_ident": ident, "c_J": J, "c_tri": tri}


def build(S, NSEQ, L, TOPK, debug_outs=False):
    nc = bass.Bass("TRN2", target_bir_lowering=False)
    NB = S // 128
    NT = S // 512
    NTOK = NSEQ * S

    NAMES = {}

    def din(name, shape, dt=F32):
        t = nc.dram_tensor(name, list(shape), dt, kind="ExternalInput")
        NAMES[id(t)] = name
        return t

    def dscr(name, shape, dt=BF16):
        t = nc.dram_tensor(name, list(shape), dt, kind="ExternalOutput" if debug_outs else "Internal")
        NAMES[id(t)] = name
        return t

    def nm(t):
        return NAMES[id(t)]

    x_in = din("x", [NTOK, D])
    mem_in = din("mem", [NSEQ * MEM, D])
    rel_bias = din("rel_bias", [32, 12])
    norm_mix = din("norm_mix", [L, D])
    w_in = din("w_in", [L, D, NIN])
    lam_q1 = din("lam_q1", [L, 64]); lam_k1 = din("lam_k1", [L, 64])
    lam_q2 = din("lam_q2", [L, 64]); lam_k2 = din("lam_k2", [L, 64])
    subln = din("subln", [L, 128])
    w_out = din("w_out", [L, D, D])
    norm_mem = din("norm_mem", [L, D]); norm_memkv = din("norm_memkv", [L, D])
    w_mq = din("w_mq", [L, D, D]); w_mkv = din("w_mkv", [L, D, 2 * D]); w_mo = din("w_mo", [L, D, D])
    norm_ffn = din("norm_ffn", [L, D])
    w_up = din("w_up", [L, D, 2 * FF]); conv_w = din("conv_w", [L, 3, 2 * FF]); conv_b = din("conv_b", [L, 2 * FF])
    w_down = din("w_down", [L, FF, D])
    norm_final = din("norm_final", [1, D])
    c_onehot = din("c_onehot", [34, NG]); c_sel = din("c_sel", [2, 12])
    c_ident = din("c_ident", [128, 128]); c_J = din("c_J", [128, 128]); c_tri = din("c_tri", [128, 128])
    y_out = nc.dram_tensor("y", [NTOK, D], F32, kind="ExternalOutput")

    QTA = dscr("QTA", [512, NSEQ, S]); KTA = dscr("KTA", [512, NSEQ, S])
    QTB = dscr("QTB", [256, NSEQ, S]); KTB = dscr("KTB", [256, NSEQ, S])
    QTC = dscr("QTC", [256, NSEQ, S]); KTC = dscr("KTC", [256, NSEQ, S])
    IQT = dscr("IQT", [512, NSEQ, S]); IKT = dscr("IKT", [64, NSEQ, S])
    VA = dscr("VA", [NTOK, 512]); VB = dscr("VB", [NTOK, 256]); VC = dscr("VC", [NTOK, 256])
    IW = dscr("IW", [NTOK, 8], F32)
    OS = dscr("OS", [NTOK, D])
    MASK = dscr("MASK", [S, S])
    XA = dscr("XA", [NTOK, D], F32); XB = dscr("XB", [NTOK, D], F32)
    GSCR = dscr("GSCR", [12, NG])
    ESCR = dscr("ESCR", [12, NG])

    def AP(t, offset, ap):
        return bass.AP(tensor=t, offset=offset, ap=[list(a) for a in ap])

    es = ExitStack()
    with es:
        P = Prog(nc, es)

        _uid = [0]

        def uq(name):
            _uid[0] += 1
            return "%s_%d" % (name, _uid[0])

        def sb(name, shape, dt=BF16, stack=es):
            return stack.enter_context(nc.sbuf_tensor(uq(name), list(shape), dt))

        def mm(out, lhsT, rhs, start, stop, r=(), w=()):
            P.op("pe", lambda e: e.matmul(out, lhsT=lhsT, rhs=rhs, start=start, stop=stop, skip_group_check=True), r, w)

        def tr(out, in_, ident, r=(), w=()):
            P.op("pe", lambda e: e.transpose(out, in_, ident), r, w)

        def act(out, in_, func, r=(), w=(), bias=None, scale=None, accum=None):
            kw = {}
            if bias is not None:
                kw["bias"] = bias
            if scale is not None:
                kw["scale"] = scale
            if accum is not None:
                kw["accum_out"] = accum
            P.op("act", lambda e: e.activation(out=out, in_=in_, func=func, **kw), r, w)

        def ts(eng, out, in0, s1, s2, op0, op1=None, r=(), w=(), accum=None):
            kw = {}
            if op1 is not None:
                kw["op1"] = op1
            if accum is not None:
                kw["accum_out"] = accum
            P.op(eng, lambda e: e.tensor_scalar(out=out, in0=in0, scalar1=s1, scalar2=s2, op0=op0, **kw), r, w)

        def tt(eng, out, in0, in1, op, r=(), w=()):
            P.op(eng, lambda e: e.tensor_tensor(out=out, in0=in0, in1=in1, op=op), r, w)

        def stt(out, in0, scalar, in1, op0, op1, r=(), w=()):
            P.op("dve", lambda e: e.scalar_tensor_tensor(out=out, in0=in0, scalar=scalar, in1=in1, op0=op0, op1=op1), r, w)

        def cp(eng, out, in_, r=(), w=()):
            if eng == "act":
                P.op("act", lambda e: e.activation(out=out, in_=in_, func=AF.Copy), r, w)
            else:
                P.op(eng, lambda e: e.tensor_copy(out=out, in_=in_), r, w)

        def red(out, in_, op, r=(), w=()):
            P.op("dve", lambda e: e.tensor_reduce(out=out, in_=in_, axis=AX.X, op=op), r, w)

        def recip(out, in_, r=(), w=()):
            P.op("dve", lambda e: e.reciprocal(out=out, in_=in_), r, w)

        def mset(eng, ap, val, r=(), w=()):
            P.op(eng, lambda e: e.memset(ap, val), r, w)

        def dma(q, out, in_, r=(), w=(), slow=False):
            if slow:
                P.op(q, lambda e: e.dma_start(out=out, in_=in_, allow_slow_non_contiguous=True), r, w, dma=True)
            else:
                P.op(q, lambda e: e.dma_start(out=out, in_=in_), r, w, dma=True)

        def rstd_from_ss(rs, ss, n, r, w):
            act(rs, ss, AF.Ln, r=r, w=w, scale=1.0 / n, bias=epsb[:, 0:1])
            act(rs, rs, AF.Exp, r=w, w=w, scale=-0.5)

        ident_f = sb("ident_f", [128, 128], F32)
        ident_b = sb("ident_b", [128, 128])
        J_b = sb("J_b", [128, 128])
        tri_f = sb("tri_f", [128, 128], F32)
        ones_b = sb("ones_b", [128, 128])
        epsb = sb("epsb", [128, 1], F32)
        pw2 = sb("pw2", [128, NBIS], F32)
        farb = sb("farb", [128, 12], F32)
        memhatT = sb("memhatT", [128, NSEQ, KC, MEM])
        B_const = Buf("const")
        B_memhat = Buf("memhat")


        with ExitStack() as ph:
            tmpf = sb("s0_tmpf", [128, 128], F32, ph)
            tbl = sb("s0_tbl", [34, 12], F32, ph)
            oh = sb("s0_oh", [34, NG], F32, ph)
            gv = sb("s0_gv", [12, NG], BF16, ph)
            ev = sb("s0_ev", [12, NG], BF16, ph)
            mt = sb("s0_mt", [128, D], F32, ph)
            mb = sb("s0_mb", [128, D], BF16, ph)
            sq = sb("s0_sq", [128, D], BF16, ph)
            ss = sb("s0_ss", [128, 1], F32, ph)
            rs = sb("s0_rs", [128, 1], F32, ph)
            ptr = ph.enter_context(nc.psum_tensor(uq("s0_ptr"), [128, 1024], BF16))
            ps0 = ph.enter_context(nc.psum_tensor(uq("s0_ps"), [128, 512], F32))
            Bps0 = Buf()
            Bt, Btbl, Boh, Bgv, Bmt, Bmb, Bss, Bptr = [Buf() for _ in range(8)]
            dma("sp", ident_f[:], c_ident.ap(), w=[B_const])
            cp("dve", ident_b[:], ident_f[:], r=[B_const], w=[B_const])
            dma("sp", tmpf[:], c_J.ap(), w=[Bt])
            cp("dve", J_b[:], tmpf[:], r=[Bt], w=[B_const])
            dma("sp", tri_f[:], c_tri.ap(), w=[B_const])
            dma("sp", farb[:], AP(rel_bias, 31 * 12, [[0, 128], [1, 12]]), w=[B_const])
            mset("dve", ones_b[:], 1.0, w=[B_const])
            mset("dve", epsb[:], EPS, w=[B_const])
            for k_ in range(NBIS):
                mset("dve", pw2[:, k_:k_ + 1], float(2.0 ** -(k_ + 1)), w=[B_const])
            dma("sp", tbl[0:32, :], rel_bias.ap(), w=[Btbl])
            dma("sp", tbl[32:34, :], c_sel.ap(), w=[Btbl])
            ts("dve", tbl[0:32, :], tbl[0:32, :], 8.0, None, ALU.mult, r=[Btbl], w=[Btbl])
            dma("sp", oh[:], c_onehot.ap(), w=[Boh])
            for c in range((NG + 511) // 512):
                n = min(512, NG - c * 512)
                mm(ps0[0:12, 0:n], tbl[:, :], oh[:, c * 512:c * 512 + n], True, True, r=[Btbl, Boh], w=[Bps0])
                cp("dve", gv[:, c * 512:c * 512 + n], ps0[0:12, 0:n], r=[Bps0], w=[Bgv])
                act(ev[:, c * 512:c * 512 + n], ps0[0:12, 0:n], AF.Exp, r=[Bps0], w=[Bgv], scale=0.125)
            B_gscr = Buf("gscr")
            dma("sp", GSCR.ap(), gv[:], r=[Bgv], w=[B_gscr])
            dma("sp", ESCR.ap(), ev[:], r=[Bgv], w=[B_gscr])
            for sq_i in range(NSEQ):
                for blk in range(MEM // 128):
                    dma("sp", mt[:], mem_in.ap()[sq_i * MEM + blk * 128: sq_i * MEM + (blk + 1) * 128, :], w=[Bmt])
                    act(sq[:], mt[:], AF.Square, r=[Bmt], w=[Bss], accum=ss[:, 0:1])
                    rstd_from_ss(rs[:, 0:1], ss[:, 0:1], D, r=[Bss], w=[Bss])
                    ts("dve", mb[:], mt[:], rs[:, 0:1], None, ALU.mult, r=[Bmt, Bss], w=[Bmb])
                    for kc in range(KC):
                        tr(ptr[:, kc * 128:(kc + 1) * 128], mb[:, kc * 128:(kc + 1) * 128], ident_b[:],
                           r=[Bmb, B_const], w=[Bptr])
                    cp("dve", memhatT[:, sq_i, :, blk * 128:(blk + 1) * 128],
                       ptr[:, :].rearrange("p (k t) -> p k t", k=KC), r=[Bptr], w=[B_memhat])
            P.flush()

        def load_weight(dst, src_t, src_off, nk, N, gain, stage, Bdst, row_stride=None):
            rs_ = N if row_stride is None else row_stride
            CH = 2048
            for kc in range(nk):
                for c0 in range(0, N, CH):
                    n = min(CH, N - c0)
                    st, Bst = stage.next()
                    dma("sp", st[:, 0:n], AP(src_t, src_off + kc * 128 * rs_ + c0, [[rs_, 128], [1, n]]), w=[Bst])
                    if gain is None:
                        cp("dve", dst[:, kc, c0:c0 + n], st[:, 0:n], r=[Bst], w=[Bdst])
                    else:
                        g, Bg = gain
                        ts("dve", dst[:, kc, c0:c0 + n], st[:, 0:n], g[:, kc:kc + 1], None, ALU.mult,
                           r=[Bst, Bg], w=[Bdst])

        def load_gain(dst, src_t, off, n, Bd):
            dma("sp", dst[:, 0:n], AP(src_t, off, [[1, 128], [128, n]]), w=[Bd], slow=True)

        B_x = {}

        def xbuf(which, t):
            return B_x.setdefault((which, t), Buf())
        B_scr = {}

        def scr(name, *idx):
            return B_scr.setdefault((name,) + idx, Buf())

        for l in range(L):
            lam_init = 0.8 - 0.6 * math.exp(-0.3 * l)
            xsrc = (x_in, "x") if l == 0 else (XB, "xb")

            with ExitStack() as ph:
                wsb = sb("p1_w", [128, KC, NIN], BF16, ph)
                stage = Rot([sb("p1_st%d" % i, [128, 2048], F32, ph) for i in range(2)])
                gmix = sb("p1_g", [128, KC], F32, ph)
                xt = Rot([sb("p1_x%d" % i, [128, D], F32, ph) for i in range(3)])
                hb = Rot([sb("p1_hb%d" % i, [128, D], BF16, ph) for i in range(2)])
                sqj = sb("p1_sq", [128, D], BF16, ph)
                ssr = Rot([sb("p1_ss%d" % i, [128, 2], F32, ph) for i in range(3)])
                hT = Rot([sb("p1_hT%d" % i, [128, KC, 512], BF16, ph) for i in range(2)])
                stF = Rot([sb("p1_sf%d" % i, [128, 4, 512], BF16, ph) for i in range(3)])
                stT = Rot([sb("p1_stt%d" % i, [128, 512], BF16, ph) for i in range(3)])
                stW = Rot([sb("p1_sw%d" % i, [128, 8], F32, ph) for i in range(2)])
                ptr = ph.enter_context(nc.psum_tensor(uq("p1_ptr"), [128, 1024], BF16))
                psx = [ph.enter_context(nc.psum_tensor(uq("p1_ps%d" % i), [128, 512], F32)) for i in range(5)]
                Bpsx = [Buf() for _ in range(5)]
                Bptr, Bw, Bg, Bsq = Buf(), Buf(), Buf(), Buf()
                load_gain(gmix, norm_mix, l * D, KC, Bg)
                load_weight(wsb, w_in, l * D * NIN, KC, NIN, (gmix, Bg), stage, Bw)
                FG = [(0, 4, 128, QTA), (512, 4, 128, KTA), (1536, 2, 128, QTB), (1792, 2, 128, KTB),
                      (2304, 2, 128, QTC), (2560, 2, 128, KTC), (3072, 4, 128, IQT), (3584, 1, 64, IKT)]
                TG = [(1024, 512, VA), (2048, 256, VB), (2816, 256, VC)]
                pi = 0
                for st_i in range(NTOK // 512):
                    seq = (st_i * 512) // S
                    t0 = st_i * 512 - seq * S
                    hTt, BhT = hT.next()
                    for u in range(4):
                        row0 = st_i * 512 + u * 128
                        x_t, Bxt = xt.next()
                        h_b, Bhb = hb.next()
                        s_s, Bs = ssr.next()
                        dma("sp", x_t[:], xsrc[0].ap()[row0:row0 + 128, :], r=[xbuf(xsrc[1], row0 // 128)], w=[Bxt])
                        act(sqj[:], x_t[:], AF.Square, r=[Bxt], w=[Bs, Bsq], accum=s_s[:, 0:1])
                        rstd_from_ss(s_s[:, 1:2], s_s[:, 0:1], D, r=[Bs], w=[Bs])
                        ts("dve", h_b[:], x_t[:], s_s[:, 1:2], None, ALU.mult, r=[Bxt, Bs], w=[Bhb])
                        for kc in range(KC):
                            tr(ptr[:, kc * 128:(kc + 1) * 128], h_b[:, kc * 128:(kc + 1) * 128], ident_b[:],
                               r=[Bhb, B_const], w=[Bptr])
                        cp("act", hTt[:, :, u * 128:(u + 1) * 128], ptr[:, :].rearrange("p (k t) -> p k t", k=KC),
                           r=[Bptr], w=[BhT])
                        for (c0, ncol, dt_) in TG:
                            b = pi % 5; pi += 1
                            for kc in range(KC):
                                mm(psx[b][:, 0:ncol], hTt[:, kc, u * 128:(u + 1) * 128], wsb[:, kc, c0:c0 + ncol],
                                   kc == 0, kc == KC - 1, r=[BhT, Bw], w=[Bpsx[b]])
                            s_t, Bst_ = stT.next()
                            cp("dve", s_t[:, 0:ncol], psx[b][:, 0:ncol], r=[Bpsx[b]], w=[Bst_])
                            dma("pool", dt_.ap()[row0:row0 + 128, :], s_t[:, 0:ncol], r=[Bst_],
                                w=[scr(nm(dt_), seq, st_i)])
                        b = pi % 5; pi += 1
                        for kc in range(KC):
                            mm(psx[b][:, 0:8], hTt[:, kc, u * 128:(u + 1) * 128], wsb[:, kc, 3648:3656],
                               kc == 0, kc == KC - 1, r=[BhT, Bw], w=[Bpsx[b]])
                        s_w, Bsw = stW.next()
                        cp("dve", s_w[:, :], psx[b][:, 0:8], r=[Bpsx[b]], w=[Bsw])
                        dma("pool", IW.ap()[row0:row0 + 128, :], s_w[:, :], r=[Bsw], w=[scr("IW", seq, st_i)])
                    for gi, (c0, nblk, bs, dt_) in enumerate(FG):
                        s_f, Bsf = stF.next()
                        for bi in range(nblk):
                            b = pi % 5; pi += 1
                            cc = c0 + bi * bs
                            for kc in range(KC):
                                mm(psx[b][0:bs, :], wsb[:, kc, cc:cc + bs], hTt[:, kc, :], kc == 0, kc == KC - 1,
                                   r=[BhT, Bw], w=[Bpsx[b]])
                            cp("act" if (bi % 2 == 0) else "dve", s_f[0:bs, bi, :], psx[b][0:bs, :], r=[Bpsx[b]], w=[Bsf])
                        dst = AP(dt_, seq * S + t0, [[NSEQ * S, bs], [bs * NSEQ * S, nblk], [1, 512]])
                        dma("pool", dst, s_f[0:bs, 0:nblk, :], r=[Bsf], w=[scr(nm(dt_), seq, st_i)])
                P.flush()

            for seq in range(NSEQ):
                def scr_all(name):
                    return [scr(name, seq, st_i) for st_i in range(seq * NT, (seq + 1) * NT)]

                def att_common(ph, tag):
                    C = {}
                    C["strips"] = sb(tag + "strips", [128, 4, MS], BF16, ph)
                    C["Bstrips"] = Buf()
                    C["kT"] = [[sb(tag + "kT%d%d" % (s_, m), [128, S], BF16, ph) for m in range(2)] for s_ in range(2)]
                    C["qT"] = [sb(tag + "qT%d" % s_, [128, S], BF16, ph) for s_ in range(2)]
                    C["V"] = [sb(tag + "V%d" % s_, [128, NB, 130], BF16, ph) for s_ in range(2)]
                    C["oh"] = [sb(tag + "oh%d" % s_, [128, NB, 128], BF16, ph) for s_ in range(2)]
                    C["Bin"] = [Buf(), Buf()]
                    C["Boh"] = [Buf(), Buf()]
                    C["pT"] = Rot([sb(tag + "pT%d" % i, [128, 512], BF16, ph) for i in range(7)])
                    C["pE"] = Rot([sb(tag + "pE%d" % i, [128, 512], BF16, ph) for i in range(4)])
                    C["stmp"] = Rot([sb(tag + "stmp%d" % i, [128, MS], BF16, ph) for i in range(1)])
                    C["sm"] = Rot([sb(tag + "sm%d" % i, [128, 8], F32, ph) for i in range(16)])
                    C["t0"] = Rot([sb(tag + "t0%d" % i, [128, 128], F32, ph) for i in range(8)])
                    C["o"] = Rot([sb(tag + "o%d" % i, [128, 128], F32, ph) for i in range(8)])
                    C["sqj4"] = [sb(tag + "sqj4%d" % i, [128, 128], BF16, ph) for i in range(4)]
                    C["Bsqj4"] = [Buf() for _ in range(4)]
                    C["sqj"] = sb(tag + "sqj", [128, 128], BF16, ph)
                    C["Bsqj"] = Buf()
                    C["acc"] = [[ph.enter_context(nc.psum_tensor(uq(tag + "acc%d%d" % (m, k)), [128, 2, 256], F32))
                                 for k in range(2)] for m in range(2)]
                    C["Bacc"] = [[Buf(), Buf()], [Buf(), Buf()]]
                    C["sc"] = Rot([ph.enter_context(nc.psum_tensor(uq(tag + "sc%d" % i), [128, 512], F32)) for i in range(4)])
                    C["set"] = 0
                    for s_ in range(2):
                        mset("pool", C["kT"][s_][0][64:128, :], 0.0, w=[C["Bin"][s_]])
                        mset("pool", C["kT"][s_][1][0:64, :], 0.0, w=[C["Bin"][s_]])
                    return C

                def load_strips(C, h0):
                    for hh in range(4):
                        tmp, Btmp = C["stmp"].next()
                        dma("sp", tmp[:, :], AP(ESCR, (h0 + hh) * NG, [[1, 128], [1, MS]]), r=[B_gscr], w=[Btmp])
                        for c in range(MS // 512):
                            ps, Bps = C["sc"].next()
                            mm(ps[:, :], J_b[:], tmp[:, c * 512:(c + 1) * 512], True, True, r=[Btmp, B_const], w=[Bps])
                            cp("dve" if c % 2 else "act", C["strips"][:, hh, c * 512:(c + 1) * 512], ps[:, :],
                               r=[Bps], w=[C["Bstrips"]])

                def att_pair(C, kind, qt, kt, row0, vloads, sidx, scale, jmin_fn, evac, ocol, dst_cols,
                             mask=None, hidx=None):
                    s_ = C["set"]; C["set"] ^= 1
                    Bin = C["Bin"][s_]
                    kT0, kT1 = C["kT"][s_]
                    qT = C["qT"][s_]
                    V = C["V"][s_]
                    oh_t = C["oh"][s_]; Boh = C["Boh"][s_]
                    rq = scr_all(nm(qt)); rk = scr_all(nm(kt))
                    dma("sp", qT[:, :], AP(qt, row0 * NSEQ * S + seq * S, [[NSEQ * S, 128], [1, S]]), r=rq, w=[Bin])
                    dma("sp", kT0[0:64, :], AP(kt, row0 * NSEQ * S + seq * S, [[NSEQ * S, 64], [1, S]]), r=rk, w=[Bin])
                    dma("sp", kT1[64:128, :], AP(kt, (row0 + 64) * NSEQ * S + seq * S, [[NSEQ * S, 64], [1, S]]), r=rk, w=[Bin])
                    for (vt, vc0, vn, dcol) in vloads:
                        W_ = {"VA": 512, "VB": 256, "VC": 256}[nm(vt)]
                        dma("sp", V[:, :, dcol:dcol + vn],
                            AP(vt, seq * S * W_ + vc0, [[W_, 128], [128 * W_, NB], [1, vn]]), r=scr_all(nm(vt)), w=[Bin])
                        mset("pool", V[:, :, dcol + vn:dcol + vn + 1], 1.0, w=[Bin])
                    kTs = (kT0, kT1)
                    nv, vcol = (129, (0, 0)) if kind == "A" else (65, (0, 65))
                    acc = C["acc"]; Bacc = C["Bacc"]
                    def pv(job):
                        (i_, j_, m_, c0_, j0_, pT_, BpT_) = job
                        for u in range(c0_ // 128, 4):
                            mm(acc[m_][u // 2][:, u % 2, 0:nv], pT_[:, u * 128:(u + 1) * 128],
                               V[:, j_, vcol[m_]:vcol[m_] + nv], (j_ == j0_ and u % 2 == 0), (j_ == 4 * i_ + u),
                               r=[BpT_, Bin], w=[Bacc[m_][u // 2]])

                    for i in range(NT):
                        if mask is not None:
                            mk, Bmk = mask(i)
                        j0 = jmin_fn(i)
                        pending = []
                        for j in range(j0, 4 * i + 4):
                            c0 = max(0, j - 4 * i) * 128
                            D0 = 512 * i - 128 * j
                            off = min(D0, DFAR) + 384
                            far = (hidx is not None) and D0 >= 1664
                            for m in range(2):
                                ps, Bps = C["sc"].next()
                                mm(ps[:, c0:512], kTs[m][:, j * 128:(j + 1) * 128], qT[:, i * 512 + c0:(i + 1) * 512],
                                   True, mask is None, r=[Bin], w=[Bps])
                                if mask is not None:
                                    for u in range(c0 // 128, 4):
                                        mm(ps[:, u * 128:(u + 1) * 128], mk[:, u, j * 128:(j + 1) * 128], ident_b[:],
                                           False, u == 3, r=[Bmk, B_const], w=[Bps])
                                pT, BpT = C["pT"].next()
                                if far:
                                    act(pT[:, c0:512], ps[:, c0:512], AF.Exp, r=[Bps, B_const], w=[BpT], scale=scale,
                                        bias=farb[:, hidx[m]:hidx[m] + 1])
                                else:
                                    pR, BpR = C["pE"].next()
                                    act(pR[:, c0:512], ps[:, c0:512], AF.Exp, r=[Bps], w=[BpR], scale=scale)
                                    tt("dve", pT[:, c0:512], pR[:, c0:512], C["strips"][:, sidx[m], off + c0:off + 512],
                                       ALU.mult, r=[BpR, C["Bstrips"]], w=[BpT])
                                pending.append((i, j, m, c0, j0, pT, BpT))
                                if len(pending) > 3:
                                    pv(pending.pop(0))
                        while pending:
                            pv(pending.pop(0))
                        evac(C, i, oh_t, Boh)
                    W_ = D
                    dma("pool", AP(OS, seq * S * W_ + ocol, [[W_, 128], [128 * W_, NB], [1, dst_cols]]),
                        oh_t[:, :, 0:dst_cols], r=[Boh], w=[scr("OS", seq, ocol)])

                def evac_bc(C, i, oh_t, Boh):
                    acc = C["acc"]; Bacc = C["Bacc"]
                    U = []
                    for m in range(2):
                        for u in range(4):
                            sm, Bsm = C["sm"].next()
                            U.append((m, u, sm, Bsm, acc[m][u // 2], Bacc[m][u // 2]))
                    for (m, u, sm, Bsm, a, Ba_) in U:
                        recip(sm[:, 0:1], a[:, u % 2, 64:65], r=[Ba_], w=[Bsm])
                    for (m, u, sm, Bsm, a, Ba_) in U:
                        ts("dve", oh_t[:, 4 * i + u, m * 64:(m + 1) * 64], a[:, u % 2, 0:64], sm[:, 0:1], None,
                           ALU.mult, r=[Ba_, Bsm], w=[Boh])

                with ExitStack() as ph:
                    C = att_common(ph, "ab_")
                    lamv = sb("ab_lam", [128, 4, 64], F32, ph)
                    lsm = sb("ab_lsm", [128, 8], F32, ph)
                    prod = sb("ab_prod", [128, 64], F32, ph)
                    gs = sb("ab_gs", [128, 128], F32, ph)
                    Blam, Bgs = Buf(), Buf()
                    for qi, t_ in enumerate((lam_q1, lam_k1, lam_q2, lam_k2)):
                        dma("sp", lamv[:, qi, :], AP(t_, l * 64, [[0, 128], [1, 64]]), w=[Blam])
                    for qi in range(2):
                        tt("dve", prod[:], lamv[:, 2 * qi, :], lamv[:, 2 * qi + 1, :], ALU.mult, r=[Blam], w=[Blam])
                        red(lsm[:, qi:qi + 1], prod[:], ALU.add, r=[Blam], w=[Blam])
                        act(lsm[:, qi:qi + 1], lsm[:, qi:qi + 1], AF.Exp, r=[Blam], w=[Blam])
                    tt("dve", lsm[:, 2:3], lsm[:, 1:2], lsm[:, 0:1], ALU.subtract, r=[Blam], w=[Blam])
                    ts("dve", lsm[:, 2:3], lsm[:, 2:3], -lam_init, None, ALU.add, r=[Blam], w=[Blam])
                    dma("sp", gs[:], AP(subln, l * 128, [[0, 128], [1, 128]]), w=[Bgs])
                    ts("dve", gs[:], gs[:], 1.0 - lam_init, None, ALU.mult, r=[Bgs], w=[Bgs])
                    neglam = lsm[:, 2:3]

                    def evac_a(C, i, oh_t, Boh):
                        acc = C["acc"]; Bacc = C["Bacc"]
                        U = []
                        for u in range(4):
                            sm, Bsm = C["sm"].next()
                            t0, Bt0 = C["t0"].next()
                            o_, Bo = C["o"].next()
                            U.append((u, sm, Bsm, t0, Bt0, o_, Bo, acc[0][u // 2], acc[1][u // 2], Bacc[0][u // 2], Bacc[1][u // 2]))
                        for (u, sm, Bsm, t0, Bt0, o_, Bo, a0, a1, B0, B1) in U:
                            recip(sm[:, 0:1], a0[:, u % 2, 128:129], r=[B0], w=[Bsm])
                        for (u, sm, Bsm, t0, Bt0, o_, Bo, a0, a1, B0, B1) in U:
                            recip(sm[:, 1:2], a1[:, u % 2, 128:129], r=[B1], w=[Bsm])
                        for (u, sm, Bsm, t0, Bt0, o_, Bo, a0, a1, B0, B1) in U:
                            tt("dve", sm[:, 1:2], sm[:, 1:2], neglam, ALU.mult, r=[Blam], w=[Bsm])
                        for (u, sm, Bsm, t0, Bt0, o_, Bo, a0, a1, B0, B1) in U:
                            act(t0[:], a0[:, u % 2, 0:128], AF.Identity, r=[B0, Bsm], w=[Bt0], scale=sm[:, 0:1])
                        for (u, sm, Bsm, t0, Bt0, o_, Bo, a0, a1, B0, B1) in U:
                            stt(o_[:], a1[:, u % 2, 0:128], sm[:, 1:2], t0[:], ALU.mult, ALU.add, r=[B1, Bsm, Bt0], w=[Bo])
                        for (u, sm, Bsm, t0, Bt0, o_, Bo, a0, a1, B0, B1) in U:
                            act(C["sqj4"][u][:], o_[:], AF.Square, r=[Bo], w=[C["Bsqj4"][u], Bsm], accum=sm[:, 2:3])
                        for (u, sm, Bsm, t0, Bt0, o_, Bo, a0, a1, B0, B1) in U:
                            act(sm[:, 3:4], sm[:, 2:3], AF.Ln, r=[Bsm], w=[Bsm], scale=1.0 / 128, bias=epsb[:, 0:1])
                        for (u, sm, Bsm, t0, Bt0, o_, Bo, a0, a1, B0, B1) in U:
                            act(sm[:, 3:4], sm[:, 3:4], AF.Exp, r=[Bsm], w=[Bsm], scale=-0.5)
                        for (u, sm, Bsm, t0, Bt0, o_, Bo, a0, a1, B0, B1) in U:
                            stt(oh_t[:, 4 * i + u, :], o_[:], sm[:, 3:4], gs[:], ALU.mult, ALU.mult,
                                r=[Bo, Bsm, Bgs], w=[Boh])

                    load_strips(C, 0)
                    for h in range(4):
                        att_pair(C, "A", QTA, KTA, h * 128, [(VA, h * 128, 128, 0)], (h, h), 0.125,
                                 lambda i: 0, evac_a, h * 128, 128, hidx=(h, h))
                    load_strips(C, 4)
                    for pr in range(2):
                        att_pair(C, "B", QTB, KTB, pr * 128,
                                 [(VB, pr * 128, 64, 0), (VB, pr * 128 + 64, 64, 65)], (2 * pr, 2 * pr + 1), 0.125,
                                 lambda i: max(0, 4 * i - 17), evac_bc, 512 + pr * 128, 128)
                    P.flush()

                with ExitStack() as ph:
                    iqT = sb("ix_iqT", [128, 4, S], BF16, ph)
                    ik = [sb("ix_ik%d" % m, [128, S], BF16, ph) for m in range(2)]
                    iw_sb = sb("ix_iw", [128, NB, 8], F32, ph)
                    score = Rot([sb("ix_sc%d" % i, [128, S], F32, ph) for i in range(6)])
                    junk = [sb("ix_junk%d" % i, [128, S], mybir.dt.int8, ph) for i in range(3)]
                    Bjunk = [Buf(), Buf(), Buf()]
                    rt = Rot([sb("ix_r%d" % i, [128, 512], BF16, ph) for i in range(4)])
                    dgw = Rot([sb("ix_dg%d" % i, [128, 8, 128], BF16, ph) for i in range(2)])
                    mo = Rot([sb("ix_mo%d" % i, [128, S], BF16, ph) for i in range(2)])
                    smr = Rot([sb("ix_sm%d" % i, [128, 8 + 2 * NBIS], F32, ph) for i in range(6)])
                    lg = Rot([ph.enter_context(nc.psum_tensor(uq("ix_lg%d" % i), [128, 512], F32)) for i in range(3)])
                    scp = Rot([ph.enter_context(nc.psum_tensor(uq("ix_scp%d" % i), [128, 512], F32)) for i in range(2)])
                    Bin = Buf()
                    dma("sp", iqT[:, :, :], AP(IQT, seq * S, [[NSEQ * S, 128], [128 * NSEQ * S, 4], [1, S]]),
                        r=scr_all("IQT"), w=[Bin])
                    mset("pool", ik[0][64:128, :], 0.0, w=[Bin])
                    mset("pool", ik[1][0:64, :], 0.0, w=[Bin])
                    dma("sp", ik[0][0:64, :], AP(IKT, seq * S, [[NSEQ * S, 64], [1, S]]), r=scr_all("IKT"), w=[Bin])
                    dma("sp", ik[1][64:128, :], AP(IKT, seq * S, [[NSEQ * S, 64], [1, S]]), r=scr_all("IKT"), w=[Bin])
                    dma("sp", iw_sb[:, :, :], AP(IW, seq * S * 8, [[8, 128], [128 * 8, NB], [1, 8]]),
                        r=scr_all("IW"), w=[Bin])
                    def scores(b, res):
                        dg, Bdg = dgw.next()
                        for hh in range(8):
                            ts("pool", dg[:, hh, :], ident_b[:], iw_sb[:, b, hh:hh + 1], 0.0, ALU.mult, ALU.add,
                               r=[Bin, B_const], w=[Bdg])
                        sc_t, Bsc = score.next()
                        res.append((sc_t, Bsc))
                        pend = None
                        cur = {}

                        def fin(p):
                            (c_, hh_, ncol_, r__, Br_) = p
                            if hh_ == 0:
                                cur["sp"] = scp.next()
                            sp_, Bsp = cur["sp"]
                            mm(sp_[:, 0:ncol_], dg[:, hh_, :], r__[:, 0:ncol_], hh_ == 0, hh_ == 7, r=[Bdg, Br_], w=[Bsp])
                            if hh_ == 7:
                                cp("act", sc_t[:, c_ * 512:c_ * 512 + ncol_], sp_[:, 0:ncol_], r=[Bsp], w=[Bsc])
                        for c in range(b // 4 + 1):
                            ncol = 512 if c < b // 4 else ((b % 4) + 1) * 128
                            for hh in range(8):
                                lg_, Blg = lg.next()
                                mm(lg_[:, 0:ncol], iqT[:, hh // 2, b * 128:(b + 1) * 128],
                                   ik[hh % 2][:, c * 512:c * 512 + ncol], True, True, r=[Bin], w=[Blg])
                                r_, Br = rt.next()
                                act(r_[:, 0:ncol], lg_[:, 0:ncol], AF.Relu, r=[Blg], w=[Br])
                                if pend is not None:
                                    fin(pend)
                                pend = (c, hh, ncol, r_, Br)
                                yield
                        fin(pend)

                    SN = 8 + NBIS

                    def bisect(b, sc_t, Bsc, jk, Bjk, on_act):
                        Sc = (b + 1) * 128
                        sm, Bsm = smr.next()
                        red(sm[:, 1:2], sc_t[:, 0:Sc], ALU.max, r=[Bsc], w=[Bsm]); yield
                        red(sm[:, 0:1], sc_t[:, 0:Sc], ALU.min, r=[Bsc], w=[Bsm]); yield
                        tt("dve", sm[:, 1:2], sm[:, 1:2], sm[:, 0:1], ALU.subtract, r=[Bsm], w=[Bsm]); yield
                        tt("dve", sc_t[:, b * 128:Sc], sc_t[:, b * 128:Sc], tri_f[:], ALU.add, r=[B_const], w=[Bsc]); yield
                        ts("dve", sm[:, 8:8 + NBIS], pw2[:, 0:NBIS], sm[:, 1:2], None, ALU.mult, r=[Bsm, B_const], w=[Bsm]); yield
                        if not on_act:
                            tt("dve", sm[:, 2:3], sm[:, 0:1], sm[:, 8:9], ALU.add, r=[Bsm], w=[Bsm]); yield
                            for it in range(NBIS):
                                ts("dve", jk[:, 0:Sc], sc_t[:, 0:Sc], sm[:, 2:3], None, ALU.is_ge, ALU.add,
                                   r=[Bsc, Bsm], w=[Bjk, Bsm], accum=sm[:, 3:4]); yield
                                ts("dve", sm[:, 4:5], sm[:, 3:4], float(TOPK) - 0.5, -0.5, ALU.is_ge, ALU.add, r=[Bsm], w=[Bsm]); yield
                                stt(sm[:, 2:3], sm[:, 4:5], sm[:, 8 + it:9 + it], sm[:, 2:3], ALU.mult, ALU.add, r=[Bsm], w=[Bsm]); yield
                            stt(sm[:, 0:1], sm[:, 8 + NBIS - 1:8 + NBIS], -1.0, sm[:, 2:3], ALU.mult, ALU.add, r=[Bsm], w=[Bsm]); yield
                        else:
                            ts("dve", sm[:, SN:SN + NBIS], sm[:, 8:8 + NBIS], -1.0, None, ALU.mult, r=[Bsm], w=[Bsm]); yield
                            stt(sm[:, 2:3], sm[:, 0:1], -1.0, sm[:, 8:9], ALU.mult, ALU.subtract, r=[Bsm], w=[Bsm]); yield
                            for it in range(NBIS):
                                act(jk[:, 0:Sc], sc_t[:, 0:Sc], AF.Sign, r=[Bsc, Bsm], w=[Bjk, Bsm], bias=sm[:, 2:3],
                                    scale=1.0, accum=sm[:, 3:4]); yield
                                ts("pool", sm[:, 4:5], sm[:, 3:4], float(2 * TOPK - Sc) - 0.5, -0.5, ALU.is_ge, ALU.add,
                                   r=[Bsm], w=[Bsm]); yield
                                ts("pool", sm[:, 2:3], sm[:, 4:5], sm[:, SN + it:SN + it + 1], sm[:, 2:3], ALU.mult, ALU.add,
                                   r=[Bsm], w=[Bsm]); yield
                            stt(sm[:, 0:1], sm[:, 2:3], -1.0, sm[:, 8 + NBIS - 1:8 + NBIS], ALU.mult, ALU.subtract, r=[Bsm], w=[Bsm]); yield
                        mo_t, Bmo = mo.next()
                        ts("dve", mo_t[:, 0:Sc], sc_t[:, 0:Sc], sm[:, 0:1], NEGM, ALU.is_lt, ALU.mult, r=[Bsc, Bsm], w=[Bmo]); yield
                        dma("sp", MASK.ap()[b * 128:(b + 1) * 128, 0:Sc], mo_t[:, 0:Sc], r=[Bmo], w=[scr("MASK", b)])

                    def run_rr(gens):
                        alive = list(gens)
                        while alive:
                            for g_ in list(alive):
                                try:
                                    next(g_)
                                except StopIteration:
                                    alive.remove(g_)

                    def chain(gl):
                        for g_ in gl:
                            yield from g_

                    groups = [list(range(b0, min(b0 + 3, NB))) for b0 in range(0, NB, 3)]
                    prev = []
                    for grp in groups:
                        res = []
                        run_rr([chain([scores(b_, res) for b_ in grp])] + prev)
                        prev = []
                        for gi, b_ in enumerate(grp):
                            on_act = (gi == 2)
                            prev.append(bisect(b_, res[gi][0], res[gi][1], junk[gi], Bjunk[gi], on_act))
                    run_rr(prev)
                    P.flush()

                with ExitStack() as ph:
                    C = att_common(ph, "c_")
                    mkr = Rot([sb("c_mk%d" % i, [128, 4, S], BF16, ph) for i in range(2)])

                    def load_mask(i):
                        mk, Bmk = mkr.next()
                        for u in range(4):
                            b = 4 * i + u
                            dma("sp", mk[:, u, 0:(b + 1) * 128], MASK.ap()[b * 128:(b + 1) * 128, 0:(b + 1) * 128],
                                r=[scr("MASK", b)], w=[Bmk])
                        return mk, Bmk
                    load_strips(C, 8)
                    for pr in range(2):
                        att_pair(C, "C", QTC, KTC, pr * 128,
                                 [(VC, pr * 128, 64, 0), (VC, pr * 128 + 64, 64, 65)], (2 * pr, 2 * pr + 1), 0.125,
                                 lambda i: 0, evac_bc, 768 + pr * 128, 128, mask=load_mask, hidx=(8 + 2 * pr, 9 + 2 * pr))
                    P.flush()

            with ExitStack() as ph:
                KTm = sb("d1_KTm", [128, NSEQ, 4, 2, MEM], BF16, ph)
                Vm = sb("d1_Vm", [128, NSEQ, 2, 4, 256], BF16, ph)
                Bkv = Buf()
                NPS = 5
                psx = [ph.enter_context(nc.psum_tensor(uq("d1_ps%d" % i), [128, 512], F32)) for i in range(NPS)]
                Bpsx = [Buf() for _ in range(NPS)]
                ptrs = [(ph.enter_context(nc.psum_tensor(uq("d1_ptr%d" % i), [128, 1024], BF16)), Buf()) for i in range(2)]
                lps = ph.enter_context(nc.psum_tensor(uq("d1_lps"), [128, 16], F32))
                Blps = Buf()
                pi = 0
                with ExitStack() as ph2:
                    wkv = sb("mk_w", [128, KC, 2 * D], BF16, ph2)
                    stage = Rot([sb("mk_st%d" % i, [128, 2048], F32, ph2) for i in range(2)])
                    gkv = sb("mk_g", [128, KC], F32, ph2)
                    Bw, Bg = Buf(), Buf()
                    load_gain(gkv, norm_memkv, l * D, KC, Bg)
                    load_weight(wkv, w_mkv, l * D * 2 * D, KC, 2 * D, (gkv, Bg), stage, Bw)
                    for sq_i in range(NSEQ):
                        for h in range(4):
                            for dc in range(2):
                                n0 = h * 256 + dc * 128
                                b = pi % NPS; pi += 1
                                for kc in range(KC):
                                    mm(psx[b][:, 0:MEM], wkv[:, kc, n0:n0 + 128], memhatT[:, sq_i, kc, :],
                                       kc == 0, kc == KC - 1, r=[Bw, B_memhat], w=[Bpsx[b]])
                                cp("act", KTm[:, sq_i, h, dc, :], psx[b][:, 0:MEM], r=[Bpsx[b]], w=[Bkv])
                        for blk in range(2):
                            for nch in range(2):
                                b = pi % NPS; pi += 1
                                for kc in range(KC):
                                    mm(psx[b][:, :], memhatT[:, sq_i, kc, blk * 128:(blk + 1) * 128],
                                       wkv[:, kc, D + nch * 512:D + (nch + 1) * 512], kc == 0, kc == KC - 1,
                                       r=[Bw, B_memhat], w=[Bpsx[b]])
                                cp("dve", Vm[:, sq_i, blk, 2 * nch:2 * nch + 2, :],
                                   psx[b][:, :].rearrange("p (h d) -> p h d", h=2), r=[Bpsx[b]], w=[Bkv])
                    P.flush()
                wo = sb("d1_wo", [128, KC, D], BF16, ph)
                wq = sb("d1_wq", [128, KC, D], BF16, ph)
                wm = sb("d1_wm", [128, KC, D], BF16, ph)
                gq = sb("d1_gq", [128, KC], F32, ph)
                stage = Rot([sb("d1_st%d" % i, [128, 2048], F32, ph) for i in range(2)])
                xt = Rot([sb("d1_x%d" % i, [128, D], F32, ph) for i in range(6)])
                ob = Rot([sb("d1_ob%d" % i, [128, D], BF16, ph) for i in range(2)])
                hb = Rot([sb("d1_hb%d" % i, [128, D], BF16, ph) for i in range(4)])
                sqr = Rot([sb("d1_sq%d" % i, [128, D], BF16, ph) for i in range(2)])
                ssr = Rot([sb("d1_ss%d" % i, [128, 2], F32, ph) for i in range(8)])
                oT = sb("d1_oT", [128, KC, 512], BF16, ph); BoT = Buf()
                hT = sb("d1_hT", [128, KC, 512], BF16, ph); BhT = Buf()
                qmT = sb("d1_qmT", [128, 8, 512], BF16, ph); BqmT = Buf()
                pTm = Rot([sb("d1_pT%d" % i, [128, 2, 512], BF16, ph) for i in range(4)])
                omT = sb("d1_omT", [128, 4, 2, 512], BF16, ph); BomT = Buf()
                rl = sb("d1_rl", [128, 16], F32, ph); Brl = Buf()
                Bwo, Bwq, Bwm, Bgq = Buf(), Buf(), Buf(), Buf()
                load_gain(gq, norm_mem, l * D, KC, Bgq)
                load_weight(wo, w_out, l * D * D, KC, D, None, stage, Bwo)
                load_weight(wq, w_mq, l * D * D, KC, D, (gq, Bgq), stage, Bwq)
                load_weight(wm, w_mo, l * D * D, KC, D, None, stage, Bwm)
                for st_i in range(NTOK // 512):
                    seq = (st_i * 512) // S
                    xs = []
                    for u in range(4):
                        row0 = st_i * 512 + u * 128
                        x_t, Bxt = xt.next()
                        o_b, Bob = ob.next()
                        xs.append((x_t, Bxt))
                        dma("sp", x_t[:], xsrc[0].ap()[row0:row0 + 128, :], r=[xbuf(xsrc[1], row0 // 128)], w=[Bxt])
                        dma("sp", o_b[:], OS.ap()[row0:row0 + 128, :],
                            r=[scr("OS", seq, oc) for oc in (0, 128, 256, 384, 512, 640, 768, 896)], w=[Bob])
                        pt_, Bpt_ = ptrs[u % 2]
                        for kc in range(KC):
                            tr(pt_[:, kc * 128:(kc + 1) * 128], o_b[:, kc * 128:(kc + 1) * 128], ident_b[:],
                               r=[Bob, B_const], w=[Bpt_])
                        cp("act", oT[:, :, u * 128:(u + 1) * 128], pt_[:, :].rearrange("p (k t) -> p k t", k=KC),
                           r=[Bpt_], w=[BoT])
                    for u in range(4):
                        x_t, Bxt = xs[u]
                        for nch in range(2):
                            b = pi % NPS; pi += 1
                            for kc in range(KC):
                                mm(psx[b][:, :], oT[:, kc, u * 128:(u + 1) * 128], wo[:, kc, nch * 512:(nch + 1) * 512],
                                   kc == 0, kc == KC - 1, r=[BoT, Bwo], w=[Bpsx[b]])
                            tt("dve", x_t[:, nch * 512:(nch + 1) * 512], psx[b][:, :], x_t[:, nch * 512:(nch + 1) * 512],
                               ALU.add, r=[Bpsx[b]], w=[Bxt])
                    hbs = [hb.next() for _ in range(4)]
                    sss = [ssr.next() for _ in range(4)]
                    for u in range(4):
                        sq_t, Bsq_ = sqr.next()
                        act(sq_t[:], xs[u][0][:], AF.Square, r=[xs[u][1]], w=[sss[u][1], Bsq_], accum=sss[u][0][:, 0:1])
                    for u in range(4):
                        act(sss[u][0][:, 1:2], sss[u][0][:, 0:1], AF.Ln, r=[sss[u][1]], w=[sss[u][1]], scale=1.0 / D,
                            bias=epsb[:, 0:1])
                    for u in range(4):
                        act(sss[u][0][:, 1:2], sss[u][0][:, 1:2], AF.Exp, r=[sss[u][1]], w=[sss[u][1]], scale=-0.5)
                    for u in range(4):
                        ts("dve", hbs[u][0][:], xs[u][0][:], sss[u][0][:, 1:2], None, ALU.mult,
                           r=[xs[u][1], sss[u][1]], w=[hbs[u][1]])
                    for u in range(4):
                        h_b, Bhb = hbs[u]
                        pt_, Bpt_ = ptrs[u % 2]
                        for kc in range(KC):
                            tr(pt_[:, kc * 128:(kc + 1) * 128], h_b[:, kc * 128:(kc + 1) * 128], ident_b[:],
                               r=[Bhb, B_const], w=[Bpt_])
                        cp("act", hT[:, :, u * 128:(u + 1) * 128], pt_[:, :].rearrange("p (k t) -> p k t", k=KC),
                           r=[Bpt_], w=[BhT])
                    for nb in range(8):
                        b = pi % NPS; pi += 1
                        for kc in range(KC):
                            mm(psx[b][:, :], wq[:, kc, nb * 128:(nb + 1) * 128], hT[:, kc, :], kc == 0, kc == KC - 1,
                               r=[BhT, Bwq], w=[Bpsx[b]])
                        cp("act" if nb % 2 else "dve", qmT[:, nb, :], psx[b][:, :], r=[Bpsx[b]], w=[BqmT])
                    pts = [pTm.next() for _ in range(4)]
                    for h in range(4):
                        p_t, Bp = pts[h]
                        for blk in range(2):
                            b = pi % NPS; pi += 1
                            for dc in range(2):
                                mm(psx[b][:, :], KTm[:, seq, h, dc, blk * 128:(blk + 1) * 128], qmT[:, 2 * h + dc, :],
                                   dc == 0, dc == 1, r=[Bkv, BqmT], w=[Bpsx[b]])
                            act(p_t[:, blk, :], psx[b][:, :], AF.Exp, r=[Bpsx[b]], w=[Bp], scale=1.0 / 16.0)
                    for h in range(4):
                        p_t, Bp = pts[h]
                        for u in range(4):
                            for blk in range(2):
                                mm(lps[:, u * 4 + h:u * 4 + h + 1], p_t[:, blk, u * 128:(u + 1) * 128], ones_b[:, 0:1],
                                   blk == 0, blk == 1, r=[Bp, B_const], w=[Blps])
                        for dc in range(2):
                            b = pi % NPS; pi += 1
                            for blk in range(2):
                                mm(psx[b][:, :], Vm[:, seq, blk, h, dc * 128:(dc + 1) * 128], p_t[:, blk, :],
                                   blk == 0, blk == 1, r=[Bkv, Bp], w=[Bpsx[b]])
                            cp("act" if dc else "dve", omT[:, h, dc, :], psx[b][:, :], r=[Bpsx[b]], w=[BomT])
                    recip(rl[:, :], lps[:, :], r=[Blps], w=[Brl])
                    for u in range(4):
                        x_t, Bxt = xs[u]
                        row0 = st_i * 512 + u * 128
                        for h in range(4):
                            for nch in range(2):
                                b = pi % NPS; pi += 1
                                for dc in range(2):
                                    mm(psx[b][:, :], omT[:, h, dc, u * 128:(u + 1) * 128],
                                       wm[:, 2 * h + dc, nch * 512:(nch + 1) * 512], dc == 0, dc == 1,
                                       r=[BomT, Bwm], w=[Bpsx[b]])
                                stt(x_t[:, nch * 512:(nch + 1) * 512], psx[b][:, :], rl[:, u * 4 + h:u * 4 + h + 1],
                                    x_t[:, nch * 512:(nch + 1) * 512], ALU.mult, ALU.add, r=[Bpsx[b], Brl], w=[Bxt])
                        dma("pool", XA.ap()[row0:row0 + 128, :], x_t[:], r=[Bxt], w=[xbuf("xa", row0 // 128)])
                P.flush()

            with ExitStack() as ph:
                T2 = 256
                wu = sb("d2_wu", [128, KC, 2 * FF], BF16, ph)
                wd = sb("d2_wd", [128, NFB, D], BF16, ph)
                gf = sb("d2_gf", [128, KC], F32, ph)
                cw = sb("d2_cw", [128, 3, 44], F32, ph)
                cb = sb("d2_cb", [128, 44], F32, ph)
                Bwu, Bwd, Bgf, Bcw = Buf(), Buf(), Buf(), Buf()
                with ExitStack() as ph2:
                    stage = Rot([sb("d2_st%d" % i, [128, 2048], F32, ph2) for i in range(2)])
                    load_gain(gf, norm_ffn, l * D, KC, Bgf)
                    for j in range(3):
                        load_gain(cw[:, j, :], conv_w, (l * 3 + j) * 2 * FF, 44, Bcw)
                    load_gain(cb, conv_b, l * 2 * FF, 44, Bcw)
                    load_weight(wu, w_up, l * D * 2 * FF, KC, 2 * FF, (gf, Bgf), stage, Bwu)
                    load_weight(wd, w_down, l * FF * D, NFB, D, None, stage, Bwd)
                    P.flush()
                xt = Rot([sb("d2_x%d" % i, [128, D], F32, ph) for i in range(4)])
                hb = Rot([sb("d2_hb%d" % i, [128, D], BF16, ph) for i in range(2)])
                sqj = sb("d2_sq", [128, D], BF16, ph); Bsq = Buf()
                ssr = Rot([sb("d2_ss%d" % i, [128, 2], F32, ph) for i in range(3)])
                hT = Rot([sb("d2_hT%d" % i, [128, KC, T2], BF16, ph) for i in range(2)])
                ubuf = Rot([sb("d2_u%d" % i, [128, T2 + 2], F32, ph) for i in range(5)])
                ubuf.t = [(t_, (Buf(), Buf())) for (t_, _) in ubuf.t]
                cbuf = Rot([sb("d2_c%d" % i, [128, T2], F32, ph) for i in range(5)])
                gbuf = Rot([sb("d2_gt%d" % i, [128, T2], BF16, ph) for i in range(3)])
                aT = Rot([sb("d2_aT%d" % i, [128, NFB, T2], BF16, ph) for i in range(2)])
                carry = sb("d2_carry", [128, 44, 2], F32, ph); Bcarry = [Buf() for _ in range(44)]
                ptr = ph.enter_context(nc.psum_tensor(uq("d2_ptr"), [128, 1024], BF16)); Bptr = Buf()
                psx = [ph.enter_context(nc.psum_tensor(uq("d2_ps%d" % i), [128, 512], F32)) for i in range(6)]
                Bpsx = [Buf() for _ in range(6)]
                pi = 0
                last = (l == L - 1)
                pend_st = []
                for st_i in range(NTOK // T2):
                    seq = (st_i * T2) // S
                    if (st_i * T2) % S == 0:
                        mset("pool", carry[:, :, :], 0.0, w=Bcarry)
                    hTt, BhT = hT.next()
                    xs = []
                    for u in range(T2 // 128):
                        row0 = st_i * T2 + u * 128
                        x_t, Bxt = xt.next()
                        xs.append((x_t, Bxt))
                        h_b, Bhb = hb.next()
                        s_s, Bs = ssr.next()
                        dma("sp", x_t[:], XA.ap()[row0:row0 + 128, :], r=[xbuf("xa", row0 // 128)], w=[Bxt])
                        act(sqj[:], x_t[:], AF.Square, r=[Bxt], w=[Bs, Bsq], accum=s_s[:, 0:1])
                        rstd_from_ss(s_s[:, 1:2], s_s[:, 0:1], D, r=[Bs], w=[Bs])
                        ts("dve", h_b[:], x_t[:], s_s[:, 1:2], None, ALU.mult, r=[Bxt, Bs], w=[Bhb])
                        for kc in range(KC):
                            tr(ptr[:, kc * 128:(kc + 1) * 128], h_b[:, kc * 128:(kc + 1) * 128], ident_b[:],
                               r=[Bhb, B_const], w=[Bptr])
                        cp("act", hTt[:, :, u * 128:(u + 1) * 128], ptr[:, :].rearrange("p (k t) -> p k t", k=KC),
                           r=[Bptr], w=[BhT])
                    for (x_p, Bx_p, r_p) in pend_st:
                        dma("sp", XB.ap()[r_p:r_p + 128, :], x_p[:], r=[Bx_p], w=[xbuf("xb", r_p // 128)])
                    pend_st = []
                    a_t, Ba = aT.next()
                    pend_g = None

                    def gate_mul(a_t_, Ba_, fb_, cg, Bcg, cv_, Bcv):
                        g_t, Bgt = gbuf.next()
                        act(g_t[:], cg[:], AF.Silu, r=[Bcg], w=[Bgt])
                        tt("dve", a_t_[:, fb_, :], cv_[:], g_t[:], ALU.mult, r=[Bcv, Bgt], w=[Ba_])
                    for fb in range(NFB):
                        cs = []
                        for which, nb in enumerate((fb, fb + NFB)):
                            b = pi % 6; pi += 1
                            for kc in range(KC):
                                mm(psx[b][:, 0:T2], wu[:, kc, nb * 128:(nb + 1) * 128], hTt[:, kc, :], kc == 0, kc == KC - 1,
                                   r=[BhT, Bwu], w=[Bpsx[b]])
                            u_t, (Bu, Buh) = ubuf.next()
                            c_t, Bc = cbuf.next()
                            cp("pool", u_t[:, 0:2], carry[:, nb, :], r=[Bcarry[nb]], w=[Buh])
                            cp("act", u_t[:, 2:T2 + 2], psx[b][:, 0:T2], r=[Bpsx[b]], w=[Bu])
                            act(c_t[:], psx[b][:, 0:T2], AF.Identity, r=[Bpsx[b], Bcw], w=[Bc],
                                scale=cw[:, 2, nb:nb + 1], bias=cb[:, nb:nb + 1])
                            cp("pool", carry[:, nb, :], u_t[:, T2:T2 + 2], r=[Bu], w=[Bcarry[nb]])
                            cs.append((c_t, Bc, u_t, Bu, Buh, nb))
                        for tap in (1, 0):
                            for (c_t, Bc, u_t, Bu, Buh, nb) in cs:
                                stt(c_t[:], u_t[:, tap:T2 + tap], cw[:, tap, nb:nb + 1], c_t[:], ALU.mult, ALU.add,
                                    r=[Bu, Buh, Bcw], w=[Bc])
                        if pend_g is not None:
                            gate_mul(*pend_g)
                        pend_g = (a_t, Ba, fb, cs[0][0], cs[0][1], cs[1][0], cs[1][1])
                    gate_mul(*pend_g)
                    pend_g = None
                    for u in range(T2 // 128):
                        x_t, Bxt = xs[u]
                        row0 = st_i * T2 + u * 128
                        for nch in range(2):
                            b = pi % 6; pi += 1
                            for fb in range(NFB):
                                mm(psx[b][:, :], a_t[:, fb, u * 128:(u + 1) * 128], wd[:, fb, nch * 512:(nch + 1) * 512],
                                   fb == 0, fb == NFB - 1, r=[Ba, Bwd], w=[Bpsx[b]])
                            tt("dve", x_t[:, nch * 512:(nch + 1) * 512], psx[b][:, :], x_t[:, nch * 512:(nch + 1) * 512],
                               ALU.add, r=[Bpsx[b]], w=[Bxt])
                        pend_st.append((x_t, Bxt, row0))
                for (x_p, Bx_p, r_p) in pend_st:
                    dma("sp", XB.ap()[r_p:r_p + 128, :], x_p[:], r=[Bx_p], w=[xbuf("xb", r_p // 128)])
                P.flush()

        with ExitStack() as ph:
            gfin = sb("fn_g", [128, D], F32, ph); Bg = Buf()
            xt = Rot([sb("fn_x%d" % i, [128, D], F32, ph) for i in range(3)])
            yt = Rot([sb("fn_y%d" % i, [128, D], F32, ph) for i in range(3)])
            sqj = sb("fn_sq", [128, D], BF16, ph); Bsq = Buf()
            ssr = Rot([sb("fn_ss%d" % i, [128, 2], F32, ph) for i in range(3)])
            dma("sp", gfin[:], AP(norm_final, 0, [[0, 128], [1, D]]), w=[Bg])
            for t in range(NTOK // 128):
                x_t, Bxt = xt.next()
                y_t, Byt = yt.next()
                s_s, Bs = ssr.next()
                dma("sp", x_t[:], XB.ap()[t * 128:(t + 1) * 128, :], r=[xbuf("xb", t)], w=[Bxt])
                act(sqj[:], x_t[:], AF.Square, r=[Bxt], w=[Bs, Bsq], accum=s_s[:, 0:1])
                rstd_from_ss(s_s[:, 1:2], s_s[:, 0:1], D, r=[Bs], w=[Bs])
                stt(y_t[:], x_t[:], s_s[:, 1:2], gfin[:], ALU.mult, ALU.mult, r=[Bxt, Bs, Bg], w=[Byt])
                dma("pool", y_out.ap()[t * 128:(t + 1) * 128, :], y_t[:], r=[Byt], w=[Buf()])
            P.flush()
        print("ops", P.nops, "waits", P.nwaits, flush=True)
    return nc


S_FULL = 4096
NSEQ_FULL = 2
L_FULL = 4
NCORES = 8
_CACHE = {}


def kernel(**inputs):
    f32 = lambda a: np.ascontiguousarray(np.asarray(a, dtype=np.float32))
    x = f32(inputs["x"])
    mem = f32(inputs["mem"])
    B = x.shape[0]
    per = B // NCORES
    consts = host_consts()
    shared = {k: f32(inputs[k]) for k in ("rel_bias", "norm_mix", "w_in", "lam_q1", "lam_k1", "lam_q2", "lam_k2",
                                          "subln", "w_out", "norm_mem", "norm_memkv", "w_mq", "w_mkv", "w_mo",
                                          "norm_ffn", "w_up", "conv_w", "conv_b", "w_down")}
    shared["norm_final"] = f32(inputs["norm_final"]).reshape(1, D)
    shared.update(consts)
    key = "full"
    if key not in _CACHE:
        _CACHE[key] = build(S_FULL, per, L_FULL, 256)
    nc = _CACHE[key]
    in_maps = []
    for c in range(NCORES):
        m = dict(shared)
        m["x"] = np.ascontiguousarray(x[c * per:(c + 1) * per].reshape(per * S_FULL, D))
        m["mem"] = np.ascontiguousarray(mem[c * per:(c + 1) * per].reshape(per * MEM, D))
        in_maps.append(m)
    res = run_bass_kernel_spmd(nc, in_maps, core_ids=list(range(NCORES)))
    out = np.concatenate([np.asarray(r["y"]).reshape(per, S_FULL, D) for r in res.results], axis=0)
    return out.astype(np.float32)
```

```python
import math
from contextlib import ExitStack
import numpy as np
import concourse.bass as bass
import concourse.mybir as mybir
from concourse.bass_utils import run_bass_kernel_spmd

F32 = mybir.dt.float32
BF16 = mybir.dt.bfloat16
AF = mybir.ActivationFunctionType
ALU = mybir.AluOpType
AX = mybir.AxisListType

D = 1024
KC = 8
NIN = 3656
FF = 2816
NFB = 22
MEM = 256
NG = 3200
MS = 3072
DOFF = 511
DFAR = 2176
EPS = 1e-6
NEGM = -30000.0
NBIS = 14


class Buf:
    __slots__ = ("name", "w", "r")

    def __init__(self, name=""):
        self.name = name
        self.w = {}
        self.r = {}


class Stream:
    def __init__(self, name):
        self.name = name
        self.ops = []
        self.cnt = 0
        self.dcnt = 0
        self.slot = [0] * NSLOT
        self.known = {}


NSLOT = 12


class Prog:
    STREAMS = ("pe", "act", "dve", "pool", "sp")

    def __init__(self, nc, es):
        self.nc = nc
        self.streams = {n: Stream(n) for n in self.STREAMS}
        self.sems = {}
        for n in self.STREAMS:
            self.sems[n] = es.enter_context(nc.semaphore("s_" + n))
        for n in ("pool", "sp"):
            for k in range(NSLOT):
                self.sems["%s.d%d" % (n, k)] = es.enter_context(nc.semaphore("d_%s%d" % (n, k)))
        self.nops = 0
        self.nwaits = 0

    def _deps(self, st, reads, writes, dma=False, extra=None):
        deps = {}
        own_d = st.name + ".d"

        def add(k, v):
            if k == st.name and k == "pe":
                return
            if deps.get(k, 0) < v:
                deps[k] = v
        for b in reads:
            for k, v in b.w.items():
                add(k, v)
        for b in writes:
            for k, v in b.w.items():
                if dma and k.startswith(own_d):
                    continue
                add(k, v)
            for k, v in b.r.items():
                add(k, v)
        if extra is not None:
            add(*extra)
        out = {}
        for k, v in deps.items():
            if st.known.get(k, 0) < v:
                st.known[k] = v
                out[k] = v
        return out

    def op(self, stream, fn, reads=(), writes=(), dma=False):
        st = self.streams[stream]
        if dma:
            slot = st.dcnt % NSLOT
            st.dcnt += 1
            kname = "%s.d%d" % (stream, slot)
            extra = (kname, st.slot[slot]) if st.slot[slot] else None
            waits = self._deps(st, reads, writes, True, extra)
            st.slot[slot] += 1
            key = (kname, st.slot[slot])
        else:
            waits = self._deps(st, reads, writes, False)
            st.cnt += 1
            key = (stream, st.cnt)
        st.ops.append((waits, fn, key[0] if dma else None))
        for b in reads:
            if b.r.get(key[0], 0) < key[1]:
                b.r[key[0]] = key[1]
        for b in writes:
            if dma:
                b.w = {k: v for k, v in b.w.items() if k.startswith(stream + ".d")}
                b.w[key[0]] = key[1]
            else:
                b.w = {key[0]: key[1]}
            b.r = {}
        self.nops += 1
        self.nwaits += len(waits)
        return key

    def flush(self):
        nc = self.nc
        sems = self.sems
        fin = {}
        for n, st in self.streams.items():
            if st.cnt:
                fin[n] = st.cnt
            for k in range(NSLOT):
                if st.slot[k]:
                    fin["%s.d%d" % (n, k)] = st.slot[k]

        def val(k, v):
            return v * 16 if ".d" in k else v

        def run(stname):
            st = self.streams[stname]
            ops = st.ops
            st.ops = []

            def body(e):
                for waits, fn, dkey in ops:
                    for k, v in waits.items():
                        e.wait_ge(sems[k], val(k, v))
                    ins = fn(e)
                    if dkey is not None:
                        ins.then_inc(sems[dkey], 16)
                    else:
                        ins.then_inc(sems[stname], 1)
                for k, v in fin.items():
                    if k == stname:
                        continue
                    if st.known.get(k, 0) < v:
                        e.wait_ge(sems[k], val(k, v))
                        st.known[k] = v
            return body

        with nc.Block() as block:
            block.tensor(run("pe"))
            block.scalar(run("act"))
            block.vector(run("dve"))
            block.gpsimd(run("pool"))
            block.sync(run("sp"))


class Rot:
    def __init__(self, tiles):
        self.t = [(t, Buf()) for t in tiles]
        self.i = 0

    def next(self):
        r = self.t[self.i % len(self.t)]
        self.i += 1
        return r


def rel_bucket_np(n):
    n = np.maximum(n, 0)
    nf = np.maximum(n, 1).astype(np.float32)
    large = 16 + (np.log(nf / np.float32(16)) / np.float32(math.log(2048 / 16)) * np.float32(16)).astype(np.int32)
    large = np.minimum(large, 31)
    return np.where(n < 16, n, large)


def host_consts():
    d = np.arange(NG, dtype=np.int64) - DOFF
    oh = np.zeros((34, NG), np.float32)
    bk = rel_bucket_np(d.astype(np.int32))
    valid = d >= 0
    oh[bk[valid], np.nonzero(valid)[0]] = 1.0
    oh[32] = np.where(valid, 0.0, NEGM)
    mult = ((d <= 128).astype(np.int64) + ((d % 4 == 0) & (d <= 512)) + ((d % 16 == 0) & (d <= 2048)))
    mult = np.where(valid, mult, 0)
    oh[33] = np.where(mult > 0, 8.0 * np.log(np.maximum(mult, 1)).astype(np.float32), NEGM)
    sel = np.zeros((2, 12), np.float32)
    sel[0, 0:4] = 1.0
    sel[0, 8:12] = 1.0
    sel[1, 4:8] = 1.0
    ident = np.eye(128, dtype=np.float32)
    J = np.ascontiguousarray(ident[::-1])
    tri = np.where(np.arange(128)[None, :] <= np.arange(128)[:, None], 0.0, -1e30).astype(np.float32)
    return {"c_onehot": oh, "c_sel": sel, "c_ident": ident, "c_J": J, "c_tri": tri}


def build(S, NSEQ, L, TOPK, debug_outs=False):
    nc = bass.Bass("TRN2", target_bir_lowering=False)
    NB = S // 128
    NT = S // 512
    NTOK = NSEQ * S

    NAMES = {}

    def din(name, shape, dt=F32):
        t = nc.dram_tensor(name, list(shape), dt, kind="ExternalInput")
        NAMES[id(t)] = name
        return t

    def dscr(name, shape, dt=BF16):
        t = nc.dram_tensor(name, list(shape), dt, kind="ExternalOutput" if debug_outs else "Internal")
        NAMES[id(t)] = name
        return t

    def nm(t):
        return NAMES[id(t)]

    x_in = din("x", [NTOK, D])
    mem_in = din("mem", [NSEQ * MEM, D])
    rel_bias = din("rel_bias", [32, 12])
    norm_mix = din("norm_mix", [L, D])
    w_in = din("w_in", [L, D, NIN])
    lam_q1 = din("lam_q1", [L, 64]); lam_k1 = din("lam_k1", [L, 64])
    lam_q2 = din("lam_q2", [L, 64]); lam_k2 = din("lam_k2", [L, 64])
    subln = din("subln", [L, 128])
    w_out = din("w_out", [L, D, D])
    norm_mem = din("norm_mem", [L, D]); norm_memkv = din("norm_memkv", [L, D])
    w_mq = din("w_mq", [L, D, D]); w_mkv = din("w_mkv", [L, D, 2 * D]); w_mo = din("w_mo", [L, D, D])
    norm_ffn = din("norm_ffn", [L, D])
    w_up = din("w_up", [L, D, 2 * FF]); conv_w = din("conv_w", [L, 3, 2 * FF]); conv_b = din("conv_b", [L, 2 * FF])
    w_down = din("w_down", [L, FF, D])
    norm_final = din("norm_final", [1, D])
    c_onehot = din("c_onehot", [34, NG]); c_sel = din("c_sel", [2, 12])
    c_ident = din("c_ident", [128, 128]); c_J = din("c_J", [128, 128]); c_tri = din("c_tri", [128, 128])
    y_out = nc.dram_tensor("y", [NTOK, D], F32, kind="ExternalOutput")

    QTA = dscr("QTA", [512, NSEQ, S]); KTA = dscr("KTA", [512, NSEQ, S])
    QTB = dscr("QTB", [256, NSEQ, S]); KTB = dscr("KTB", [256, NSEQ, S])
    QTC = dscr("QTC", [256, NSEQ, S]); KTC = dscr("KTC", [256, NSEQ, S])
    IQT = dscr("IQT", [512, NSEQ, S]); IKT = dscr("IKT", [64, NSEQ, S])
    VA = dscr("VA", [NTOK, 512]); VB = dscr("VB", [NTOK, 256]); VC = dscr("VC", [NTOK, 256])
    IW = dscr("IW", [NTOK, 8], F32)
    OS = dscr("OS", [NTOK, D])
    MASK = dscr("MASK", [S, S])
    XA = dscr("XA", [NTOK, D], F32); XB = dscr("XB", [NTOK, D], F32)
    GSCR = dscr("GSCR", [12, NG])
    ESCR = dscr("ESCR", [12, NG])

    def AP(t, offset, ap):
        return bass.AP(tensor=t, offset=offset, ap=[list(a) for a in ap])

    es = ExitStack()
    with es:
        P = Prog(nc, es)

        _uid = [0]

        def uq(name):
            _uid[0] += 1
            return "%s_%d" % (name, _uid[0])

        def sb(name, shape, dt=BF16, stack=es):
            return stack.enter_context(nc.sbuf_tensor(uq(name), list(shape), dt))

        def mm(out, lhsT, rhs, start, stop, r=(), w=()):
            P.op("pe", lambda e: e.matmul(out, lhsT=lhsT, rhs=rhs, start=start, stop=stop, skip_group_check=True), r, w)

        def tr(out, in_, ident, r=(), w=()):
            P.op("pe", lambda e: e.transpose(out, in_, ident), r, w)

        def act(out, in_, func, r=(), w=(), bias=None, scale=None, accum=None):
            kw = {}
            if bias is not None:
                kw["bias"] = bias
            if scale is not None:
                kw["scale"] = scale
            if accum is not None:
                kw["accum_out"] = accum
            P.op("act", lambda e: e.activation(out=out, in_=in_, func=func, **kw), r, w)

        def ts(eng, out, in0, s1, s2, op0, op1=None, r=(), w=(), accum=None):
            kw = {}
            if op1 is not None:
                kw["op1"] = op1
            if accum is not None:
                kw["accum_out"] = accum
            P.op(eng, lambda e: e.tensor_scalar(out=out, in0=in0, scalar1=s1, scalar2=s2, op0=op0, **kw), r, w)

        def tt(eng, out, in0, in1, op, r=(), w=()):
            P.op(eng, lambda e: e.tensor_tensor(out=out, in0=in0, in1=in1, op=op), r, w)

        def stt(out, in0, scalar, in1, op0, op1, r=(), w=()):
            P.op("dve", lambda e: e.scalar_tensor_tensor(out=out, in0=in0, scalar=scalar, in1=in1, op0=op0, op1=op1), r, w)

        def cp(eng, out, in_, r=(), w=()):
            if eng == "act":
                P.op("act", lambda e: e.activation(out=out, in_=in_, func=AF.Copy), r, w)
            else:
                P.op(eng, lambda e: e.tensor_copy(out=out, in_=in_), r, w)

        def red(out, in_, op, r=(), w=()):
            P.op("dve", lambda e: e.tensor_reduce(out=out, in_=in_, axis=AX.X, op=op), r, w)

        def recip(out, in_, r=(), w=()):
            P.op("dve", lambda e: e.reciprocal(out=out, in_=in_), r, w)

        def mset(eng, ap, val, r=(), w=()):
            P.op(eng, lambda e: e.memset(ap, val), r, w)

        def dma(q, out, in_, r=(), w=(), slow=False):
            if slow:
                P.op(q, lambda e: e.dma_start(out=out, in_=in_, allow_slow_non_contiguous=True), r, w, dma=True)
            else:
                P.op(q, lambda e: e.dma_start(out=out, in_=in_), r, w, dma=True)

        def rstd_from_ss(rs, ss, n, r, w):
            act(rs, ss, AF.Ln, r=r, w=w, scale=1.0 / n, bias=epsb[:, 0:1])
            act(rs, rs, AF.Exp, r=w, w=w, scale=-0.5)

        ident_f = sb("ident_f", [128, 128], F32)
        ident_b = sb("ident_b", [128, 128])
        J_b = sb("J_b", [128, 128])
        tri_f = sb("tri_f", [128, 128], F32)
        ones_b = sb("ones_b", [128, 128])
        epsb = sb("epsb", [128, 1], F32)
        pw2 = sb("pw2", [128, NBIS], F32)
        farb = sb("farb", [128, 12], F32)
        memhatT = sb("memhatT", [128, NSEQ, KC, MEM])
        B_const = Buf("const")
        B_memhat = Buf("memhat")


        with ExitStack() as ph:
            tmpf = sb("s0_tmpf", [128, 128], F32, ph)
            tbl = sb("s0_tbl", [34, 12], F32, ph)
            oh = sb("s0_oh", [34, NG], F32, ph)
            gv = sb("s0_gv", [12, NG], BF16, ph)
            ev = sb("s0_ev", [12, NG], BF16, ph)
            mt = sb("s0_mt", [128, D], F32, ph)
            mb = sb("s0_mb", [128, D], BF16, ph)
            sq = sb("s0_sq", [128, D], BF16, ph)
            ss = sb("s0_ss", [128, 1], F32, ph)
            rs = sb("s0_rs", [128, 1], F32, ph)
            ptr = ph.enter_context(nc.psum_tensor(uq("s0_ptr"), [128, 1024], BF16))
            ps0 = ph.enter_context(nc.psum_tensor(uq("s0_ps"), [128, 512], F32))
            Bps0 = Buf()
            Bt, Btbl, Boh, Bgv, Bmt, Bmb, Bss, Bptr = [Buf() for _ in range(8)]
            dma("sp", ident_f[:], c_ident.ap(), w=[B_const])
            cp("dve", ident_b[:], ident_f[:], r=[B_const], w=[B_const])
            dma("sp", tmpf[:], c_J.ap(), w=[Bt])
            cp("dve", J_b[:], tmpf[:], r=[Bt], w=[B_const])
            dma("sp", tri_f[:], c_tri.ap(), w=[B_const])
            dma("sp", farb[:], AP(rel_bias, 31 * 12, [[0, 128], [1, 12]]), w=[B_const])
            mset("dve", ones_b[:], 1.0, w=[B_const])
            mset("dve", epsb[:], EPS, w=[B_const])
            for k_ in range(NBIS):
                mset("dve", pw2[:, k_:k_ + 1], float(2.0 ** -(k_ + 1)), w=[B_const])
            dma("sp", tbl[0:32, :], rel_bias.ap(), w=[Btbl])
            dma("sp", tbl[32:34, :], c_sel.ap(), w=[Btbl])
            ts("dve", tbl[0:32, :], tbl[0:32, :], 8.0, None, ALU.mult, r=[Btbl], w=[Btbl])
            dma("sp", oh[:], c_onehot.ap(), w=[Boh])
            for c in range((NG + 511) // 512):
                n = min(512, NG - c * 512)
                mm(ps0[0:12, 0:n], tbl[:, :], oh[:, c * 512:c * 512 + n], True, True, r=[Btbl, Boh], w=[Bps0])
                cp("dve", gv[:, c * 512:c * 512 + n], ps0[0:12, 0:n], r=[Bps0], w=[Bgv])
                act(ev[:, c * 512:c * 512 + n], ps0[0:12, 0:n], AF.Exp, r=[Bps0], w=[Bgv], scale=0.125)
            B_gscr = Buf("gscr")
            dma("sp", GSCR.ap(), gv[:], r=[Bgv], w=[B_gscr])
            dma("sp", ESCR.ap(), ev[:], r=[Bgv], w=[B_gscr])
            for sq_i in range(NSEQ):
                for blk in range(MEM // 128):
                    dma("sp", mt[:], mem_in.ap()[sq_i * MEM + blk * 128: sq_i * MEM + (blk + 1) * 128, :], w=[Bmt])
                    act(sq[:], mt[:], AF.Square, r=[Bmt], w=[Bss], accum=ss[:, 0:1])
                    rstd_from_ss(rs[:, 0:1], ss[:, 0:1], D, r=[Bss], w=[Bss])
                    ts("dve", mb[:], mt[:], rs[:, 0:1], None, ALU.mult, r=[Bmt, Bss], w=[Bmb])
                    for kc in range(KC):
                        tr(ptr[:, kc * 128:(kc + 1) * 128], mb[:, kc * 128:(kc + 1) * 128], ident_b[:],
                           r=[Bmb, B_const], w=[Bptr])
                    cp("dve", memhatT[:, sq_i, :, blk * 128:(blk + 1) * 128],
                       ptr[:, :].rearrange("p (k t) -> p k t", k=KC), r=[Bptr], w=[B_memhat])
            P.flush()

        def load_weight(dst, src_t, src_off, nk, N, gain, stage, Bdst, row_stride=None):
            rs_ = N if row_stride is None else row_stride
            CH = 2048
            for kc in range(nk):
                for c0 in range(0, N, CH):
                    n = min(CH, N - c0)
                    st, Bst = stage.next()
                    dma("sp", st[:, 0:n], AP(src_t, src_off + kc * 128 * rs_ + c0, [[rs_, 128], [1, n]]), w=[Bst])
                    if gain is None:
                        cp("dve", dst[:, kc, c0:c0 + n], st[:, 0:n], r=[Bst], w=[Bdst])
                    else:
                        g, Bg = gain
                        ts("dve", dst[:, kc, c0:c0 + n], st[:, 0:n], g[:, kc:kc + 1], None, ALU.mult,
                           r=[Bst, Bg], w=[Bdst])

        def load_gain(dst, src_t, off, n, Bd):
            dma("sp", dst[:, 0:n], AP(src_t, off, [[1, 128], [128, n]]), w=[Bd], slow=True)

        B_x = {}

        def xbuf(which, t):
            return B_x.setdefault((which, t), Buf())
        B_scr = {}

        def scr(name, *idx):
            return B_scr.setdefault((name,) + idx, Buf())

        for l in range(L):
            lam_init = 0.8 - 0.6 * math.exp(-0.3 * l)
            xsrc = (x_in, "x") if l == 0 else (XB, "xb")

            with ExitStack() as ph:
                wsb = sb("p1_w", [128, KC, NIN], BF16, ph)
                stage = Rot([sb("p1_st%d" % i, [128, 2048], F32, ph) for i in range(2)])
                gmix = sb("p1_g", [128, KC], F32, ph)
                xt = Rot([sb("p1_x%d" % i, [128, D], F32, ph) for i in range(3)])
                hb = Rot([sb("p1_hb%d" % i, [128, D], BF16, ph) for i in range(2)])
                sqj = sb("p1_sq", [128, D], BF16, ph)
                ssr = Rot([sb("p1_ss%d" % i, [128, 2], F32, ph) for i in range(3)])
                hT = Rot([sb("p1_hT%d" % i, [128, KC, 512], BF16, ph) for i in range(2)])
                stF = Rot([sb("p1_sf%d" % i, [128, 4, 512], BF16, ph) for i in range(3)])
                stT = Rot([sb("p1_stt%d" % i, [128, 512], BF16, ph) for i in range(3)])
                stW = Rot([sb("p1_sw%d" % i, [128, 8], F32, ph) for i in range(2)])
                ptr = ph.enter_context(nc.psum_tensor(uq("p1_ptr"), [128, 1024], BF16))
                psx = [ph.enter_context(nc.psum_tensor(uq("p1_ps%d" % i), [128, 512], F32)) for i in range(5)]
                Bpsx = [Buf() for _ in range(5)]
                Bptr, Bw, Bg, Bsq = Buf(), Buf(), Buf(), Buf()
                load_gain(gmix, norm_mix, l * D, KC, Bg)
                load_weight(wsb, w_in, l * D * NIN, KC, NIN, (gmix, Bg), stage, Bw)
                FG = [(0, 4, 128, QTA), (512, 4, 128, KTA), (1536, 2, 128, QTB), (1792, 2, 128, KTB),
                      (2304, 2, 128, QTC), (2560, 2, 128, KTC), (3072, 4, 128, IQT), (3584, 1, 64, IKT)]
                TG = [(1024, 512, VA), (2048, 256, VB), (2816, 256, VC)]
                pi = 0
                for st_i in range(NTOK // 512):
                    seq = (st_i * 512) // S
                    t0 = st_i * 512 - seq * S
                    hTt, BhT = hT.next()
                    for u in range(4):
                        row0 = st_i * 512 + u * 128
                        x_t, Bxt = xt.next()
                        h_b, Bhb = hb.next()
                        s_s, Bs = ssr.next()
                        dma("sp", x_t[:], xsrc[0].ap()[row0:row0 + 128, :], r=[xbuf(xsrc[1], row0 // 128)], w=[Bxt])
                        act(sqj[:], x_t[:], AF.Square, r=[Bxt], w=[Bs, Bsq], accum=s_s[:, 0:1])
                        rstd_from_ss(s_s[:, 1:2], s_s[:, 0:1], D, r=[Bs], w=[Bs])
                        ts("dve", h_b[:], x_t[:], s_s[:, 1:2], None, ALU.mult, r=[Bxt, Bs], w=[Bhb])
                        for kc in range(KC):
                            tr(ptr[:, kc * 128:(kc + 1) * 128], h_b[:, kc * 128:(kc + 1) * 128], ident_b[:],
                               r=[Bhb, B_const], w=[Bptr])
                        cp("act", hTt[:, :, u * 128:(u + 1) * 128], ptr[:, :].rearrange("p (k t) -> p k t", k=KC),
                           r=[Bptr], w=[BhT])
                        for (c0, ncol, dt_) in TG:
                            b = pi % 5; pi += 1
                            for kc in range(KC):
                                mm(psx[b][:, 0:ncol], hTt[:, kc, u * 128:(u + 1) * 128], wsb[:, kc, c0:c0 + ncol],
                                   kc == 0, kc == KC - 1, r=[BhT, Bw], w=[Bpsx[b]])
                            s_t, Bst_ = stT.next()
                            cp("dve", s_t[:, 0:ncol], psx[b][:, 0:ncol], r=[Bpsx[b]], w=[Bst_])
                            dma("pool", dt_.ap()[row0:row0 + 128, :], s_t[:, 0:ncol], r=[Bst_],
                                w=[scr(nm(dt_), seq, st_i)])
                        b = pi % 5; pi += 1
                        for kc in range(KC):
                            mm(psx[b][:, 0:8], hTt[:, kc, u * 128:(u + 1) * 128], wsb[:, kc, 3648:3656],
                               kc == 0, kc == KC - 1, r=[BhT, Bw], w=[Bpsx[b]])
                        s_w, Bsw = stW.next()
                        cp("dve", s_w[:, :], psx[b][:, 0:8], r=[Bpsx[b]], w=[Bsw])
                        dma("pool", IW.ap()[row0:row0 + 128, :], s_w[:, :], r=[Bsw], w=[scr("IW", seq, st_i)])
                    for gi, (c0, nblk, bs, dt_) in enumerate(FG):
                        s_f, Bsf = stF.next()
                        for bi in range(nblk):
                            b = pi % 5; pi += 1
                            cc = c0 + bi * bs
                            for kc in range(KC):
                                mm(psx[b][0:bs, :], wsb[:, kc, cc:cc + bs], hTt[:, kc, :], kc == 0, kc == KC - 1,
                                   r=[BhT, Bw], w=[Bpsx[b]])
                            cp("act" if (bi % 2 == 0) else "dve", s_f[0:bs, bi, :], psx[b][0:bs, :], r=[Bpsx[b]], w=[Bsf])
                        dst = AP(dt_, seq * S + t0, [[NSEQ * S, bs], [bs * NSEQ * S, nblk], [1, 512]])
                        dma("pool", dst, s_f[0:bs, 0:nblk, :], r=[Bsf], w=[scr(nm(dt_), seq, st_i)])
                P.flush()

            for seq in range(NSEQ):
                def scr_all(name):
                    return [scr(name, seq, st_i) for st_i in range(seq * NT, (seq + 1) * NT)]

                def att_common(ph, tag):
                    C = {}
                    C["strips"] = sb(tag + "strips", [128, 4, MS], BF16, ph)
                    C["Bstrips"] = Buf()
                    C["kT"] = [[sb(tag + "kT%d%d" % (s_, m), [128, S], BF16, ph) for m in range(2)] for s_ in range(2)]
                    C["qT"] = [sb(tag + "qT%d" % s_, [128, S], BF16, ph) for s_ in range(2)]
                    C["V"] = [sb(tag + "V%d" % s_, [128, NB, 130], BF16, ph) for s_ in range(2)]
                    C["oh"] = [sb(tag + "oh%d" % s_, [128, NB, 128], BF16, ph) for s_ in range(2)]
                    C["Bin"] = [Buf(), Buf()]
                    C["Boh"] = [Buf(), Buf()]
                    C["pT"] = Rot([sb(tag + "pT%d" % i, [128, 2, 512], BF16, ph) for i in range(4)])
                    C["pE"] = Rot([sb(tag + "pE%d" % i, [128, 2, 512], BF16, ph) for i in range(3)])
                    C["stmp"] = Rot([sb(tag + "stmp%d" % i, [128, MS], BF16, ph) for i in range(1)])
                    C["sm"] = Rot([sb(tag + "sm%d" % i, [128, 8], F32, ph) for i in range(16)])
                    C["t0"] = Rot([sb(tag + "t0%d" % i, [128, 128], F32, ph) for i in range(4)])
                    C["o"] = Rot([sb(tag + "o%d" % i, [128, 128], F32, ph) for i in range(4)])
                    C["sqj4"] = [sb(tag + "sqj4%d" % i, [128, 128], BF16, ph) for i in range(4)]
                    C["Bsqj4"] = [Buf() for _ in range(4)]
                    C["sqj"] = sb(tag + "sqj", [128, 128], BF16, ph)
                    C["Bsqj"] = Buf()
                    C["acc"] = [[ph.enter_context(nc.psum_tensor(uq(tag + "acc%d%d" % (m, k)), [128, 2, 256], F32))
                                 for k in range(2)] for m in range(2)]
                    C["Bacc"] = [[Buf(), Buf()], [Buf(), Buf()]]
                    C["sc2"] = Rot([ph.enter_context(nc.psum_tensor(uq(tag + "sc%d" % i), [128, 2, 512], F32)) for i in range(2)])
                    C["set"] = 0
                    for s_ in range(2):
                        mset("pool", C["kT"][s_][0][64:128, :], 0.0, w=[C["Bin"][s_]])
                        mset("pool", C["kT"][s_][1][0:64, :], 0.0, w=[C["Bin"][s_]])
                    return C

                def load_strips(C, h0):
                    for hh in range(4):
                        tmp, Btmp = C["stmp"].next()
                        dma("sp", tmp[:, :], AP(ESCR, (h0 + hh) * NG, [[1, 128], [1, MS]]), r=[B_gscr], w=[Btmp])
                        for c in range(MS // 512):
                            ps, Bps = C["sc2"].next()
                            mm(ps[:, 0, :], J_b[:], tmp[:, c * 512:(c + 1) * 512], True, True, r=[Btmp, B_const], w=[Bps])
                            cp("dve" if c % 2 else "act", C["strips"][:, hh, c * 512:(c + 1) * 512], ps[:, 0, :],
                               r=[Bps], w=[C["Bstrips"]])

                def att_pair(C, kind, qt, kt, row0, vloads, sidx, scale, jmin_fn, evac, ocol, dst_cols,
                             mask=None, hidx=None):
                    s_ = C["set"]; C["set"] ^= 1
                    Bin = C["Bin"][s_]
                    kT0, kT1 = C["kT"][s_]
                    qT = C["qT"][s_]
                    V = C["V"][s_]
                    oh_t = C["oh"][s_]; Boh = C["Boh"][s_]
                    rq = scr_all(nm(qt)); rk = scr_all(nm(kt))
                    dma("sp", qT[:, :], AP(qt, row0 * NSEQ * S + seq * S, [[NSEQ * S, 128], [1, S]]), r=rq, w=[Bin])
                    dma("sp", kT0[0:64, :], AP(kt, row0 * NSEQ * S + seq * S, [[NSEQ * S, 64], [1, S]]), r=rk, w=[Bin])
                    dma("sp", kT1[64:128, :], AP(kt, (row0 + 64) * NSEQ * S + seq * S, [[NSEQ * S, 64], [1, S]]), r=rk, w=[Bin])
                    for (vt, vc0, vn, dcol) in vloads:
                        W_ = {"VA": 512, "VB": 256, "VC": 256}[nm(vt)]
                        dma("sp", V[:, :, dcol:dcol + vn],
                            AP(vt, seq * S * W_ + vc0, [[W_, 128], [128 * W_, NB], [1, vn]]), r=scr_all(nm(vt)), w=[Bin])
                        mset("pool", V[:, :, dcol + vn:dcol + vn + 1], 1.0, w=[Bin])
                    kTs = (kT0, kT1)
                    nv, vcol = (129, (0, 0)) if kind == "A" else (65, (0, 65))
                    acc = C["acc"]; Bacc = C["Bacc"]
                    def pv(job):
                        (i_, j_, c0_, j0_, pT_, BpT_) = job
                        for m_ in range(2):
                            for u in range(c0_ // 128, 4):
                                mm(acc[m_][u // 2][:, u % 2, 0:nv], pT_[:, m_, u * 128:(u + 1) * 128],
                                   V[:, j_, vcol[m_]:vcol[m_] + nv], (j_ == j0_ and u % 2 == 0), (j_ == 4 * i_ + u),
                                   r=[BpT_, Bin], w=[Bacc[m_][u // 2]])

                    for i in range(NT):
                        if mask is not None:
                            mk, Bmk = mask(i)
                        j0 = jmin_fn(i)
                        pending = []
                        for j in range(j0, 4 * i + 4):
                            c0 = max(0, j - 4 * i) * 128
                            D0 = 512 * i - 128 * j
                            off = min(D0, DFAR) + 384
                            far = (hidx is not None) and D0 >= 1664
                            ps, Bps = C["sc2"].next()
                            for m in range(2):
                                mm(ps[:, m, c0:512], kTs[m][:, j * 128:(j + 1) * 128], qT[:, i * 512 + c0:(i + 1) * 512],
                                   True, mask is None, r=[Bin], w=[Bps])
                                if mask is not None:
                                    for u in range(c0 // 128, 4):
                                        mm(ps[:, m, u * 128:(u + 1) * 128], mk[:, u, j * 128:(j + 1) * 128], ident_b[:],
                                           False, u == 3, r=[Bmk, B_const], w=[Bps])
                            pT, BpT = C["pT"].next()
                            if far and hidx[0] == hidx[1]:
                                act(pT[:, :, c0:512], ps[:, :, c0:512], AF.Exp, r=[Bps, B_const], w=[BpT], scale=scale,
                                    bias=farb[:, hidx[0]:hidx[0] + 1])
                            elif far:
                                for m in range(2):
                                    act(pT[:, m, c0:512], ps[:, m, c0:512], AF.Exp, r=[Bps, B_const], w=[BpT], scale=scale,
                                        bias=farb[:, hidx[m]:hidx[m] + 1])
                            else:
                                pR, BpR = C["pE"].next()
                                act(pR[:, :, c0:512], ps[:, :, c0:512], AF.Exp, r=[Bps], w=[BpR], scale=scale)
                                for m in range(2):
                                    tt("dve", pT[:, m, c0:512], pR[:, m, c0:512],
                                       C["strips"][:, sidx[m], off + c0:off + 512], ALU.mult,
                                       r=[BpR, C["Bstrips"]], w=[BpT])
                            pending.append((i, j, c0, j0, pT, BpT))
                            if len(pending) > 2:
                                pv(pending.pop(0))
                        while pending:
                            pv(pending.pop(0))
                        evac(C, i, oh_t, Boh)
                    W_ = D
                    dma("pool", AP(OS, seq * S * W_ + ocol, [[W_, 128], [128 * W_, NB], [1, dst_cols]]),
                        oh_t[:, :, 0:dst_cols], r=[Boh], w=[scr("OS", seq, ocol)])

                def evac_bc(C, i, oh_t, Boh):
                    acc = C["acc"]; Bacc = C["Bacc"]
                    U = []
                    for m in range(2):
                        for u in range(4):
                            sm, Bsm = C["sm"].next()
                            U.append((m, u, sm, Bsm, acc[m][u // 2], Bacc[m][u // 2]))
                    for (m, u, sm, Bsm, a, Ba_) in U:
                        recip(sm[:, 0:1], a[:, u % 2, 64:65], r=[Ba_], w=[Bsm])
                    for (m, u, sm, Bsm, a, Ba_) in U:
                        ts("dve", oh_t[:, 4 * i + u, m * 64:(m + 1) * 64], a[:, u % 2, 0:64], sm[:, 0:1], None,
                           ALU.mult, r=[Ba_, Bsm], w=[Boh])

                with ExitStack() as ph:
                    C = att_common(ph, "ab_")
                    lamv = sb("ab_lam", [128, 4, 64], F32, ph)
                    lsm = sb("ab_lsm", [128, 8], F32, ph)
                    prod = sb("ab_prod", [128, 64], F32, ph)
                    gs = sb("ab_gs", [128, 128], F32, ph)
                    Blam, Bgs = Buf(), Buf()
                    for qi, t_ in enumerate((lam_q1, lam_k1, lam_q2, lam_k2)):
                        dma("sp", lamv[:, qi, :], AP(t_, l * 64, [[0, 128], [1, 64]]), w=[Blam])
                    for qi in range(2):
                        tt("dve", prod[:], lamv[:, 2 * qi, :], lamv[:, 2 * qi + 1, :], ALU.mult, r=[Blam], w=[Blam])
                        red(lsm[:, qi:qi + 1], prod[:], ALU.add, r=[Blam], w=[Blam])
                        act(lsm[:, qi:qi + 1], lsm[:, qi:qi + 1], AF.Exp, r=[Blam], w=[Blam])
                    tt("dve", lsm[:, 2:3], lsm[:, 1:2], lsm[:, 0:1], ALU.subtract, r=[Blam], w=[Blam])
                    ts("dve", lsm[:, 2:3], lsm[:, 2:3], -lam_init, None, ALU.add, r=[Blam], w=[Blam])
                    dma("sp", gs[:], AP(subln, l * 128, [[0, 128], [1, 128]]), w=[Bgs])
                    ts("dve", gs[:], gs[:], 1.0 - lam_init, None, ALU.mult, r=[Bgs], w=[Bgs])
                    neglam = lsm[:, 2:3]

                    def evac_a(C, i, oh_t, Boh):
                        acc = C["acc"]; Bacc = C["Bacc"]
                        U = []
                        for u in range(4):
                            sm, Bsm = C["sm"].next()
                            t0, Bt0 = C["t0"].next()
                            o_, Bo = C["o"].next()
                            U.append((u, sm, Bsm, t0, Bt0, o_, Bo, acc[0][u // 2], acc[1][u // 2], Bacc[0][u // 2], Bacc[1][u // 2]))
                        for (u, sm, Bsm, t0, Bt0, o_, Bo, a0, a1, B0, B1) in U:
                            recip(sm[:, 0:1], a0[:, u % 2, 128:129], r=[B0], w=[Bsm])
                        for (u, sm, Bsm, t0, Bt0, o_, Bo, a0, a1, B0, B1) in U:
                            recip(sm[:, 1:2], a1[:, u % 2, 128:129], r=[B1], w=[Bsm])
                        for (u, sm, Bsm, t0, Bt0, o_, Bo, a0, a1, B0, B1) in U:
                            tt("dve", sm[:, 1:2], sm[:, 1:2], neglam, ALU.mult, r=[Blam], w=[Bsm])
                        for (u, sm, Bsm, t0, Bt0, o_, Bo, a0, a1, B0, B1) in U:
                            act(t0[:], a0[:, u % 2, 0:128], AF.Identity, r=[B0, Bsm], w=[Bt0], scale=sm[:, 0:1])
                        for (u, sm, Bsm, t0, Bt0, o_, Bo, a0, a1, B0, B1) in U:
                            stt(o_[:], a1[:, u % 2, 0:128], sm[:, 1:2], t0[:], ALU.mult, ALU.add, r=[B1, Bsm, Bt0], w=[Bo])
                        for (u, sm, Bsm, t0, Bt0, o_, Bo, a0, a1, B0, B1) in U:
                            act(C["sqj4"][u][:], o_[:], AF.Square, r=[Bo], w=[C["Bsqj4"][u], Bsm], accum=sm[:, 2:3])
                        for (u, sm, Bsm, t0, Bt0, o_, Bo, a0, a1, B0, B1) in U:
                            act(sm[:, 3:4], sm[:, 2:3], AF.Ln, r=[Bsm], w=[Bsm], scale=1.0 / 128, bias=epsb[:, 0:1])
                        for (u, sm, Bsm, t0, Bt0, o_, Bo, a0, a1, B0, B1) in U:
                            act(sm[:, 3:4], sm[:, 3:4], AF.Exp, r=[Bsm], w=[Bsm], scale=-0.5)
                        for (u, sm, Bsm, t0, Bt0, o_, Bo, a0, a1, B0, B1) in U:
                            stt(oh_t[:, 4 * i + u, :], o_[:], sm[:, 3:4], gs[:], ALU.mult, ALU.mult,
                                r=[Bo, Bsm, Bgs], w=[Boh])

                    load_strips(C, 0)
                    for h in range(4):
                        att_pair(C, "A", QTA, KTA, h * 128, [(VA, h * 128, 128, 0)], (h, h), 0.125,
                                 lambda i: 0, evac_a, h * 128, 128, hidx=(h, h))
                    load_strips(C, 4)
                    for pr in range(2):
                        att_pair(C, "B", QTB, KTB, pr * 128,
                                 [(VB, pr * 128, 64, 0), (VB, pr * 128 + 64, 64, 65)], (2 * pr, 2 * pr + 1), 0.125,
                                 lambda i: max(0, 4 * i - 17), evac_bc, 512 + pr * 128, 128)
                    P.flush()

                with ExitStack() as ph:
                    iqT = sb("ix_iqT", [128, 4, S], BF16, ph)
                    ik = [sb("ix_ik%d" % m, [128, S], BF16, ph) for m in range(2)]
                    iw_sb = sb("ix_iw", [128, NB, 8], F32, ph)
                    score = Rot([sb("ix_sc%d" % i, [128, S], F32, ph) for i in range(6)])
                    junk = [sb("ix_junk%d" % i, [128, S], mybir.dt.int8, ph) for i in range(3)]
                    Bjunk = [Buf(), Buf(), Buf()]
                    rt = Rot([sb("ix_r%d" % i, [128, 2, 512], BF16, ph) for i in range(3)])
                    dgw = Rot([sb("ix_dg%d" % i, [128, 8, 128], BF16, ph) for i in range(2)])
                    mo = Rot([sb("ix_mo%d" % i, [128, S], BF16, ph) for i in range(2)])
                    smr = Rot([sb("ix_sm%d" % i, [128, 8 + 2 * NBIS], F32, ph) for i in range(6)])
                    lg = Rot([ph.enter_context(nc.psum_tensor(uq("ix_lg%d" % i), [128, 2, 512], F32)) for i in range(2)])
                    scp = Rot([ph.enter_context(nc.psum_tensor(uq("ix_scp%d" % i), [128, 512], F32)) for i in range(2)])
                    Bin = Buf()
                    dma("sp", iqT[:, :, :], AP(IQT, seq * S, [[NSEQ * S, 128], [128 * NSEQ * S, 4], [1, S]]),
                        r=scr_all("IQT"), w=[Bin])
                    mset("pool", ik[0][64:128, :], 0.0, w=[Bin])
                    mset("pool", ik[1][0:64, :], 0.0, w=[Bin])
                    dma("sp", ik[0][0:64, :], AP(IKT, seq * S, [[NSEQ * S, 64], [1, S]]), r=scr_all("IKT"), w=[Bin])
                    dma("sp", ik[1][64:128, :], AP(IKT, seq * S, [[NSEQ * S, 64], [1, S]]), r=scr_all("IKT"), w=[Bin])
                    dma("sp", iw_sb[:, :, :], AP(IW, seq * S * 8, [[8, 128], [128 * 8, NB], [1, 8]]),
                        r=scr_all("IW"), w=[Bin])
                    def scores(b, res):
                        dg, Bdg = dgw.next()
                        for hh in range(8):
                            ts("pool", dg[:, hh, :], ident_b[:], iw_sb[:, b, hh:hh + 1], 0.0, ALU.mult, ALU.add,
                               r=[Bin, B_const], w=[Bdg])
                        sc_t, Bsc = score.next()
                        res.append((sc_t, Bsc))
                        pend = None
                        cur = {}

                        def fin(p):
                            (c_, hp_, ncol_, r__, Br_) = p
                            if hp_ == 0:
                                cur["sp"] = scp.next()
                            sp_, Bsp = cur["sp"]
                            for k_ in range(2):
                                hh_ = 2 * hp_ + k_
                                mm(sp_[:, 0:ncol_], dg[:, hh_, :], r__[:, k_, 0:ncol_], hh_ == 0, hh_ == 7, r=[Bdg, Br_], w=[Bsp])
                            if hp_ == 3:
                                cp("act", sc_t[:, c_ * 512:c_ * 512 + ncol_], sp_[:, 0:ncol_], r=[Bsp], w=[Bsc])
                        for c in range(b // 4 + 1):
                            ncol = 512 if c < b // 4 else ((b % 4) + 1) * 128
                            for hp in range(4):
                                lg_, Blg = lg.next()
                                for k_ in range(2):
                                    mm(lg_[:, k_, 0:ncol], iqT[:, hp, b * 128:(b + 1) * 128],
                                       ik[k_][:, c * 512:c * 512 + ncol], True, True, r=[Bin], w=[Blg])
                                r_, Br = rt.next()
                                act(r_[:, :, 0:ncol], lg_[:, :, 0:ncol], AF.Relu, r=[Blg], w=[Br])
                                if pend is not None:
                                    fin(pend)
                                pend = (c, hp, ncol, r_, Br)
                                yield
                        fin(pend)

                    SN = 8 + NBIS

                    def bisect(b, sc_t, Bsc, jk, Bjk, on_act):
                        Sc = (b + 1) * 128
                        sm, Bsm = smr.next()
                        red(sm[:, 1:2], sc_t[:, 0:Sc], ALU.max, r=[Bsc], w=[Bsm]); yield
                        red(sm[:, 0:1], sc_t[:, 0:Sc], ALU.min, r=[Bsc], w=[Bsm]); yield
                        tt("dve", sm[:, 1:2], sm[:, 1:2], sm[:, 0:1], ALU.subtract, r=[Bsm], w=[Bsm]); yield
                        tt("dve", sc_t[:, b * 128:Sc], sc_t[:, b * 128:Sc], tri_f[:], ALU.add, r=[B_const], w=[Bsc]); yield
                        ts("dve", sm[:, 8:8 + NBIS], pw2[:, 0:NBIS], sm[:, 1:2], None, ALU.mult, r=[Bsm, B_const], w=[Bsm]); yield
                        if not on_act:
                            tt("dve", sm[:, 2:3], sm[:, 0:1], sm[:, 8:9], ALU.add, r=[Bsm], w=[Bsm]); yield
                            for it in range(NBIS):
                                ts("dve", jk[:, 0:Sc], sc_t[:, 0:Sc], sm[:, 2:3], None, ALU.is_ge, ALU.add,
                                   r=[Bsc, Bsm], w=[Bjk, Bsm], accum=sm[:, 3:4]); yield
                                ts("dve", sm[:, 4:5], sm[:, 3:4], float(TOPK) - 0.5, -0.5, ALU.is_ge, ALU.add, r=[Bsm], w=[Bsm]); yield
                                stt(sm[:, 2:3], sm[:, 4:5], sm[:, 8 + it:9 + it], sm[:, 2:3], ALU.mult, ALU.add, r=[Bsm], w=[Bsm]); yield
                            stt(sm[:, 0:1], sm[:, 8 + NBIS - 1:8 + NBIS], -1.0, sm[:, 2:3], ALU.mult, ALU.add, r=[Bsm], w=[Bsm]); yield
                        else:
                            ts("dve", sm[:, SN:SN + NBIS], sm[:, 8:8 + NBIS], -1.0, None, ALU.mult, r=[Bsm], w=[Bsm]); yield
                            stt(sm[:, 2:3], sm[:, 0:1], -1.0, sm[:, 8:9], ALU.mult, ALU.subtract, r=[Bsm], w=[Bsm]); yield
                            for it in range(NBIS):
                                act(jk[:, 0:Sc], sc_t[:, 0:Sc], AF.Sign, r=[Bsc, Bsm], w=[Bjk, Bsm], bias=sm[:, 2:3],
                                    scale=1.0, accum=sm[:, 3:4]); yield; yield
                                ts("pool", sm[:, 4:5], sm[:, 3:4], float(2 * TOPK - Sc) - 0.5, -0.5, ALU.is_ge, ALU.add,
                                   r=[Bsm], w=[Bsm]); yield
                                ts("pool", sm[:, 2:3], sm[:, 4:5], sm[:, SN + it:SN + it + 1], sm[:, 2:3], ALU.mult, ALU.add,
                                   r=[Bsm], w=[Bsm]); yield
                            stt(sm[:, 0:1], sm[:, 2:3], -1.0, sm[:, 8 + NBIS - 1:8 + NBIS], ALU.mult, ALU.subtract, r=[Bsm], w=[Bsm]); yield
                        mo_t, Bmo = mo.next()
                        ts("dve", mo_t[:, 0:Sc], sc_t[:, 0:Sc], sm[:, 0:1], NEGM, ALU.is_lt, ALU.mult, r=[Bsc, Bsm], w=[Bmo]); yield
                        dma("sp", MASK.ap()[b * 128:(b + 1) * 128, 0:Sc], mo_t[:, 0:Sc], r=[Bmo], w=[scr("MASK", b)])

                    def run_rr(gens):
                        alive = list(gens)
                        while alive:
                            for g_ in list(alive):
                                try:
                                    next(g_)
                                except StopIteration:
                                    alive.remove(g_)

                    def chain(gl):
                        for g_ in gl:
                            yield from g_

                    groups = [list(range(b0, min(b0 + 3, NB))) for b0 in range(0, NB, 3)]
                    prev = []
                    for grp in groups:
                        res = []
                        run_rr([chain([scores(b_, res) for b_ in grp])] + prev)
                        prev = []
                        for gi, b_ in enumerate(grp):
                            on_act = (gi == 2)
                            prev.append(bisect(b_, res[gi][0], res[gi][1], junk[gi], Bjunk[gi], on_act))
                    run_rr(prev)
                    P.flush()

                with ExitStack() as ph:
                    C = att_common(ph, "c_")
                    mkr = Rot([sb("c_mk%d" % i, [128, 4, S], BF16, ph) for i in range(2)])

                    def load_mask(i):
                        mk, Bmk = mkr.next()
                        for u in range(4):
                            b = 4 * i + u
                            dma("sp", mk[:, u, 0:(b + 1) * 128], MASK.ap()[b * 128:(b + 1) * 128, 0:(b + 1) * 128],
                                r=[scr("MASK", b)], w=[Bmk])
                        return mk, Bmk
                    load_strips(C, 8)
                    for pr in range(2):
                        att_pair(C, "C", QTC, KTC, pr * 128,
                                 [(VC, pr * 128, 64, 0), (VC, pr * 128 + 64, 64, 65)], (2 * pr, 2 * pr + 1), 0.125,
                                 lambda i: 0, evac_bc, 768 + pr * 128, 128, mask=load_mask, hidx=(8 + 2 * pr, 9 + 2 * pr))
                    P.flush()

            with ExitStack() as ph:
                KTm = sb("d1_KTm", [128, NSEQ, 4, 2, MEM], BF16, ph)
                Vm = sb("d1_Vm", [128, NSEQ, 2, 4, 256], BF16, ph)
                Bkv = Buf()
                NPS = 5
                psx = [ph.enter_context(nc.psum_tensor(uq("d1_ps%d" % i), [128, 512], F32)) for i in range(NPS)]
                Bpsx = [Buf() for _ in range(NPS)]
                ptrs = [(ph.enter_context(nc.psum_tensor(uq("d1_ptr%d" % i), [128, 1024], BF16)), Buf()) for i in range(2)]
                lps = ph.enter_context(nc.psum_tensor(uq("d1_lps"), [128, 16], F32))
                Blps = Buf()
                pi = 0
                with ExitStack() as ph2:
                    wkv = sb("mk_w", [128, KC, 2 * D], BF16, ph2)
                    stage = Rot([sb("mk_st%d" % i, [128, 2048], F32, ph2) for i in range(2)])
                    gkv = sb("mk_g", [128, KC], F32, ph2)
                    Bw, Bg = Buf(), Buf()
                    load_gain(gkv, norm_memkv, l * D, KC, Bg)
                    load_weight(wkv, w_mkv, l * D * 2 * D, KC, 2 * D, (gkv, Bg), stage, Bw)
                    for sq_i in range(NSEQ):
                        for h in range(4):
                            for dc in range(2):
                                n0 = h * 256 + dc * 128
                                b = pi % NPS; pi += 1
                                for kc in range(KC):
                                    mm(psx[b][:, 0:MEM], wkv[:, kc, n0:n0 + 128], memhatT[:, sq_i, kc, :],
                                       kc == 0, kc == KC - 1, r=[Bw, B_memhat], w=[Bpsx[b]])
                                cp("act", KTm[:, sq_i, h, dc, :], psx[b][:, 0:MEM], r=[Bpsx[b]], w=[Bkv])
                        for blk in range(2):
                            for nch in range(2):
                                b = pi % NPS; pi += 1
                                for kc in range(KC):
                                    mm(psx[b][:, :], memhatT[:, sq_i, kc, blk * 128:(blk + 1) * 128],
                                       wkv[:, kc, D + nch * 512:D + (nch + 1) * 512], kc == 0, kc == KC - 1,
                                       r=[Bw, B_memhat], w=[Bpsx[b]])
                                cp("dve", Vm[:, sq_i, blk, 2 * nch:2 * nch + 2, :],
                                   psx[b][:, :].rearrange("p (h d) -> p h d", h=2), r=[Bpsx[b]], w=[Bkv])
                    P.flush()
                wo = sb("d1_wo", [128, KC, D], BF16, ph)
                wq = sb("d1_wq", [128, KC, D], BF16, ph)
                wm = sb("d1_wm", [128, KC, D], BF16, ph)
                gq = sb("d1_gq", [128, KC], F32, ph)
                stage = Rot([sb("d1_st%d" % i, [128, 2048], F32, ph) for i in range(2)])
                xt = Rot([sb("d1_x%d" % i, [128, D], F32, ph) for i in range(6)])
                ob = Rot([sb("d1_ob%d" % i, [128, D], BF16, ph) for i in range(2)])
                hb = Rot([sb("d1_hb%d" % i, [128, D], BF16, ph) for i in range(4)])
                sqr = Rot([sb("d1_sq%d" % i, [128, D], BF16, ph) for i in range(2)])
                ssr = Rot([sb("d1_ss%d" % i, [128, 2], F32, ph) for i in range(8)])
                oT = sb("d1_oT", [128, KC, 512], BF16, ph); BoT = Buf()
                hT = sb("d1_hT", [128, KC, 512], BF16, ph); BhT = Buf()
                qmT = sb("d1_qmT", [128, 8, 512], BF16, ph); BqmT = Buf()
                pTm = Rot([sb("d1_pT%d" % i, [128, 2, 512], BF16, ph) for i in range(4)])
                omT = sb("d1_omT", [128, 4, 2, 512], BF16, ph); BomT = Buf()
                rl = sb("d1_rl", [128, 16], F32, ph); Brl = Buf()
                Bwo, Bwq, Bwm, Bgq = Buf(), Buf(), Buf(), Buf()
                load_gain(gq, norm_mem, l * D, KC, Bgq)
                load_weight(wo, w_out, l * D * D, KC, D, None, stage, Bwo)
                load_weight(wq, w_mq, l * D * D, KC, D, (gq, Bgq), stage, Bwq)
                load_weight(wm, w_mo, l * D * D, KC, D, None, stage, Bwm)
                for st_i in range(NTOK // 512):
                    seq = (st_i * 512) // S
                    xs = []
                    for u in range(4):
                        row0 = st_i * 512 + u * 128
                        x_t, Bxt = xt.next()
                        o_b, Bob = ob.next()
                        xs.append((x_t, Bxt))
                        dma("sp", x_t[:], xsrc[0].ap()[row0:row0 + 128, :], r=[xbuf(xsrc[1], row0 // 128)], w=[Bxt])
                        dma("sp", o_b[:], OS.ap()[row0:row0 + 128, :],
                            r=[scr("OS", seq, oc) for oc in (0, 128, 256, 384, 512, 640, 768, 896)], w=[Bob])
                        pt_, Bpt_ = ptrs[u % 2]
                        for kc in range(KC):
                            tr(pt_[:, kc * 128:(kc + 1) * 128], o_b[:, kc * 128:(kc + 1) * 128], ident_b[:],
                               r=[Bob, B_const], w=[Bpt_])
                        cp("act", oT[:, :, u * 128:(u + 1) * 128], pt_[:, :].rearrange("p (k t) -> p k t", k=KC),
                           r=[Bpt_], w=[BoT])
                    for u in range(4):
                        x_t, Bxt = xs[u]
                        for nch in range(2):
                            b = pi % NPS; pi += 1
                            for kc in range(KC):
                                mm(psx[b][:, :], oT[:, kc, u * 128:(u + 1) * 128], wo[:, kc, nch * 512:(nch + 1) * 512],
                                   kc == 0, kc == KC - 1, r=[BoT, Bwo], w=[Bpsx[b]])
                            tt("dve", x_t[:, nch * 512:(nch + 1) * 512], psx[b][:, :], x_t[:, nch * 512:(nch + 1) * 512],
                               ALU.add, r=[Bpsx[b]], w=[Bxt])
                    hbs = [hb.next() for _ in range(4)]
                    sss = [ssr.next() for _ in range(4)]
                    for u in range(4):
                        sq_t, Bsq_ = sqr.next()
                        act(sq_t[:], xs[u][0][:], AF.Square, r=[xs[u][1]], w=[sss[u][1], Bsq_], accum=sss[u][0][:, 0:1])
                    for u in range(4):
                        act(sss[u][0][:, 1:2], sss[u][0][:, 0:1], AF.Ln, r=[sss[u][1]], w=[sss[u][1]], scale=1.0 / D,
                            bias=epsb[:, 0:1])
                    for u in range(4):
                        act(sss[u][0][:, 1:2], sss[u][0][:, 1:2], AF.Exp, r=[sss[u][1]], w=[sss[u][1]], scale=-0.5)
                    for u in range(4):
                        ts("dve", hbs[u][0][:], xs[u][0][:], sss[u][0][:, 1:2], None, ALU.mult,
                           r=[xs[u][1], sss[u][1]], w=[hbs[u][1]])
                    for u in range(4):
                        h_b, Bhb = hbs[u]
                        pt_, Bpt_ = ptrs[u % 2]
                        for kc in range(KC):
                            tr(pt_[:, kc * 128:(kc + 1) * 128], h_b[:, kc * 128:(kc + 1) * 128], ident_b[:],
                               r=[Bhb, B_const], w=[Bpt_])
                        cp("act", hT[:, :, u * 128:(u + 1) * 128], pt_[:, :].rearrange("p (k t) -> p k t", k=KC),
                           r=[Bpt_], w=[BhT])
                    for nb in range(8):
                        b = pi % NPS; pi += 1
                        for kc in range(KC):
                            mm(psx[b][:, :], wq[:, kc, nb * 128:(nb + 1) * 128], hT[:, kc, :], kc == 0, kc == KC - 1,
                               r=[BhT, Bwq], w=[Bpsx[b]])
                        cp("act" if nb % 2 else "dve", qmT[:, nb, :], psx[b][:, :], r=[Bpsx[b]], w=[BqmT])
                    pts = [pTm.next() for _ in range(4)]
                    for h in range(4):
                        p_t, Bp = pts[h]
                        for blk in range(2):
                            b = pi % NPS; pi += 1
                            for dc in range(2):
                                mm(psx[b][:, :], KTm[:, seq, h, dc, blk * 128:(blk + 1) * 128], qmT[:, 2 * h + dc, :],
                                   dc == 0, dc == 1, r=[Bkv, BqmT], w=[Bpsx[b]])
                            act(p_t[:, blk, :], psx[b][:, :], AF.Exp, r=[Bpsx[b]], w=[Bp], scale=1.0 / 16.0)
                    for h in range(4):
                        p_t, Bp = pts[h]
                        for u in range(4):
                            for blk in range(2):
                                mm(lps[:, u * 4 + h:u * 4 + h + 1], p_t[:, blk, u * 128:(u + 1) * 128], ones_b[:, 0:1],
                                   blk == 0, blk == 1, r=[Bp, B_const], w=[Blps])
                        for dc in range(2):
                            b = pi % NPS; pi += 1
                            for blk in range(2):
                                mm(psx[b][:, :], Vm[:, seq, blk, h, dc * 128:(dc + 1) * 128], p_t[:, blk, :],
                                   blk == 0, blk == 1, r=[Bkv, Bp], w=[Bpsx[b]])
                            cp("act" if dc else "dve", omT[:, h, dc, :], psx[b][:, :], r=[Bpsx[b]], w=[BomT])
                    recip(rl[:, :], lps[:, :], r=[Blps], w=[Brl])
                    for u in range(4):
                        x_t, Bxt = xs[u]
                        row0 = st_i * 512 + u * 128
                        for h in range(4):
                            for nch in range(2):
                                b = pi % NPS; pi += 1
                                for dc in range(2):
                                    mm(psx[b][:, :], omT[:, h, dc, u * 128:(u + 1) * 128],
                                       wm[:, 2 * h + dc, nch * 512:(nch + 1) * 512], dc == 0, dc == 1,
                                       r=[BomT, Bwm], w=[Bpsx[b]])
                                stt(x_t[:, nch * 512:(nch + 1) * 512], psx[b][:, :], rl[:, u * 4 + h:u * 4 + h + 1],
                                    x_t[:, nch * 512:(nch + 1) * 512], ALU.mult, ALU.add, r=[Bpsx[b], Brl], w=[Bxt])
                        dma("pool", XA.ap()[row0:row0 + 128, :], x_t[:], r=[Bxt], w=[xbuf("xa", row0 // 128)])
                P.flush()

            with ExitStack() as ph:
                T2 = 256
                wu = sb("d2_wu", [128, KC, 2 * FF], BF16, ph)
                wd = sb("d2_wd", [128, NFB, D], BF16, ph)
                gf = sb("d2_gf", [128, KC], F32, ph)
                cw = sb("d2_cw", [128, 3, 44], F32, ph)
                cb = sb("d2_cb", [128, 44], F32, ph)
                Bwu, Bwd, Bgf, Bcw = Buf(), Buf(), Buf(), Buf()
                with ExitStack() as ph2:
                    stage = Rot([sb("d2_st%d" % i, [128, 2048], F32, ph2) for i in range(2)])
                    load_gain(gf, norm_ffn, l * D, KC, Bgf)
                    for j in range(3):
                        load_gain(cw[:, j, :], conv_w, (l * 3 + j) * 2 * FF, 44, Bcw)
                    load_gain(cb, conv_b, l * 2 * FF, 44, Bcw)
                    load_weight(wu, w_up, l * D * 2 * FF, KC, 2 * FF, (gf, Bgf), stage, Bwu)
                    load_weight(wd, w_down, l * FF * D, NFB, D, None, stage, Bwd)
                    P.flush()
                xt = Rot([sb("d2_x%d" % i, [128, D], F32, ph) for i in range(4)])
                hb = Rot([sb("d2_hb%d" % i, [128, D], BF16, ph) for i in range(2)])
                sqj = sb("d2_sq", [128, D], BF16, ph); Bsq = Buf()
                ssr = Rot([sb("d2_ss%d" % i, [128, 2], F32, ph) for i in range(3)])
                hT = Rot([sb("d2_hT%d" % i, [128, KC, T2], BF16, ph) for i in range(2)])
                ubuf = Rot([sb("d2_u%d" % i, [128, T2 + 2], F32, ph) for i in range(5)])
                ubuf.t = [(t_, (Buf(), Buf())) for (t_, _) in ubuf.t]
                cbuf = Rot([sb("d2_c%d" % i, [128, T2], F32, ph) for i in range(5)])
                gbuf = Rot([sb("d2_gt%d" % i, [128, T2], BF16, ph) for i in range(3)])
                aT = Rot([sb("d2_aT%d" % i, [128, NFB, T2], BF16, ph) for i in range(2)])
                carry = sb("d2_carry", [128, 44, 2], F32, ph); Bcarry = [Buf() for _ in range(44)]
                ptr = ph.enter_context(nc.psum_tensor(uq("d2_ptr"), [128, 1024], BF16)); Bptr = Buf()
                psx = [ph.enter_context(nc.psum_tensor(uq("d2_ps%d" % i), [128, 512], F32)) for i in range(6)]
                Bpsx = [Buf() for _ in range(6)]
                pi = 0
                last = (l == L - 1)
                pend_st = []
                for st_i in range(NTOK // T2):
                    seq = (st_i * T2) // S
                    if (st_i * T2) % S == 0:
                        mset("pool", carry[:, :, :], 0.0, w=Bcarry)
                    hTt, BhT = hT.next()
                    xs = []
                    for u in range(T2 // 128):
                        row0 = st_i * T2 + u * 128
                        x_t, Bxt = xt.next()
                        xs.append((x_t, Bxt))
                        h_b, Bhb = hb.next()
                        s_s, Bs = ssr.next()
                        dma("sp", x_t[:], XA.ap()[row0:row0 + 128, :], r=[xbuf("xa", row0 // 128)], w=[Bxt])
                        act(sqj[:], x_t[:], AF.Square, r=[Bxt], w=[Bs, Bsq], accum=s_s[:, 0:1])
                        rstd_from_ss(s_s[:, 1:2], s_s[:, 0:1], D, r=[Bs], w=[Bs])
                        ts("dve", h_b[:], x_t[:], s_s[:, 1:2], None, ALU.mult, r=[Bxt, Bs], w=[Bhb])
                        for kc in range(KC):
                            tr(ptr[:, kc * 128:(kc + 1) * 128], h_b[:, kc * 128:(kc + 1) * 128], ident_b[:],
                               r=[Bhb, B_const], w=[Bptr])
                        cp("act", hTt[:, :, u * 128:(u + 1) * 128], ptr[:, :].rearrange("p (k t) -> p k t", k=KC),
                           r=[Bptr], w=[BhT])
                    for (x_p, Bx_p, r_p) in pend_st:
                        dma("sp", XB.ap()[r_p:r_p + 128, :], x_p[:], r=[Bx_p], w=[xbuf("xb", r_p // 128)])
                    pend_st = []
                    a_t, Ba = aT.next()
                    pend_g = None

                    def gate_mul(a_t_, Ba_, fb_, cg, Bcg, cv_, Bcv):
                        g_t, Bgt = gbuf.next()
                        act(g_t[:], cg[:], AF.Silu, r=[Bcg], w=[Bgt])
                        tt("dve", a_t_[:, fb_, :], cv_[:], g_t[:], ALU.mult, r=[Bcv, Bgt], w=[Ba_])
                    for fb in range(NFB):
                        cs = []
                        for which, nb in enumerate((fb, fb + NFB)):
                            b = pi % 6; pi += 1
                            for kc in range(KC):
                                mm(psx[b][:, 0:T2], wu[:, kc, nb * 128:(nb + 1) * 128], hTt[:, kc, :], kc == 0, kc == KC - 1,
                                   r=[BhT, Bwu], w=[Bpsx[b]])
                            u_t, (Bu, Buh) = ubuf.next()
                            c_t, Bc = cbuf.next()
                            cp("pool", u_t[:, 0:2], carry[:, nb, :], r=[Bcarry[nb]], w=[Buh])
                            cp("act", u_t[:, 2:T2 + 2], psx[b][:, 0:T2], r=[Bpsx[b]], w=[Bu])
                            act(c_t[:], psx[b][:, 0:T2], AF.Identity, r=[Bpsx[b], Bcw], w=[Bc],
                                scale=cw[:, 2, nb:nb + 1], bias=cb[:, nb:nb + 1])
                            cp("pool", carry[:, nb, :], u_t[:, T2:T2 + 2], r=[Bu], w=[Bcarry[nb]])
                            cs.append((c_t, Bc, u_t, Bu, Buh, nb))
                        for tap in (1, 0):
                            for (c_t, Bc, u_t, Bu, Buh, nb) in cs:
                                stt(c_t[:], u_t[:, tap:T2 + tap], cw[:, tap, nb:nb + 1], c_t[:], ALU.mult, ALU.add,
                                    r=[Bu, Buh, Bcw], w=[Bc])
                        if pend_g is not None:
                            gate_mul(*pend_g)
                        pend_g = (a_t, Ba, fb, cs[0][0], cs[0][1], cs[1][0], cs[1][1])
                    gate_mul(*pend_g)
                    pend_g = None
                    for u in range(T2 // 128):
                        x_t, Bxt = xs[u]
                        row0 = st_i * T2 + u * 128
                        for nch in range(2):
                            b = pi % 6; pi += 1
                            for fb in range(NFB):
                                mm(psx[b][:, :], a_t[:, fb, u * 128:(u + 1) * 128], wd[:, fb, nch * 512:(nch + 1) * 512],
                                   fb == 0, fb == NFB - 1, r=[Ba, Bwd], w=[Bpsx[b]])
                            tt("dve", x_t[:, nch * 512:(nch + 1) * 512], psx[b][:, :], x_t[:, nch * 512:(nch + 1) * 512],
                               ALU.add, r=[Bpsx[b]], w=[Bxt])
                        pend_st.append((x_t, Bxt, row0))
                for (x_p, Bx_p, r_p) in pend_st:
                    dma("sp", XB.ap()[r_p:r_p + 128, :], x_p[:], r=[Bx_p], w=[xbuf("xb", r_p // 128)])
                P.flush()

        with ExitStack() as ph:
            gfin = sb("fn_g", [128, D], F32, ph); Bg = Buf()
            xt = Rot([sb("fn_x%d" % i, [128, D], F32, ph) for i in range(3)])
            yt = Rot([sb("fn_y%d" % i, [128, D], F32, ph) for i in range(3)])
            sqj = sb("fn_sq", [128, D], BF16, ph); Bsq = Buf()
            ssr = Rot([sb("fn_ss%d" % i, [128, 2], F32, ph) for i in range(3)])
            dma("sp", gfin[:], AP(norm_final, 0, [[0, 128], [1, D]]), w=[Bg])
            for t in range(NTOK // 128):
                x_t, Bxt = xt.next()
                y_t, Byt = yt.next()
                s_s, Bs = ssr.next()
                dma("sp", x_t[:], XB.ap()[t * 128:(t + 1) * 128, :], r=[xbuf("xb", t)], w=[Bxt])
                act(sqj[:], x_t[:], AF.Square, r=[Bxt], w=[Bs, Bsq], accum=s_s[:, 0:1])
                rstd_from_ss(s_s[:, 1:2], s_s[:, 0:1], D, r=[Bs], w=[Bs])
                stt(y_t[:], x_t[:], s_s[:, 1:2], gfin[:], ALU.mult, ALU.mult, r=[Bxt, Bs, Bg], w=[Byt])
                dma("pool", y_out.ap()[t * 128:(t + 1) * 128, :], y_t[:], r=[Byt], w=[Buf()])
            P.flush()
        print("ops", P.nops, "waits", P.nwaits, flush=True)
    return nc


S_FULL = 4096
NSEQ_FULL = 2
L_FULL = 4
NCORES = 8
_CACHE = {}


def kernel(**inputs):
    f32 = lambda a: np.ascontiguousarray(np.asarray(a, dtype=np.float32))
    x = f32(inputs["x"])
    mem = f32(inputs["mem"])
    B = x.shape[0]
    per = B // NCORES
    consts = host_consts()
    shared = {k: f32(inputs[k]) for k in ("rel_bias", "norm_mix", "w_in", "lam_q1", "lam_k1", "lam_q2", "lam_k2",
                                          "subln", "w_out", "norm_mem", "norm_memkv", "w_mq", "w_mkv", "w_mo",
                                          "norm_ffn", "w_up", "conv_w", "conv_b", "w_down")}
    shared["norm_final"] = f32(inputs["norm_final"]).reshape(1, D)
    shared.update(consts)
    key = "full"
    if key not in _CACHE:
        _CACHE[key] = build(S_FULL, per, L_FULL, 256)
    nc = _CACHE[key]
    in_maps = []
    for c in range(NCORES):
        m = dict(shared)
        m["x"] = np.ascontiguousarray(x[c * per:(c + 1) * per].reshape(per * S_FULL, D))
        m["mem"] = np.ascontiguousarray(mem[c * per:(c + 1) * per].reshape(per * MEM, D))
        in_maps.append(m)
    res = run_bass_kernel_spmd(nc, in_maps, core_ids=list(range(NCORES)))
    out = np.concatenate([np.asarray(r["y"]).reshape(per, S_FULL, D) for r in res.results], axis=0)
    return out.astype(np.float32)
```

```python
import math
from contextlib import ExitStack
import numpy as np
import concourse.bass as bass
import concourse.mybir as mybir
from concourse.bass_utils import run_bass_kernel_spmd

F32 = mybir.dt.float32
BF16 = mybir.dt.bfloat16
AF = mybir.ActivationFunctionType
ALU = mybir.AluOpType
AX = mybir.AxisListType

D = 1024
KC = 8
NIN = 3656
FF = 2816
NFB = 22
MEM = 256
NG = 3200
MS = 3072
DOFF = 511
DFAR = 2176
EPS = 1e-6
NEGM = -30000.0
NBIS = 14


class Buf:
    __slots__ = ("name", "w", "r")

    def __init__(self, name=""):
        self.name = name
        self.w = {}
        self.r = {}


class Stream:
    def __init__(self, name):
        self.name = name
        self.ops = []
        self.cnt = 0
        self.dcnt = 0
        self.slot = [0] * NSLOT
        self.known = {}


NSLOT = 12


class Prog:
    STREAMS = ("pe", "act", "dve", "pool", "sp")

    def __init__(self, nc, es):
        self.nc = nc
        self.streams = {n: Stream(n) for n in self.STREAMS}
        self.sems = {}
        for n in self.STREAMS:
            self.sems[n] = es.enter_context(nc.semaphore("s_" + n))
        for n in ("pool", "sp"):
            for k in range(NSLOT):
                self.sems["%s.d%d" % (n, k)] = es.enter_context(nc.semaphore("d_%s%d" % (n, k)))
        self.nops = 0
        self.nwaits = 0

    def _deps(self, st, reads, writes, dma=False, extra=None):
        deps = {}
        own_d = st.name + ".d"

        def add(k, v):
            if k == st.name and k == "pe":
                return
            if deps.get(k, 0) < v:
                deps[k] = v
        for b in reads:
            for k, v in b.w.items():
                add(k, v)
        for b in writes:
            for k, v in b.w.items():
                if dma and k.startswith(own_d):
                    continue
                add(k, v)
            for k, v in b.r.items():
                add(k, v)
        if extra is not None:
            add(*extra)
        out = {}
        for k, v in deps.items():
            if st.known.get(k, 0) < v:
                st.known[k] = v
                out[k] = v
        return out

    def op(self, stream, fn, reads=(), writes=(), dma=False):
        st = self.streams[stream]
        if dma:
            slot = st.dcnt % NSLOT
            st.dcnt += 1
            kname = "%s.d%d" % (stream, slot)
            extra = (kname, st.slot[slot]) if st.slot[slot] else None
            waits = self._deps(st, reads, writes, True, extra)
            st.slot[slot] += 1
            key = (kname, st.slot[slot])
        else:
            waits = self._deps(st, reads, writes, False)
            st.cnt += 1
            key = (stream, st.cnt)
        st.ops.append((waits, fn, key[0] if dma else None))
        for b in reads:
            if b.r.get(key[0], 0) < key[1]:
                b.r[key[0]] = key[1]
        for b in writes:
            if dma:
                b.w = {k: v for k, v in b.w.items() if k.startswith(stream + ".d")}
                b.w[key[0]] = key[1]
            else:
                b.w = {key[0]: key[1]}
            b.r = {}
        self.nops += 1
        self.nwaits += len(waits)
        return key

    def flush(self):
        nc = self.nc
        sems = self.sems
        fin = {}
        for n, st in self.streams.items():
            if st.cnt:
                fin[n] = st.cnt
            for k in range(NSLOT):
                if st.slot[k]:
                    fin["%s.d%d" % (n, k)] = st.slot[k]

        def val(k, v):
            return v * 16 if ".d" in k else v

        def run(stname):
            st = self.streams[stname]
            ops = st.ops
            st.ops = []

            def body(e):
                for waits, fn, dkey in ops:
                    for k, v in waits.items():
                        e.wait_ge(sems[k], val(k, v))
                    ins = fn(e)
                    if dkey is not None:
                        ins.then_inc(sems[dkey], 16)
                    else:
                        ins.then_inc(sems[stname], 1)
                for k, v in fin.items():
                    if k == stname:
                        continue
                    if st.known.get(k, 0) < v:
                        e.wait_ge(sems[k], val(k, v))
                        st.known[k] = v
            return body

        with nc.Block() as block:
            block.tensor(run("pe"))
            block.scalar(run("act"))
            block.vector(run("dve"))
            block.gpsimd(run("pool"))
            block.sync(run("sp"))


class Rot:
    def __init__(self, tiles):
        self.t = [(t, Buf()) for t in tiles]
        self.i = 0

    def next(self):
        r = self.t[self.i % len(self.t)]
        self.i += 1
        return r


def rel_bucket_np(n):
    n = np.maximum(n, 0)
    nf = np.maximum(n, 1).astype(np.float32)
    large = 16 + (np.log(nf / np.float32(16)) / np.float32(math.log(2048 / 16)) * np.float32(16)).astype(np.int32)
    large = np.minimum(large, 31)
    return np.where(n < 16, n, large)


def host_consts():
    d = np.arange(NG, dtype=np.int64) - DOFF
    oh = np.zeros((34, NG), np.float32)
    bk = rel_bucket_np(d.astype(np.int32))
    valid = d >= 0
    oh[bk[valid], np.nonzero(valid)[0]] = 1.0
    oh[32] = np.where(valid, 0.0, NEGM)
    mult = ((d <= 128).astype(np.int64) + ((d % 4 == 0) & (d <= 512)) + ((d % 16 == 0) & (d <= 2048)))
    mult = np.where(valid, mult, 0)
    oh[33] = np.where(mult > 0, 8.0 * np.log(np.maximum(mult, 1)).astype(np.float32), NEGM)
    sel = np.zeros((2, 12), np.float32)
    sel[0, 0:4] = 1.0
    sel[0, 8:12] = 1.0
    sel[1, 4:8] = 1.0
    ident = np.eye(128, dtype=np.float32)
    J = np.ascontiguousarray(ident[::-1])
    tri = np.where(np.arange(128)[None, :] <= np.arange(128)[:, None], 0.0, -1e30).astype(np.float32)
    return {"c_onehot": oh, "c_sel": sel, "c_ident": ident, "c_J": J, "c_tri": tri}


def build(S, NSEQ, L, TOPK, debug_outs=False):
    nc = bass.Bass("TRN2", target_bir_lowering=False)
    NB = S // 128
    NT = S // 512
    NTOK = NSEQ * S

    NAMES = {}

    def din(name, shape, dt=F32):
        t = nc.dram_tensor(name, list(shape), dt, kind="ExternalInput")
        NAMES[id(t)] = name
        return t

    def dscr(name, shape, dt=BF16):
        t = nc.dram_tensor(name, list(shape), dt, kind="ExternalOutput" if debug_outs else "Internal")
        NAMES[id(t)] = name
        return t

    def nm(t):
        return NAMES[id(t)]

    x_in = din("x", [NTOK, D])
    mem_in = din("mem", [NSEQ * MEM, D])
    rel_bias = din("rel_bias", [32, 12])
    norm_mix = din("norm_mix", [L, D])
    w_in = din("w_in", [L, D, NIN])
    lam_q1 = din("lam_q1", [L, 64]); lam_k1 = din("lam_k1", [L, 64])
    lam_q2 = din("lam_q2", [L, 64]); lam_k2 = din("lam_k2", [L, 64])
    subln = din("subln", [L, 128])
    w_out = din("w_out", [L, D, D])
    norm_mem = din("norm_mem", [L, D]); norm_memkv = din("norm_memkv", [L, D])
    w_mq = din("w_mq", [L, D, D]); w_mkv = din("w_mkv", [L, D, 2 * D]); w_mo = din("w_mo", [L, D, D])
    norm_ffn = din("norm_ffn", [L, D])
    w_up = din("w_up", [L, D, 2 * FF]); conv_w = din("conv_w", [L, 3, 2 * FF]); conv_b = din("conv_b", [L, 2 * FF])
    w_down = din("w_down", [L, FF, D])
    norm_final = din("norm_final", [1, D])
    c_onehot = din("c_onehot", [34, NG]); c_sel = din("c_sel", [2, 12])
    c_ident = din("c_ident", [128, 128]); c_J = din("c_J", [128, 128]); c_tri = din("c_tri", [128, 128])
    y_out = nc.dram_tensor("y", [NTOK, D], F32, kind="ExternalOutput")

    QTA = dscr("QTA", [512, NSEQ, S]); KTA = dscr("KTA", [512, NSEQ, S])
    QTB = dscr("QTB", [256, NSEQ, S]); KTB = dscr("KTB", [256, NSEQ, S])
    QTC = dscr("QTC", [256, NSEQ, S]); KTC = dscr("KTC", [256, NSEQ, S])
    IQT = dscr("IQT", [512, NSEQ, S]); IKT = dscr("IKT", [64, NSEQ, S])
    VA = dscr("VA", [NTOK, 512]); VB = dscr("VB", [NTOK, 256]); VC = dscr("VC", [NTOK, 256])
    IW = dscr("IW", [NTOK, 8], F32)
    OS = dscr("OS", [NTOK, D])
    MASK = dscr("MASK", [S, S])
    XA = dscr("XA", [NTOK, D], F32); XB = dscr("XB", [NTOK, D], F32)
    GSCR = dscr("GSCR", [12, NG])
    ESCR = dscr("ESCR", [12, NG])

    def AP(t, offset, ap):
        return bass.AP(tensor=t, offset=offset, ap=[list(a) for a in ap])

    es = ExitStack()
    with es:
        P = Prog(nc, es)

        _uid = [0]

        def uq(name):
            _uid[0] += 1
            return "%s_%d" % (name, _uid[0])

        def sb(name, shape, dt=BF16, stack=es):
            return stack.enter_context(nc.sbuf_tensor(uq(name), list(shape), dt))

        def mm(out, lhsT, rhs, start, stop, r=(), w=()):
            P.op("pe", lambda e: e.matmul(out, lhsT=lhsT, rhs=rhs, start=start, stop=stop, skip_group_check=True), r, w)

        def tr(out, in_, ident, r=(), w=()):
            P.op("pe", lambda e: e.transpose(out, in_, ident), r, w)

        def act(out, in_, func, r=(), w=(), bias=None, scale=None, accum=None):
            kw = {}
            if bias is not None:
                kw["bias"] = bias
            if scale is not None:
                kw["scale"] = scale
            if accum is not None:
                kw["accum_out"] = accum
            P.op("act", lambda e: e.activation(out=out, in_=in_, func=func, **kw), r, w)

        def ts(eng, out, in0, s1, s2, op0, op1=None, r=(), w=(), accum=None):
            kw = {}
            if op1 is not None:
                kw["op1"] = op1
            if accum is not None:
                kw["accum_out"] = accum
            P.op(eng, lambda e: e.tensor_scalar(out=out, in0=in0, scalar1=s1, scalar2=s2, op0=op0, **kw), r, w)

        def tt(eng, out, in0, in1, op, r=(), w=()):
            P.op(eng, lambda e: e.tensor_tensor(out=out, in0=in0, in1=in1, op=op), r, w)

        def stt(out, in0, scalar, in1, op0, op1, r=(), w=()):
            P.op("dve", lambda e: e.scalar_tensor_tensor(out=out, in0=in0, scalar=scalar, in1=in1, op0=op0, op1=op1), r, w)

        def cp(eng, out, in_, r=(), w=()):
            if eng == "act":
                P.op("act", lambda e: e.activation(out=out, in_=in_, func=AF.Copy), r, w)
            else:
                P.op(eng, lambda e: e.tensor_copy(out=out, in_=in_), r, w)

        def red(out, in_, op, r=(), w=()):
            P.op("dve", lambda e: e.tensor_reduce(out=out, in_=in_, axis=AX.X, op=op), r, w)

        def recip(out, in_, r=(), w=()):
            P.op("dve", lambda e: e.reciprocal(out=out, in_=in_), r, w)

        def mset(eng, ap, val, r=(), w=()):
            P.op(eng, lambda e: e.memset(ap, val), r, w)

        def dma(q, out, in_, r=(), w=(), slow=False):
            if slow:
                P.op(q, lambda e: e.dma_start(out=out, in_=in_, allow_slow_non_contiguous=True), r, w, dma=True)
            else:
                P.op(q, lambda e: e.dma_start(out=out, in_=in_), r, w, dma=True)

        def rstd_from_ss(rs, ss, n, r, w):
            act(rs, ss, AF.Ln, r=r, w=w, scale=1.0 / n, bias=epsb[:, 0:1])
            act(rs, rs, AF.Exp, r=w, w=w, scale=-0.5)

        ident_f = sb("ident_f", [128, 128], F32)
        ident_b = sb("ident_b", [128, 128])
        J_b = sb("J_b", [128, 128])
        tri_f = sb("tri_f", [128, 128], F32)
        ones_b = sb("ones_b", [128, 128])
        epsb = sb("epsb", [128, 1], F32)
        pw2 = sb("pw2", [128, NBIS], F32)
        farb = sb("farb", [128, 12], F32)
        memhatT = sb("memhatT", [128, NSEQ, KC, MEM])
        B_const = Buf("const")
        B_memhat = Buf("memhat")


        with ExitStack() as ph:
            tmpf = sb("s0_tmpf", [128, 128], F32, ph)
            tbl = sb("s0_tbl", [34, 12], F32, ph)
            oh = sb("s0_oh", [34, NG], F32, ph)
            gv = sb("s0_gv", [12, NG], BF16, ph)
            ev = sb("s0_ev", [12, NG], BF16, ph)
            mt = sb("s0_mt", [128, D], F32, ph)
            mb = sb("s0_mb", [128, D], BF16, ph)
            sq = sb("s0_sq", [128, D], BF16, ph)
            ss = sb("s0_ss", [128, 1], F32, ph)
            rs = sb("s0_rs", [128, 1], F32, ph)
            ptr = ph.enter_context(nc.psum_tensor(uq("s0_ptr"), [128, 1024], BF16))
            ps0 = ph.enter_context(nc.psum_tensor(uq("s0_ps"), [128, 512], F32))
            Bps0 = Buf()
            Bt, Btbl, Boh, Bgv, Bmt, Bmb, Bss, Bptr = [Buf() for _ in range(8)]
            dma("sp", ident_f[:], c_ident.ap(), w=[B_const])
            cp("dve", ident_b[:], ident_f[:], r=[B_const], w=[B_const])
            dma("sp", tmpf[:], c_J.ap(), w=[Bt])
            cp("dve", J_b[:], tmpf[:], r=[Bt], w=[B_const])
            dma("sp", tri_f[:], c_tri.ap(), w=[B_const])
            dma("sp", farb[:], AP(rel_bias, 31 * 12, [[0, 128], [1, 12]]), w=[B_const])
            mset("dve", ones_b[:], 1.0, w=[B_const])
            mset("dve", epsb[:], EPS, w=[B_const])
            for k_ in range(NBIS):
                mset("dve", pw2[:, k_:k_ + 1], float(2.0 ** -(k_ + 1)), w=[B_const])
            dma("sp", tbl[0:32, :], rel_bias.ap(), w=[Btbl])
            dma("sp", tbl[32:34, :], c_sel.ap(), w=[Btbl])
            ts("dve", tbl[0:32, :], tbl[0:32, :], 8.0, None, ALU.mult, r=[Btbl], w=[Btbl])
            dma("sp", oh[:], c_onehot.ap(), w=[Boh])
            for c in range((NG + 511) // 512):
                n = min(512, NG - c * 512)
                mm(ps0[0:12, 0:n], tbl[:, :], oh[:, c * 512:c * 512 + n], True, True, r=[Btbl, Boh], w=[Bps0])
                cp("dve", gv[:, c * 512:c * 512 + n], ps0[0:12, 0:n], r=[Bps0], w=[Bgv])
                act(ev[:, c * 512:c * 512 + n], ps0[0:12, 0:n], AF.Exp, r=[Bps0], w=[Bgv], scale=0.125)
            B_gscr = Buf("gscr")
            dma("sp", GSCR.ap(), gv[:], r=[Bgv], w=[B_gscr])
            dma("sp", ESCR.ap(), ev[:], r=[Bgv], w=[B_gscr])
            for sq_i in range(NSEQ):
                for blk in range(MEM // 128):
                    dma("sp", mt[:], mem_in.ap()[sq_i * MEM + blk * 128: sq_i * MEM + (blk + 1) * 128, :], w=[Bmt])
                    act(sq[:], mt[:], AF.Square, r=[Bmt], w=[Bss], accum=ss[:, 0:1])
                    rstd_from_ss(rs[:, 0:1], ss[:, 0:1], D, r=[Bss], w=[Bss])
                    ts("dve", mb[:], mt[:], rs[:, 0:1], None, ALU.mult, r=[Bmt, Bss], w=[Bmb])
                    for kc in range(KC):
                        tr(ptr[:, kc * 128:(kc + 1) * 128], mb[:, kc * 128:(kc + 1) * 128], ident_b[:],
                           r=[Bmb, B_const], w=[Bptr])
                    cp("dve", memhatT[:, sq_i, :, blk * 128:(blk + 1) * 128],
                       ptr[:, :].rearrange("p (k t) -> p k t", k=KC), r=[Bptr], w=[B_memhat])
            P.flush()

        wl_cnt = [0]

        def load_weight(dst, src_t, src_off, nk, N, gain, stage, Bdst, row_stride=None):
            rs_ = N if row_stride is None else row_stride
            CH = 2048
            for kc in range(nk):
                for c0 in range(0, N, CH):
                    n = min(CH, N - c0)
                    st, Bst = stage.next()
                    dma("sp", st[:, 0:n], AP(src_t, src_off + kc * 128 * rs_ + c0, [[rs_, 128], [1, n]]), w=[Bst])
                    wl_cnt[0] += 1
                    on_act = (wl_cnt[0] % 2 == 0)
                    if gain is None:
                        cp("act" if on_act else "dve", dst[:, kc, c0:c0 + n], st[:, 0:n], r=[Bst], w=[Bdst])
                    elif on_act:
                        g, Bg = gain
                        act(dst[:, kc, c0:c0 + n], st[:, 0:n], AF.Identity, r=[Bst, Bg], w=[Bdst], scale=g[:, kc:kc + 1])
                    else:
                        g, Bg = gain
                        ts("dve", dst[:, kc, c0:c0 + n], st[:, 0:n], g[:, kc:kc + 1], None, ALU.mult,
                           r=[Bst, Bg], w=[Bdst])

        def load_gain(dst, src_t, off, n, Bd):
            dma("sp", dst[:, 0:n], AP(src_t, off, [[1, 128], [128, n]]), w=[Bd], slow=True)

        B_x = {}

        def xbuf(which, t):
            return B_x.setdefault((which, t), Buf())
        B_scr = {}

        def scr(name, *idx):
            return B_scr.setdefault((name,) + idx, Buf())

        for l in range(L):
            lam_init = 0.8 - 0.6 * math.exp(-0.3 * l)
            xsrc = (x_in, "x") if l == 0 else (XB, "xb")

            with ExitStack() as ph:
                wsb = sb("p1_w", [128, KC, NIN], BF16, ph)
                stage = Rot([sb("p1_st%d" % i, [128, 2048], F32, ph) for i in range(3)])
                gmix = sb("p1_g", [128, KC], F32, ph)
                xt = Rot([sb("p1_x%d" % i, [128, D], F32, ph) for i in range(3)])
                hb = Rot([sb("p1_hb%d" % i, [128, D], BF16, ph) for i in range(2)])
                sqj = sb("p1_sq", [128, D], BF16, ph)
                ssr = Rot([sb("p1_ss%d" % i, [128, 2], F32, ph) for i in range(3)])
                hT = Rot([sb("p1_hT%d" % i, [128, KC, 512], BF16, ph) for i in range(2)])
                stF = Rot([sb("p1_sf%d" % i, [128, 4, 512], BF16, ph) for i in range(3)])
                stT = Rot([sb("p1_stt%d" % i, [128, 512], BF16, ph) for i in range(3)])
                stW = Rot([sb("p1_sw%d" % i, [128, 8], F32, ph) for i in range(2)])
                ptr = ph.enter_context(nc.psum_tensor(uq("p1_ptr"), [128, 1024], BF16))
                psx = [ph.enter_context(nc.psum_tensor(uq("p1_ps%d" % i), [128, 512], F32)) for i in range(5)]
                Bpsx = [Buf() for _ in range(5)]
                Bptr, Bw, Bg, Bsq = Buf(), Buf(), Buf(), Buf()
                load_gain(gmix, norm_mix, l * D, KC, Bg)
                load_weight(wsb, w_in, l * D * NIN, KC, NIN, (gmix, Bg), stage, Bw)
                FG = [(0, 4, 128, QTA), (512, 4, 128, KTA), (1536, 2, 128, QTB), (1792, 2, 128, KTB),
                      (2304, 2, 128, QTC), (2560, 2, 128, KTC), (3072, 4, 128, IQT), (3584, 1, 64, IKT)]
                TG = [(1024, 512, VA), (2048, 256, VB), (2816, 256, VC)]
                pi = 0
                for st_i in range(NTOK // 512):
                    seq = (st_i * 512) // S
                    t0 = st_i * 512 - seq * S
                    hTt, BhT = hT.next()
                    for u in range(4):
                        row0 = st_i * 512 + u * 128
                        x_t, Bxt = xt.next()
                        h_b, Bhb = hb.next()
                        s_s, Bs = ssr.next()
                        dma("sp", x_t[:], xsrc[0].ap()[row0:row0 + 128, :], r=[xbuf(xsrc[1], row0 // 128)], w=[Bxt])
                        act(sqj[:], x_t[:], AF.Square, r=[Bxt], w=[Bs, Bsq], accum=s_s[:, 0:1])
                        rstd_from_ss(s_s[:, 1:2], s_s[:, 0:1], D, r=[Bs], w=[Bs])
                        ts("dve", h_b[:], x_t[:], s_s[:, 1:2], None, ALU.mult, r=[Bxt, Bs], w=[Bhb])
                        for kc in range(KC):
                            tr(ptr[:, kc * 128:(kc + 1) * 128], h_b[:, kc * 128:(kc + 1) * 128], ident_b[:],
                               r=[Bhb, B_const], w=[Bptr])
                        cp("act", hTt[:, :, u * 128:(u + 1) * 128], ptr[:, :].rearrange("p (k t) -> p k t", k=KC),
                           r=[Bptr], w=[BhT])
                        for (c0, ncol, dt_) in TG:
                            b = pi % 5; pi += 1
                            for kc in range(KC):
                                mm(psx[b][:, 0:ncol], hTt[:, kc, u * 128:(u + 1) * 128], wsb[:, kc, c0:c0 + ncol],
                                   kc == 0, kc == KC - 1, r=[BhT, Bw], w=[Bpsx[b]])
                            s_t, Bst_ = stT.next()
                            cp("dve", s_t[:, 0:ncol], psx[b][:, 0:ncol], r=[Bpsx[b]], w=[Bst_])
                            dma("pool", dt_.ap()[row0:row0 + 128, :], s_t[:, 0:ncol], r=[Bst_],
                                w=[scr(nm(dt_), seq, st_i)])
                        b = pi % 5; pi += 1
                        for kc in range(KC):
                            mm(psx[b][:, 0:8], hTt[:, kc, u * 128:(u + 1) * 128], wsb[:, kc, 3648:3656],
                               kc == 0, kc == KC - 1, r=[BhT, Bw], w=[Bpsx[b]])
                        s_w, Bsw = stW.next()
                        cp("dve", s_w[:, :], psx[b][:, 0:8], r=[Bpsx[b]], w=[Bsw])
                        dma("pool", IW.ap()[row0:row0 + 128, :], s_w[:, :], r=[Bsw], w=[scr("IW", seq, st_i)])
                    for gi, (c0, nblk, bs, dt_) in enumerate(FG):
                        s_f, Bsf = stF.next()
                        for bi in range(nblk):
                            b = pi % 5; pi += 1
                            cc = c0 + bi * bs
                            for kc in range(KC):
                                mm(psx[b][0:bs, :], wsb[:, kc, cc:cc + bs], hTt[:, kc, :], kc == 0, kc == KC - 1,
                                   r=[BhT, Bw], w=[Bpsx[b]])
                            cp("act" if (bi % 2 == 0) else "dve", s_f[0:bs, bi, :], psx[b][0:bs, :], r=[Bpsx[b]], w=[Bsf])
                        dst = AP(dt_, seq * S + t0, [[NSEQ * S, bs], [bs * NSEQ * S, nblk], [1, 512]])
                        dma("pool", dst, s_f[0:bs, 0:nblk, :], r=[Bsf], w=[scr(nm(dt_), seq, st_i)])
                P.flush()

            for seq in range(NSEQ):
                def scr_all(name):
                    return [scr(name, seq, st_i) for st_i in range(seq * NT, (seq + 1) * NT)]

                def att_common(ph, tag):
                    C = {}
                    C["strips"] = sb(tag + "strips", [128, 4, MS], BF16, ph)
                    C["Bstrips"] = Buf()
                    C["kT"] = [[sb(tag + "kT%d%d" % (s_, m), [128, S], BF16, ph) for m in range(2)] for s_ in range(2)]
                    C["qT"] = [sb(tag + "qT%d" % s_, [128, S], BF16, ph) for s_ in range(2)]
                    C["V"] = [sb(tag + "V%d" % s_, [128, NB, 130], BF16, ph) for s_ in range(2)]
                    C["oh"] = [sb(tag + "oh%d" % s_, [128, NB, 128], BF16, ph) for s_ in range(2)]
                    C["Bin"] = [Buf(), Buf()]
                    C["Boh"] = [Buf(), Buf()]
                    C["pT"] = Rot([sb(tag + "pT%d" % i, [128, 2, 512], BF16, ph) for i in range(4)])
                    C["pE"] = Rot([sb(tag + "pE%d" % i, [128, 2, 512], BF16, ph) for i in range(3)])
                    C["stmp"] = Rot([sb(tag + "stmp%d" % i, [128, MS], BF16, ph) for i in range(1)])
                    C["sm"] = Rot([sb(tag + "sm%d" % i, [128, 8], F32, ph) for i in range(16)])
                    C["t0"] = Rot([sb(tag + "t0%d" % i, [128, 128], F32, ph) for i in range(4)])
                    C["o"] = Rot([sb(tag + "o%d" % i, [128, 128], F32, ph) for i in range(4)])
                    C["sqj4"] = [sb(tag + "sqj4%d" % i, [128, 128], BF16, ph) for i in range(4)]
                    C["Bsqj4"] = [Buf() for _ in range(4)]
                    C["sqj"] = sb(tag + "sqj", [128, 128], BF16, ph)
                    C["Bsqj"] = Buf()
                    C["accb"] = [ph.enter_context(nc.psum_tensor(uq(tag + "acc%d" % k), [128, 512], F32)) for k in range(4)]
                    C["Baccb"] = [Buf() for _ in range(4)]
                    C["sc2"] = Rot([ph.enter_context(nc.psum_tensor(uq(tag + "sc%d" % i), [128, 2, 512], F32)) for i in range(2)])
                    C["set"] = 0
                    for s_ in range(2):
                        mset("pool", C["kT"][s_][0][64:128, :], 0.0, w=[C["Bin"][s_]])
                        mset("pool", C["kT"][s_][1][0:64, :], 0.0, w=[C["Bin"][s_]])
                    return C

                def load_strips(C, h0):
                    for hh in range(4):
                        tmp, Btmp = C["stmp"].next()
                        dma("sp", tmp[:, :], AP(ESCR, (h0 + hh) * NG, [[1, 128], [1, MS]]), r=[B_gscr], w=[Btmp])
                        for c in range(MS // 512):
                            ps, Bps = C["sc2"].next()
                            mm(ps[:, 0, :], J_b[:], tmp[:, c * 512:(c + 1) * 512], True, True, r=[Btmp, B_const], w=[Bps])
                            cp("dve" if c % 2 else "act", C["strips"][:, hh, c * 512:(c + 1) * 512], ps[:, 0, :],
                               r=[Bps], w=[C["Bstrips"]])

                def att_pair(C, kind, qt, kt, row0, vloads, sidx, scale, jmin_fn, evac, ocol, dst_cols,
                             mask=None, hidx=None):
                    s_ = C["set"]; C["set"] ^= 1
                    Bin = C["Bin"][s_]
                    kT0, kT1 = C["kT"][s_]
                    qT = C["qT"][s_]
                    V = C["V"][s_]
                    oh_t = C["oh"][s_]; Boh = C["Boh"][s_]
                    rq = scr_all(nm(qt)); rk = scr_all(nm(kt))
                    dma("sp", qT[:, :], AP(qt, row0 * NSEQ * S + seq * S, [[NSEQ * S, 128], [1, S]]), r=rq, w=[Bin])
                    dma("sp", kT0[0:64, :], AP(kt, row0 * NSEQ * S + seq * S, [[NSEQ * S, 64], [1, S]]), r=rk, w=[Bin])
                    dma("sp", kT1[64:128, :], AP(kt, (row0 + 64) * NSEQ * S + seq * S, [[NSEQ * S, 64], [1, S]]), r=rk, w=[Bin])
                    for (vt, vc0, vn, dcol) in vloads:
                        W_ = {"VA": 512, "VB": 256, "VC": 256}[nm(vt)]
                        dma("sp", V[:, :, dcol:dcol + vn],
                            AP(vt, seq * S * W_ + vc0, [[W_, 128], [128 * W_, NB], [1, vn]]), r=scr_all(nm(vt)), w=[Bin])
                        mset("pool", V[:, :, dcol + vn:dcol + vn + 1], 1.0, w=[Bin])
                    kTs = (kT0, kT1)
                    nv, vcol = (129, (0, 0)) if kind == "A" else (65, (0, 65))
                    def accv(i_, m_, u, w_):
                        if kind == "A":
                            k_ = 2 * m_ + u // 2
                            return C["accb"][k_][:, (u % 2) * 256:(u % 2) * 256 + w_], C["Baccb"][k_], (u % 2 == 0)
                        k_ = 2 * (i_ % 2) + m_
                        return C["accb"][k_][:, u * 128:u * 128 + w_], C["Baccb"][k_], (u == 0)

                    def pv(job):
                        (i_, j_, c0_, j0_, pT_, BpT_) = job
                        for m_ in range(2):
                            for u in range(c0_ // 128, 4):
                                a_, Ba_, first = accv(i_, m_, u, nv)
                                mm(a_, pT_[:, m_, u * 128:(u + 1) * 128],
                                   V[:, j_, vcol[m_]:vcol[m_] + nv], (j_ == j0_ and first), (j_ == 4 * i_ + u),
                                   r=[BpT_, Bin], w=[Ba_])
                    C["accv"] = accv

                    for i in range(NT):
                        if mask is not None:
                            mk, Bmk = mask(i)
                        j0 = jmin_fn(i)
                        pending = []
                        for j in range(j0, 4 * i + 4):
                            c0 = max(0, j - 4 * i) * 128
                            D0 = 512 * i - 128 * j
                            off = min(D0, DFAR) + 384
                            far = (hidx is not None) and D0 >= 1664
                            ps, Bps = C["sc2"].next()
                            for m in range(2):
                                mm(ps[:, m, c0:512], kTs[m][:, j * 128:(j + 1) * 128], qT[:, i * 512 + c0:(i + 1) * 512],
                                   True, mask is None, r=[Bin], w=[Bps])
                                if mask is not None:
                                    for u in range(c0 // 128, 4):
                                        mm(ps[:, m, u * 128:(u + 1) * 128], mk[:, u, j * 128:(j + 1) * 128], ident_b[:],
                                           False, u == 3, r=[Bmk, B_const], w=[Bps])
                            pT, BpT = C["pT"].next()
                            if far and hidx[0] == hidx[1]:
                                act(pT[:, :, c0:512], ps[:, :, c0:512], AF.Exp, r=[Bps, B_const], w=[BpT], scale=scale,
                                    bias=farb[:, hidx[0]:hidx[0] + 1])
                            elif far:
                                for m in range(2):
                                    act(pT[:, m, c0:512], ps[:, m, c0:512], AF.Exp, r=[Bps, B_const], w=[BpT], scale=scale,
                                        bias=farb[:, hidx[m]:hidx[m] + 1])
                            else:
                                pR, BpR = C["pE"].next()
                                act(pR[:, :, c0:512], ps[:, :, c0:512], AF.Exp, r=[Bps], w=[BpR], scale=scale)
                                for m in range(2):
                                    tt("dve", pT[:, m, c0:512], pR[:, m, c0:512],
                                       C["strips"][:, sidx[m], off + c0:off + 512], ALU.mult,
                                       r=[BpR, C["Bstrips"]], w=[BpT])
                            pending.append((i, j, c0, j0, pT, BpT))
                            if len(pending) > 2:
                                pv(pending.pop(0))
                        while pending:
                            pv(pending.pop(0))
                        evac(C, i, oh_t, Boh)
                    W_ = D
                    dma("pool", AP(OS, seq * S * W_ + ocol, [[W_, 128], [128 * W_, NB], [1, dst_cols]]),
                        oh_t[:, :, 0:dst_cols], r=[Boh], w=[scr("OS", seq, ocol)])

                def evac_bc(C, i, oh_t, Boh):
                    U = []
                    for m in range(2):
                        for u in range(4):
                            sm, Bsm = C["sm"].next()
                            a_, Ba_, _ = C["accv"](i, m, u, 65)
                            U.append((m, u, sm, Bsm, a_, Ba_))
                    for (m, u, sm, Bsm, a, Ba_) in U:
                        recip(sm[:, 0:1], a[:, 64:65], r=[Ba_], w=[Bsm])
                    for (m, u, sm, Bsm, a, Ba_) in U:
                        ts("dve", oh_t[:, 4 * i + u, m * 64:(m + 1) * 64], a[:, 0:64], sm[:, 0:1], None,
                           ALU.mult, r=[Ba_, Bsm], w=[Boh])

                with ExitStack() as ph:
                    C = att_common(ph, "ab_")
                    lamv = sb("ab_lam", [128, 4, 64], F32, ph)
                    lsm = sb("ab_lsm", [128, 8], F32, ph)
                    prod = sb("ab_prod", [128, 64], F32, ph)
                    gs = sb("ab_gs", [128, 128], F32, ph)
                    Blam, Bgs = Buf(), Buf()
                    for qi, t_ in enumerate((lam_q1, lam_k1, lam_q2, lam_k2)):
                        dma("sp", lamv[:, qi, :], AP(t_, l * 64, [[0, 128], [1, 64]]), w=[Blam])
                    for qi in range(2):
                        tt("dve", prod[:], lamv[:, 2 * qi, :], lamv[:, 2 * qi + 1, :], ALU.mult, r=[Blam], w=[Blam])
                        red(lsm[:, qi:qi + 1], prod[:], ALU.add, r=[Blam], w=[Blam])
                        act(lsm[:, qi:qi + 1], lsm[:, qi:qi + 1], AF.Exp, r=[Blam], w=[Blam])
                    tt("dve", lsm[:, 2:3], lsm[:, 1:2], lsm[:, 0:1], ALU.subtract, r=[Blam], w=[Blam])
                    ts("dve", lsm[:, 2:3], lsm[:, 2:3], -lam_init, None, ALU.add, r=[Blam], w=[Blam])
                    dma("sp", gs[:], AP(subln, l * 128, [[0, 128], [1, 128]]), w=[Bgs])
                    ts("dve", gs[:], gs[:], 1.0 - lam_init, None, ALU.mult, r=[Bgs], w=[Bgs])
                    neglam = lsm[:, 2:3]

                    def evac_a(C, i, oh_t, Boh):
                        U = []
                        for u in range(4):
                            sm, Bsm = C["sm"].next()
                            t0, Bt0 = C["t0"].next()
                            o_, Bo = C["o"].next()
                            a0_, B0_, _ = C["accv"](i, 0, u, 129)
                            a1_, B1_, _ = C["accv"](i, 1, u, 129)
                            U.append((u, sm, Bsm, t0, Bt0, o_, Bo, a0_, a1_, B0_, B1_))
                        for (u, sm, Bsm, t0, Bt0, o_, Bo, a0, a1, B0, B1) in U:
                            recip(sm[:, 0:1], a0[:, 128:129], r=[B0], w=[Bsm])
                        for (u, sm, Bsm, t0, Bt0, o_, Bo, a0, a1, B0, B1) in U:
                            recip(sm[:, 1:2], a1[:, 128:129], r=[B1], w=[Bsm])
                        for (u, sm, Bsm, t0, Bt0, o_, Bo, a0, a1, B0, B1) in U:
                            tt("dve", sm[:, 1:2], sm[:, 1:2], neglam, ALU.mult, r=[Blam], w=[Bsm])
                        for (u, sm, Bsm, t0, Bt0, o_, Bo, a0, a1, B0, B1) in U:
                            act(t0[:], a0[:, 0:128], AF.Identity, r=[B0, Bsm], w=[Bt0], scale=sm[:, 0:1])
                        for (u, sm, Bsm, t0, Bt0, o_, Bo, a0, a1, B0, B1) in U:
                            stt(o_[:], a1[:, 0:128], sm[:, 1:2], t0[:], ALU.mult, ALU.add, r=[B1, Bsm, Bt0], w=[Bo])
                        for (u, sm, Bsm, t0, Bt0, o_, Bo, a0, a1, B0, B1) in U:
                            act(C["sqj4"][u][:], o_[:], AF.Square, r=[Bo], w=[C["Bsqj4"][u], Bsm], accum=sm[:, 2:3])
                        for (u, sm, Bsm, t0, Bt0, o_, Bo, a0, a1, B0, B1) in U:
                            act(sm[:, 3:4], sm[:, 2:3], AF.Ln, r=[Bsm], w=[Bsm], scale=1.0 / 128, bias=epsb[:, 0:1])
                        for (u, sm, Bsm, t0, Bt0, o_, Bo, a0, a1, B0, B1) in U:
                            act(sm[:, 3:4], sm[:, 3:4], AF.Exp, r=[Bsm], w=[Bsm], scale=-0.5)
                        for (u, sm, Bsm, t0, Bt0, o_, Bo, a0, a1, B0, B1) in U:
                            stt(oh_t[:, 4 * i + u, :], o_[:], sm[:, 3:4], gs[:], ALU.mult, ALU.mult,
                                r=[Bo, Bsm, Bgs], w=[Boh])

                    load_strips(C, 0)
                    for h in range(4):
                        att_pair(C, "A", QTA, KTA, h * 128, [(VA, h * 128, 128, 0)], (h, h), 0.125,
                                 lambda i: 0, evac_a, h * 128, 128, hidx=(h, h))
                    load_strips(C, 4)
                    for pr in range(2):
                        att_pair(C, "B", QTB, KTB, pr * 128,
                                 [(VB, pr * 128, 64, 0), (VB, pr * 128 + 64, 64, 65)], (2 * pr, 2 * pr + 1), 0.125,
                                 lambda i: max(0, 4 * i - 17), evac_bc, 512 + pr * 128, 128)
                    P.flush()

                with ExitStack() as ph:
                    iqT = sb("ix_iqT", [128, 4, S], BF16, ph)
                    ik = [sb("ix_ik%d" % m, [128, S], BF16, ph) for m in range(2)]
                    iw_sb = sb("ix_iw", [128, NB, 8], F32, ph)
                    score = Rot([sb("ix_sc%d" % i, [128, S], F32, ph) for i in range(6)])
                    junk = [sb("ix_junk%d" % i, [128, S], mybir.dt.int8, ph) for i in range(3)]
                    Bjunk = [Buf(), Buf(), Buf()]
                    rt = Rot([sb("ix_r%d" % i, [128, 2, 512], BF16, ph) for i in range(3)])
                    dgw = Rot([sb("ix_dg%d" % i, [128, 8, 128], BF16, ph) for i in range(2)])
                    mo = Rot([sb("ix_mo%d" % i, [128, S], BF16, ph) for i in range(2)])
                    smr = Rot([sb("ix_sm%d" % i, [128, 8 + 2 * NBIS], F32, ph) for i in range(6)])
                    lg = Rot([ph.enter_context(nc.psum_tensor(uq("ix_lg%d" % i), [128, 2, 512], F32)) for i in range(2)])
                    scp = Rot([ph.enter_context(nc.psum_tensor(uq("ix_scp%d" % i), [128, 512], F32)) for i in range(2)])
                    Bin = Buf()
                    dma("sp", iqT[:, :, :], AP(IQT, seq * S, [[NSEQ * S, 128], [128 * NSEQ * S, 4], [1, S]]),
                        r=scr_all("IQT"), w=[Bin])
                    mset("pool", ik[0][64:128, :], 0.0, w=[Bin])
                    mset("pool", ik[1][0:64, :], 0.0, w=[Bin])
                    dma("sp", ik[0][0:64, :], AP(IKT, seq * S, [[NSEQ * S, 64], [1, S]]), r=scr_all("IKT"), w=[Bin])
                    dma("sp", ik[1][64:128, :], AP(IKT, seq * S, [[NSEQ * S, 64], [1, S]]), r=scr_all("IKT"), w=[Bin])
                    dma("sp", iw_sb[:, :, :], AP(IW, seq * S * 8, [[8, 128], [128 * 8, NB], [1, 8]]),
                        r=scr_all("IW"), w=[Bin])
                    def scores(b, res):
                        dg, Bdg = dgw.next()
                        for hh in range(8):
                            ts("pool", dg[:, hh, :], ident_b[:], iw_sb[:, b, hh:hh + 1], 0.0, ALU.mult, ALU.add,
                               r=[Bin, B_const], w=[Bdg])
                        sc_t, Bsc = score.next()
                        res.append((sc_t, Bsc))
                        pend = None
                        cur = {}

                        def fin(p):
                            (c_, hp_, ncol_, r__, Br_) = p
                            if hp_ == 0:
                                cur["sp"] = scp.next()
                            sp_, Bsp = cur["sp"]
                            for k_ in range(2):
                                hh_ = 2 * hp_ + k_
                                mm(sp_[:, 0:ncol_], dg[:, hh_, :], r__[:, k_, 0:ncol_], hh_ == 0, hh_ == 7, r=[Bdg, Br_], w=[Bsp])
                            if hp_ == 3:
                                cp("act", sc_t[:, c_ * 512:c_ * 512 + ncol_], sp_[:, 0:ncol_], r=[Bsp], w=[Bsc])
                        for c in range(b // 4 + 1):
                            ncol = 512 if c < b // 4 else ((b % 4) + 1) * 128
                            for hp in range(4):
                                lg_, Blg = lg.next()
                                for k_ in range(2):
                                    mm(lg_[:, k_, 0:ncol], iqT[:, hp, b * 128:(b + 1) * 128],
                                       ik[k_][:, c * 512:c * 512 + ncol], True, True, r=[Bin], w=[Blg])
                                r_, Br = rt.next()
                                act(r_[:, :, 0:ncol], lg_[:, :, 0:ncol], AF.Relu, r=[Blg], w=[Br])
                                if pend is not None:
                                    fin(pend)
                                pend = (c, hp, ncol, r_, Br)
                                yield
                        fin(pend)

                    SN = 8 + NBIS

                    def bisect(b, sc_t, Bsc, jk, Bjk, on_act):
                        Sc = (b + 1) * 128
                        sm, Bsm = smr.next()
                        red(sm[:, 1:2], sc_t[:, 0:Sc], ALU.max, r=[Bsc], w=[Bsm]); yield
                        red(sm[:, 0:1], sc_t[:, 0:Sc], ALU.min, r=[Bsc], w=[Bsm]); yield
                        tt("dve", sm[:, 1:2], sm[:, 1:2], sm[:, 0:1], ALU.subtract, r=[Bsm], w=[Bsm]); yield
                        tt("dve", sc_t[:, b * 128:Sc], sc_t[:, b * 128:Sc], tri_f[:], ALU.add, r=[B_const], w=[Bsc]); yield
                        ts("dve", sm[:, 8:8 + NBIS], pw2[:, 0:NBIS], sm[:, 1:2], None, ALU.mult, r=[Bsm, B_const], w=[Bsm]); yield
                        if not on_act:
                            tt("dve", sm[:, 2:3], sm[:, 0:1], sm[:, 8:9], ALU.add, r=[Bsm], w=[Bsm]); yield
                            for it in range(NBIS):
                                ts("dve", jk[:, 0:Sc], sc_t[:, 0:Sc], sm[:, 2:3], None, ALU.is_ge, ALU.add,
                                   r=[Bsc, Bsm], w=[Bjk, Bsm], accum=sm[:, 3:4]); yield
                                ts("dve", sm[:, 4:5], sm[:, 3:4], float(TOPK) - 0.5, -0.5, ALU.is_ge, ALU.add, r=[Bsm], w=[Bsm]); yield
                                stt(sm[:, 2:3], sm[:, 4:5], sm[:, 8 + it:9 + it], sm[:, 2:3], ALU.mult, ALU.add, r=[Bsm], w=[Bsm]); yield
                            stt(sm[:, 0:1], sm[:, 8 + NBIS - 1:8 + NBIS], -1.0, sm[:, 2:3], ALU.mult, ALU.add, r=[Bsm], w=[Bsm]); yield
                        else:
                            ts("dve", sm[:, SN:SN + NBIS], sm[:, 8:8 + NBIS], -1.0, None, ALU.mult, r=[Bsm], w=[Bsm]); yield
                            stt(sm[:, 2:3], sm[:, 0:1], -1.0, sm[:, 8:9], ALU.mult, ALU.subtract, r=[Bsm], w=[Bsm]); yield
                            for it in range(NBIS):
                                act(jk[:, 0:Sc], sc_t[:, 0:Sc], AF.Sign, r=[Bsc, Bsm], w=[Bjk, Bsm], bias=sm[:, 2:3],
                                    scale=1.0, accum=sm[:, 3:4]); yield; yield
                                ts("pool", sm[:, 4:5], sm[:, 3:4], float(2 * TOPK - Sc) - 0.5, -0.5, ALU.is_ge, ALU.add,
                                   r=[Bsm], w=[Bsm]); yield
                                ts("pool", sm[:, 2:3], sm[:, 4:5], sm[:, SN + it:SN + it + 1], sm[:, 2:3], ALU.mult, ALU.add,
                                   r=[Bsm], w=[Bsm]); yield
                            stt(sm[:, 0:1], sm[:, 2:3], -1.0, sm[:, 8 + NBIS - 1:8 + NBIS], ALU.mult, ALU.subtract, r=[Bsm], w=[Bsm]); yield
                        mo_t, Bmo = mo.next()
                        ts("dve", mo_t[:, 0:Sc], sc_t[:, 0:Sc], sm[:, 0:1], NEGM, ALU.is_lt, ALU.mult, r=[Bsc, Bsm], w=[Bmo]); yield
                        dma("sp", MASK.ap()[b * 128:(b + 1) * 128, 0:Sc], mo_t[:, 0:Sc], r=[Bmo], w=[scr("MASK", b)])

                    def run_rr(gens):
                        alive = list(gens)
                        while alive:
                            for g_ in list(alive):
                                try:
                                    next(g_)
                                except StopIteration:
                                    alive.remove(g_)

                    def chain(gl):
                        for g_ in gl:
                            yield from g_

                    groups = [list(range(b0, min(b0 + 3, NB))) for b0 in range(0, NB, 3)]
                    prev = []
                    for grp in groups:
                        res = []
                        run_rr([chain([scores(b_, res) for b_ in grp])] + prev)
                        prev = []
                        for gi, b_ in enumerate(grp):
                            on_act = (gi == 2)
                            prev.append(bisect(b_, res[gi][0], res[gi][1], junk[gi], Bjunk[gi], on_act))
                    run_rr(prev)
                    P.flush()

                with ExitStack() as ph:
                    C = att_common(ph, "c_")
                    mkr = Rot([sb("c_mk%d" % i, [128, 4, S], BF16, ph) for i in range(2)])

                    def load_mask(i):
                        mk, Bmk = mkr.next()
                        for u in range(4):
                            b = 4 * i + u
                            dma("sp", mk[:, u, 0:(b + 1) * 128], MASK.ap()[b * 128:(b + 1) * 128, 0:(b + 1) * 128],
                                r=[scr("MASK", b)], w=[Bmk])
                        return mk, Bmk
                    load_strips(C, 8)
                    for pr in range(2):
                        att_pair(C, "C", QTC, KTC, pr * 128,
                                 [(VC, pr * 128, 64, 0), (VC, pr * 128 + 64, 64, 65)], (2 * pr, 2 * pr + 1), 0.125,
                                 lambda i: 0, evac_bc, 768 + pr * 128, 128, mask=load_mask, hidx=(8 + 2 * pr, 9 + 2 * pr))
                    P.flush()

            with ExitStack() as ph:
                KTm = sb("d1_KTm", [128, NSEQ, 4, 2, MEM], BF16, ph)
                Vm = sb("d1_Vm", [128, NSEQ, 2, 4, 256], BF16, ph)
                Bkv = Buf()
                NPS = 5
                psx = [ph.enter_context(nc.psum_tensor(uq("d1_ps%d" % i), [128, 512], F32)) for i in range(NPS)]
                Bpsx = [Buf() for _ in range(NPS)]
                ptrs = [(ph.enter_context(nc.psum_tensor(uq("d1_ptr%d" % i), [128, 1024], BF16)), Buf()) for i in range(2)]
                lps = ph.enter_context(nc.psum_tensor(uq("d1_lps"), [128, 16], F32))
                Blps = Buf()
                pi = 0
                with ExitStack() as ph2:
                    wkv = sb("mk_w", [128, KC, 2 * D], BF16, ph2)
                    stage = Rot([sb("mk_st%d" % i, [128, 2048], F32, ph2) for i in range(3)])
                    gkv = sb("mk_g", [128, KC], F32, ph2)
                    Bw, Bg = Buf(), Buf()
                    load_gain(gkv, norm_memkv, l * D, KC, Bg)
                    load_weight(wkv, w_mkv, l * D * 2 * D, KC, 2 * D, (gkv, Bg), stage, Bw)
                    for sq_i in range(NSEQ):
                        for h in range(4):
                            for dc in range(2):
                                n0 = h * 256 + dc * 128
                                b = pi % NPS; pi += 1
                                for kc in range(KC):
                                    mm(psx[b][:, 0:MEM], wkv[:, kc, n0:n0 + 128], memhatT[:, sq_i, kc, :],
                                       kc == 0, kc == KC - 1, r=[Bw, B_memhat], w=[Bpsx[b]])
                                cp("act", KTm[:, sq_i, h, dc, :], psx[b][:, 0:MEM], r=[Bpsx[b]], w=[Bkv])
                        for blk in range(2):
                            for nch in range(2):
                                b = pi % NPS; pi += 1
                                for kc in range(KC):
                                    mm(psx[b][:, :], memhatT[:, sq_i, kc, blk * 128:(blk + 1) * 128],
                                       wkv[:, kc, D + nch * 512:D + (nch + 1) * 512], kc == 0, kc == KC - 1,
                                       r=[Bw, B_memhat], w=[Bpsx[b]])
                                cp("dve", Vm[:, sq_i, blk, 2 * nch:2 * nch + 2, :],
                                   psx[b][:, :].rearrange("p (h d) -> p h d", h=2), r=[Bpsx[b]], w=[Bkv])
                    P.flush()
                wo = sb("d1_wo", [128, KC, D], BF16, ph)
                wq = sb("d1_wq", [128, KC, D], BF16, ph)
                wm = sb("d1_wm", [128, KC, D], BF16, ph)
                gq = sb("d1_gq", [128, KC], F32, ph)
                stage = Rot([sb("d1_st%d" % i, [128, 2048], F32, ph) for i in range(3)])
                xt = Rot([sb("d1_x%d" % i, [128, D], F32, ph) for i in range(6)])
                ob = Rot([sb("d1_ob%d" % i, [128, D], BF16, ph) for i in range(2)])
                hb = Rot([sb("d1_hb%d" % i, [128, D], BF16, ph) for i in range(4)])
                sqr = Rot([sb("d1_sq%d" % i, [128, D], BF16, ph) for i in range(2)])
                ssr = Rot([sb("d1_ss%d" % i, [128, 2], F32, ph) for i in range(8)])
                oT = sb("d1_oT", [128, KC, 512], BF16, ph); BoT = Buf()
                hT = sb("d1_hT", [128, KC, 512], BF16, ph); BhT = Buf()
                qmT = sb("d1_qmT", [128, 8, 512], BF16, ph); BqmT = Buf()
                pTm = Rot([sb("d1_pT%d" % i, [128, 2, 512], BF16, ph) for i in range(4)])
                omT = sb("d1_omT", [128, 4, 2, 512], BF16, ph); BomT = Buf()
                rl = sb("d1_rl", [128, 16], F32, ph); Brl = Buf()
                Bwo, Bwq, Bwm, Bgq = Buf(), Buf(), Buf(), Buf()
                load_gain(gq, norm_mem, l * D, KC, Bgq)
                load_weight(wo, w_out, l * D * D, KC, D, None, stage, Bwo)
                load_weight(wq, w_mq, l * D * D, KC, D, (gq, Bgq), stage, Bwq)
                load_weight(wm, w_mo, l * D * D, KC, D, None, stage, Bwm)
                for st_i in range(NTOK // 512):
                    seq = (st_i * 512) // S
                    xs = []
                    for u in range(4):
                        row0 = st_i * 512 + u * 128
                        x_t, Bxt = xt.next()
                        o_b, Bob = ob.next()
                        xs.append((x_t, Bxt))
                        dma("sp", x_t[:], xsrc[0].ap()[row0:row0 + 128, :], r=[xbuf(xsrc[1], row0 // 128)], w=[Bxt])
                        dma("sp", o_b[:], OS.ap()[row0:row0 + 128, :],
                            r=[scr("OS", seq, oc) for oc in (0, 128, 256, 384, 512, 640, 768, 896)], w=[Bob])
                        pt_, Bpt_ = ptrs[u % 2]
                        for kc in range(KC):
                            tr(pt_[:, kc * 128:(kc + 1) * 128], o_b[:, kc * 128:(kc + 1) * 128], ident_b[:],
                               r=[Bob, B_const], w=[Bpt_])
                        cp("act", oT[:, :, u * 128:(u + 1) * 128], pt_[:, :].rearrange("p (k t) -> p k t", k=KC),
                           r=[Bpt_], w=[BoT])
                    for u in range(4):
                        x_t, Bxt = xs[u]
                        for nch in range(2):
                            b = pi % NPS; pi += 1
                            for kc in range(KC):
                                mm(psx[b][:, :], oT[:, kc, u * 128:(u + 1) * 128], wo[:, kc, nch * 512:(nch + 1) * 512],
                                   kc == 0, kc == KC - 1, r=[BoT, Bwo], w=[Bpsx[b]])
                            tt("dve", x_t[:, nch * 512:(nch + 1) * 512], psx[b][:, :], x_t[:, nch * 512:(nch + 1) * 512],
                               ALU.add, r=[Bpsx[b]], w=[Bxt])
                    hbs = [hb.next() for _ in range(4)]
                    sss = [ssr.next() for _ in range(4)]
                    for u in range(4):
                        sq_t, Bsq_ = sqr.next()
                        act(sq_t[:], xs[u][0][:], AF.Square, r=[xs[u][1]], w=[sss[u][1], Bsq_], accum=sss[u][0][:, 0:1])
                    for u in range(4):
                        act(sss[u][0][:, 1:2], sss[u][0][:, 0:1], AF.Ln, r=[sss[u][1]], w=[sss[u][1]], scale=1.0 / D,
                            bias=epsb[:, 0:1])
                    for u in range(4):
                        act(sss[u][0][:, 1:2], sss[u][0][:, 1:2], AF.Exp, r=[sss[u][1]], w=[sss[u][1]], scale=-0.5)
                    for u in range(4):
                        ts("dve", hbs[u][0][:], xs[u][0][:], sss[u][0][:, 1:2], None, ALU.mult,
                           r=[xs[u][1], sss[u][1]], w=[hbs[u][1]])
                    for u in range(4):
                        h_b, Bhb = hbs[u]
                        pt_, Bpt_ = ptrs[u % 2]
                        for kc in range(KC):
                            tr(pt_[:, kc * 128:(kc + 1) * 128], h_b[:, kc * 128:(kc + 1) * 128], ident_b[:],
                               r=[Bhb, B_const], w=[Bpt_])
                        cp("act", hT[:, :, u * 128:(u + 1) * 128], pt_[:, :].rearrange("p (k t) -> p k t", k=KC),
                           r=[Bpt_], w=[BhT])
                    for nb in range(8):
                        b = pi % NPS; pi += 1
                        for kc in range(KC):
                            mm(psx[b][:, :], wq[:, kc, nb * 128:(nb + 1) * 128], hT[:, kc, :], kc == 0, kc == KC - 1,
                               r=[BhT, Bwq], w=[Bpsx[b]])
                        cp("act" if nb % 2 else "dve", qmT[:, nb, :], psx[b][:, :], r=[Bpsx[b]], w=[BqmT])
                    pts = [pTm.next() for _ in range(4)]
                    for h in range(4):
                        p_t, Bp = pts[h]
                        for blk in range(2):
                            b = pi % NPS; pi += 1
                            for dc in range(2):
                                mm(psx[b][:, :], KTm[:, seq, h, dc, blk * 128:(blk + 1) * 128], qmT[:, 2 * h + dc, :],
                                   dc == 0, dc == 1, r=[Bkv, BqmT], w=[Bpsx[b]])
                            act(p_t[:, blk, :], psx[b][:, :], AF.Exp, r=[Bpsx[b]], w=[Bp], scale=1.0 / 16.0)
                    for h in range(4):
                        p_t, Bp = pts[h]
                        for u in range(4):
                            for blk in range(2):
                                mm(lps[:, u * 4 + h:u * 4 + h + 1], p_t[:, blk, u * 128:(u + 1) * 128], ones_b[:, 0:1],
                                   blk == 0, blk == 1, r=[Bp, B_const], w=[Blps])
                        for dc in range(2):
                            b = pi % NPS; pi += 1
                            for blk in range(2):
                                mm(psx[b][:, :], Vm[:, seq, blk, h, dc * 128:(dc + 1) * 128], p_t[:, blk, :],
                                   blk == 0, blk == 1, r=[Bkv, Bp], w=[Bpsx[b]])
                            cp("act" if dc else "dve", omT[:, h, dc, :], psx[b][:, :], r=[Bpsx[b]], w=[BomT])
                    recip(rl[:, :], lps[:, :], r=[Blps], w=[Brl])
                    for u in range(4):
                        x_t, Bxt = xs[u]
                        row0 = st_i * 512 + u * 128
                        for h in range(4):
                            for nch in range(2):
                                b = pi % NPS; pi += 1
                                for dc in range(2):
                                    mm(psx[b][:, :], omT[:, h, dc, u * 128:(u + 1) * 128],
                                       wm[:, 2 * h + dc, nch * 512:(nch + 1) * 512], dc == 0, dc == 1,
                                       r=[BomT, Bwm], w=[Bpsx[b]])
                                stt(x_t[:, nch * 512:(nch + 1) * 512], psx[b][:, :], rl[:, u * 4 + h:u * 4 + h + 1],
                                    x_t[:, nch * 512:(nch + 1) * 512], ALU.mult, ALU.add, r=[Bpsx[b], Brl], w=[Bxt])
                        dma("pool", XA.ap()[row0:row0 + 128, :], x_t[:], r=[Bxt], w=[xbuf("xa", row0 // 128)])
                P.flush()

            with ExitStack() as ph:
                T2 = 256
                wu = sb("d2_wu", [128, KC, 2 * FF], BF16, ph)
                wd = sb("d2_wd", [128, NFB, D], BF16, ph)
                gf = sb("d2_gf", [128, KC], F32, ph)
                cw = sb("d2_cw", [128, 3, 44], F32, ph)
                cb = sb("d2_cb", [128, 44], F32, ph)
                Bwu, Bwd, Bgf, Bcw = Buf(), Buf(), Buf(), Buf()
                with ExitStack() as ph2:
                    stage = Rot([sb("d2_st%d" % i, [128, 2048], F32, ph2) for i in range(3)])
                    load_gain(gf, norm_ffn, l * D, KC, Bgf)
                    for j in range(3):
                        load_gain(cw[:, j, :], conv_w, (l * 3 + j) * 2 * FF, 44, Bcw)
                    load_gain(cb, conv_b, l * 2 * FF, 44, Bcw)
                    load_weight(wu, w_up, l * D * 2 * FF, KC, 2 * FF, (gf, Bgf), stage, Bwu)
                    load_weight(wd, w_down, l * FF * D, NFB, D, None, stage, Bwd)
                    P.flush()
                xt = Rot([sb("d2_x%d" % i, [128, D], F32, ph) for i in range(4)])
                hb = Rot([sb("d2_hb%d" % i, [128, D], BF16, ph) for i in range(2)])
                sqj = sb("d2_sq", [128, D], BF16, ph); Bsq = Buf()
                ssr = Rot([sb("d2_ss%d" % i, [128, 2], F32, ph) for i in range(3)])
                hT = Rot([sb("d2_hT%d" % i, [128, KC, T2], BF16, ph) for i in range(2)])
                ubuf = Rot([sb("d2_u%d" % i, [128, T2 + 2], F32, ph) for i in range(5)])
                ubuf.t = [(t_, (Buf(), Buf())) for (t_, _) in ubuf.t]
                cbuf = Rot([sb("d2_c%d" % i, [128, T2], F32, ph) for i in range(5)])
                gbuf = Rot([sb("d2_gt%d" % i, [128, T2], BF16, ph) for i in range(3)])
                aT = Rot([sb("d2_aT%d" % i, [128, NFB, T2], BF16, ph) for i in range(2)])
                carry = sb("d2_carry", [128, 44, 2], F32, ph); Bcarry = [Buf() for _ in range(44)]
                ptr = ph.enter_context(nc.psum_tensor(uq("d2_ptr"), [128, 1024], BF16)); Bptr = Buf()
                psx = [ph.enter_context(nc.psum_tensor(uq("d2_ps%d" % i), [128, 512], F32)) for i in range(6)]
                Bpsx = [Buf() for _ in range(6)]
                pi = 0
                last = (l == L - 1)
                pend_st = []
                for st_i in range(NTOK // T2):
                    seq = (st_i * T2) // S
                    if (st_i * T2) % S == 0:
                        mset("pool", carry[:, :, :], 0.0, w=Bcarry)
                    hTt, BhT = hT.next()
                    xs = []
                    for u in range(T2 // 128):
                        row0 = st_i * T2 + u * 128
                        x_t, Bxt = xt.next()
                        xs.append((x_t, Bxt))
                        h_b, Bhb = hb.next()
                        s_s, Bs = ssr.next()
                        dma("sp", x_t[:], XA.ap()[row0:row0 + 128, :], r=[xbuf("xa", row0 // 128)], w=[Bxt])
                        act(sqj[:], x_t[:], AF.Square, r=[Bxt], w=[Bs, Bsq], accum=s_s[:, 0:1])
                        rstd_from_ss(s_s[:, 1:2], s_s[:, 0:1], D, r=[Bs], w=[Bs])
                        ts("dve", h_b[:], x_t[:], s_s[:, 1:2], None, ALU.mult, r=[Bxt, Bs], w=[Bhb])
                        for kc in range(KC):
                            tr(ptr[:, kc * 128:(kc + 1) * 128], h_b[:, kc * 128:(kc + 1) * 128], ident_b[:],
                               r=[Bhb, B_const], w=[Bptr])
                        cp("act", hTt[:, :, u * 128:(u + 1) * 128], ptr[:, :].rearrange("p (k t) -> p k t", k=KC),
                           r=[Bptr], w=[BhT])
                    for (x_p, Bx_p, r_p) in pend_st:
                        dma("sp", XB.ap()[r_p:r_p + 128, :], x_p[:], r=[Bx_p], w=[xbuf("xb", r_p // 128)])
                    pend_st = []
                    a_t, Ba = aT.next()
                    pend_g = None

                    def gate_mul(a_t_, Ba_, fb_, cg, Bcg, cv_, Bcv):
                        g_t, Bgt = gbuf.next()
                        act(g_t[:], cg[:], AF.Silu, r=[Bcg], w=[Bgt])
                        tt("dve", a_t_[:, fb_, :], cv_[:], g_t[:], ALU.mult, r=[Bcv, Bgt], w=[Ba_])
                    for fb in range(NFB):
                        cs = []
                        for which, nb in enumerate((fb, fb + NFB)):
                            b = pi % 6; pi += 1
                            for kc in range(KC):
                                mm(psx[b][:, 0:T2], wu[:, kc, nb * 128:(nb + 1) * 128], hTt[:, kc, :], kc == 0, kc == KC - 1,
                                   r=[BhT, Bwu], w=[Bpsx[b]])
                            u_t, (Bu, Buh) = ubuf.next()
                            c_t, Bc = cbuf.next()
                            cp("pool", u_t[:, 0:2], carry[:, nb, :], r=[Bcarry[nb]], w=[Buh])
                            cp("act", u_t[:, 2:T2 + 2], psx[b][:, 0:T2], r=[Bpsx[b]], w=[Bu])
                            act(c_t[:], psx[b][:, 0:T2], AF.Identity, r=[Bpsx[b], Bcw], w=[Bc],
                                scale=cw[:, 2, nb:nb + 1], bias=cb[:, nb:nb + 1])
                            cp("pool", carry[:, nb, :], u_t[:, T2:T2 + 2], r=[Bu], w=[Bcarry[nb]])
                            cs.append((c_t, Bc, u_t, Bu, Buh, nb))
                        for tap in (1, 0):
                            for (c_t, Bc, u_t, Bu, Buh, nb) in cs:
                                stt(c_t[:], u_t[:, tap:T2 + tap], cw[:, tap, nb:nb + 1], c_t[:], ALU.mult, ALU.add,
                                    r=[Bu, Buh, Bcw], w=[Bc])
                        if pend_g is not None:
                            gate_mul(*pend_g)
                        pend_g = (a_t, Ba, fb, cs[0][0], cs[0][1], cs[1][0], cs[1][1])
                    gate_mul(*pend_g)
                    pend_g = None
                    for u in range(T2 // 128):
                        x_t, Bxt = xs[u]
                        row0 = st_i * T2 + u * 128
                        for nch in range(2):
                            b = pi % 6; pi += 1
                            for fb in range(NFB):
                                mm(psx[b][:, :], a_t[:, fb, u * 128:(u + 1) * 128], wd[:, fb, nch * 512:(nch + 1) * 512],
                                   fb == 0, fb == NFB - 1, r=[Ba, Bwd], w=[Bpsx[b]])
                            tt("dve", x_t[:, nch * 512:(nch + 1) * 512], psx[b][:, :], x_t[:, nch * 512:(nch + 1) * 512],
                               ALU.add, r=[Bpsx[b]], w=[Bxt])
                        pend_st.append((x_t, Bxt, row0))
                for (x_p, Bx_p, r_p) in pend_st:
                    dma("sp", XB.ap()[r_p:r_p + 128, :], x_p[:], r=[Bx_p], w=[xbuf("xb", r_p // 128)])
                P.flush()

        with ExitStack() as ph:
            gfin = sb("fn_g", [128, D], F32, ph); Bg = Buf()
            xt = Rot([sb("fn_x%d" % i, [128, D], F32, ph) for i in range(3)])
            yt = Rot([sb("fn_y%d" % i, [128, D], F32, ph) for i in range(3)])
            sqj = sb("fn_sq", [128, D], BF16, ph); Bsq = Buf()
            ssr = Rot([sb("fn_ss%d" % i, [128, 2], F32, ph) for i in range(3)])
            dma("sp", gfin[:], AP(norm_final, 0, [[0, 128], [1, D]]), w=[Bg])
            for t in range(NTOK // 128):
                x_t, Bxt = xt.next()
                y_t, Byt = yt.next()
                s_s, Bs = ssr.next()
                dma("sp", x_t[:], XB.ap()[t * 128:(t + 1) * 128, :], r=[xbuf("xb", t)], w=[Bxt])
                act(sqj[:], x_t[:], AF.Square, r=[Bxt], w=[Bs, Bsq], accum=s_s[:, 0:1])
                rstd_from_ss(s_s[:, 1:2], s_s[:, 0:1], D, r=[Bs], w=[Bs])
                stt(y_t[:], x_t[:], s_s[:, 1:2], gfin[:], ALU.mult, ALU.mult, r=[Bxt, Bs, Bg], w=[Byt])
                dma("pool", y_out.ap()[t * 128:(t + 1) * 128, :], y_t[:], r=[Byt], w=[Buf()])
            P.flush()
        print("ops", P.nops, "waits", P.nwaits, flush=True)
    return nc


S_FULL = 4096
NSEQ_FULL = 2
L_FULL = 4
NCORES = 8
_CACHE = {}


def kernel(**inputs):
    f32 = lambda a: np.ascontiguousarray(np.asarray(a, dtype=np.float32))
    x = f32(inputs["x"])
    mem = f32(inputs["mem"])
    B = x.shape[0]
    per = B // NCORES
    consts = host_consts()
    shared = {k: f32(inputs[k]) for k in ("rel_bias", "norm_mix", "w_in", "lam_q1", "lam_k1", "lam_q2", "lam_k2",
                                          "subln", "w_out", "norm_mem", "norm_memkv", "w_mq", "w_mkv", "w_mo",
                                          "norm_ffn", "w_up", "conv_w", "conv_b", "w_down")}
    shared["norm_final"] = f32(inputs["norm_final"]).reshape(1, D)
    shared.update(consts)
    key = "full"
    if key not in _CACHE:
        _CACHE[key] = build(S_FULL, per, L_FULL, 256)
    nc = _CACHE[key]
    in_maps = []
    for c in range(NCORES):
        m = dict(shared)
        m["x"] = np.ascontiguousarray(x[c * per:(c + 1) * per].reshape(per * S_FULL, D))
        m["mem"] = np.ascontiguousarray(mem[c * per:(c + 1) * per].reshape(per * MEM, D))
        in_maps.append(m)
    res = run_bass_kernel_spmd(nc, in_maps, core_ids=list(range(NCORES)))
    out = np.concatenate([np.asarray(r["y"]).reshape(per, S_FULL, D) for r in res.results], axis=0)
    return out.astype(np.float32)
```

```python
import math
from contextlib import ExitStack
import numpy as np
import concourse.bass as bass
import concourse.mybir as mybir
from concourse.bass_utils import run_bass_kernel_spmd

F32 = mybir.dt.float32
BF16 = mybir.dt.bfloat16
AF = mybir.ActivationFunctionType
ALU = mybir.AluOpType
AX = mybir.AxisListType

D = 1024
KC = 8
NIN = 3656
FF = 2816
NFB = 22
MEM = 256
NG = 3200
MS = 3072
DOFF = 511
DFAR = 2176
EPS = 1e-6
NEGM = -30000.0
NBIS = 14


class Buf:
    __slots__ = ("name", "w", "r")

    def __init__(self, name=""):
        self.name = name
        self.w = {}
        self.r = {}


class Stream:
    def __init__(self, name):
        self.name = name
        self.ops = []
        self.cnt = 0
        self.dcnt = 0
        self.slot = [0] * NSLOT
        self.known = {}


NSLOT = 12


class Prog:
    STREAMS = ("pe", "act", "dve", "pool", "sp")

    def __init__(self, nc, es):
        self.nc = nc
        self.streams = {n: Stream(n) for n in self.STREAMS}
        self.sems = {}
        for n in self.STREAMS:
            self.sems[n] = es.enter_context(nc.semaphore("s_" + n))
        for n in ("pool", "sp"):
            for k in range(NSLOT):
                self.sems["%s.d%d" % (n, k)] = es.enter_context(nc.semaphore("d_%s%d" % (n, k)))
        self.nops = 0
        self.nwaits = 0

    def _deps(self, st, reads, writes, dma=False, extra=None):
        deps = {}
        own_d = st.name + ".d"

        def add(k, v):
            if k == st.name and k == "pe":
                return
            if deps.get(k, 0) < v:
                deps[k] = v
        for b in reads:
            for k, v in b.w.items():
                add(k, v)
        for b in writes:
            for k, v in b.w.items():
                if dma and k.startswith(own_d):
                    continue
                add(k, v)
            for k, v in b.r.items():
                add(k, v)
        if extra is not None:
            add(*extra)
        out = {}
        for k, v in deps.items():
            if st.known.get(k, 0) < v:
                st.known[k] = v
                out[k] = v
        return out

    def op(self, stream, fn, reads=(), writes=(), dma=False):
        st = self.streams[stream]
        if dma:
            slot = st.dcnt % NSLOT
            st.dcnt += 1
            kname = "%s.d%d" % (stream, slot)
            extra = (kname, st.slot[slot]) if st.slot[slot] else None
            waits = self._deps(st, reads, writes, True, extra)
            st.slot[slot] += 1
            key = (kname, st.slot[slot])
        else:
            waits = self._deps(st, reads, writes, False)
            st.cnt += 1
            key = (stream, st.cnt)
        st.ops.append((waits, fn, key[0] if dma else None))
        for b in reads:
            if b.r.get(key[0], 0) < key[1]:
                b.r[key[0]] = key[1]
        for b in writes:
            if dma:
                b.w = {k: v for k, v in b.w.items() if k.startswith(stream + ".d")}
                b.w[key[0]] = key[1]
            else:
                b.w = {key[0]: key[1]}
            b.r = {}
        self.nops += 1
        self.nwaits += len(waits)
        return key

    def flush(self):
        nc = self.nc
        sems = self.sems
        fin = {}
        for n, st in self.streams.items():
            if st.cnt:
                fin[n] = st.cnt
            for k in range(NSLOT):
                if st.slot[k]:
                    fin["%s.d%d" % (n, k)] = st.slot[k]

        def val(k, v):
            return v * 16 if ".d" in k else v

        def run(stname):
            st = self.streams[stname]
            ops = st.ops
            st.ops = []

            def body(e):
                for waits, fn, dkey in ops:
                    for k, v in waits.items():
                        e.wait_ge(sems[k], val(k, v))
                    ins = fn(e)
                    if dkey is not None:
                        ins.then_inc(sems[dkey], 16)
                    else:
                        ins.then_inc(sems[stname], 1)
                for k, v in fin.items():
                    if k == stname:
                        continue
                    if st.known.get(k, 0) < v:
                        e.wait_ge(sems[k], val(k, v))
                        st.known[k] = v
            return body

        with nc.Block() as block:
            block.tensor(run("pe"))
            block.scalar(run("act"))
            block.vector(run("dve"))
            block.gpsimd(run("pool"))
            block.sync(run("sp"))


class Rot:
    def __init__(self, tiles):
        self.t = [(t, Buf()) for t in tiles]
        self.i = 0

    def next(self):
        r = self.t[self.i % len(self.t)]
        self.i += 1
        return r


def rel_bucket_np(n):
    n = np.maximum(n, 0)
    nf = np.maximum(n, 1).astype(np.float32)
    large = 16 + (np.log(nf / np.float32(16)) / np.float32(math.log(2048 / 16)) * np.float32(16)).astype(np.int32)
    large = np.minimum(large, 31)
    return np.where(n < 16, n, large)


def host_consts():
    d = np.arange(NG, dtype=np.int64) - DOFF
    oh = np.zeros((34, NG), np.float32)
    bk = rel_bucket_np(d.astype(np.int32))
    valid = d >= 0
    oh[bk[valid], np.nonzero(valid)[0]] = 1.0
    oh[32] = np.where(valid, 0.0, NEGM)
    mult = ((d <= 128).astype(np.int64) + ((d % 4 == 0) & (d <= 512)) + ((d % 16 == 0) & (d <= 2048)))
    mult = np.where(valid, mult, 0)
    oh[33] = np.where(mult > 0, 8.0 * np.log(np.maximum(mult, 1)).astype(np.float32), NEGM)
    sel = np.zeros((2, 12), np.float32)
    sel[0, 0:4] = 1.0
    sel[0, 8:12] = 1.0
    sel[1, 4:8] = 1.0
    ident = np.eye(128, dtype=np.float32)
    J = np.ascontiguousarray(ident[::-1])
    tri = np.where(np.arange(128)[None, :] <= np.arange(128)[:, None], 0.0, -1e30).astype(np.float32)
    return {"c_onehot": oh, "c_sel": sel, "c_ident": ident, "c_J": J, "c_tri": tri}


def build(S, NSEQ, L, TOPK, debug_outs=False):
    nc = bass.Bass("TRN2", target_bir_lowering=False)
    NB = S // 128
    NT = S // 512
    NTOK = NSEQ * S

    NAMES = {}

    def din(name, shape, dt=F32):
        t = nc.dram_tensor(name, list(shape), dt, kind="ExternalInput")
        NAMES[id(t)] = name
        return t

    def dscr(name, shape, dt=BF16):
        t = nc.dram_tensor(name, list(shape), dt, kind="ExternalOutput" if debug_outs else "Internal")
        NAMES[id(t)] = name
        return t

    def nm(t):
        return NAMES[id(t)]

    x_in = din("x", [NTOK, D])
    mem_in = din("mem", [NSEQ * MEM, D])
    rel_bias = din("rel_bias", [32, 12])
    norm_mix = din("norm_mix", [L, D])
    w_in = din("w_in", [L, D, NIN])
    lam_q1 = din("lam_q1", [L, 64]); lam_k1 = din("lam_k1", [L, 64])
    lam_q2 = din("lam_q2", [L, 64]); lam_k2 = din("lam_k2", [L, 64])
    subln = din("subln", [L, 128])
    w_out = din("w_out", [L, D, D])
    norm_mem = din("norm_mem", [L, D]); norm_memkv = din("norm_memkv", [L, D])
    w_mq = din("w_mq", [L, D, D]); w_mkv = din("w_mkv", [L, D, 2 * D]); w_mo = din("w_mo", [L, D, D])
    norm_ffn = din("norm_ffn", [L, D])
    w_up = din("w_up", [L, D, 2 * FF]); conv_w = din("conv_w", [L, 3, 2 * FF]); conv_b = din("conv_b", [L, 2 * FF])
    w_down = din("w_down", [L, FF, D])
    norm_final = din("norm_final", [1, D])
    c_onehot = din("c_onehot", [34, NG]); c_sel = din("c_sel", [2, 12])
    c_ident = din("c_ident", [128, 128]); c_J = din("c_J", [128, 128]); c_tri = din("c_tri", [128, 128])
    y_out = nc.dram_tensor("y", [NTOK, D], F32, kind="ExternalOutput")

    QTA = dscr("QTA", [512, NSEQ, S]); KTA = dscr("KTA", [512, NSEQ, S])
    QTB = dscr("QTB", [256, NSEQ, S]); KTB = dscr("KTB", [256, NSEQ, S])
    QTC = dscr("QTC", [256, NSEQ, S]); KTC = dscr("KTC", [256, NSEQ, S])
    IQT = dscr("IQT", [512, NSEQ, S]); IKT = dscr("IKT", [64, NSEQ, S])
    VA = dscr("VA", [NTOK, 512]); VB = dscr("VB", [NTOK, 256]); VC = dscr("VC", [NTOK, 256])
    IW = dscr("IW", [NTOK, 8], F32)
    OS = dscr("OS", [NTOK, D])
    MASK = dscr("MASK", [S, S])
    XA = dscr("XA", [NTOK, D], F32); XB = dscr("XB", [NTOK, D], F32)
    GSCR = dscr("GSCR", [12, NG])
    ESCR = dscr("ESCR", [12, NG])

    def AP(t, offset, ap):
        return bass.AP(tensor=t, offset=offset, ap=[list(a) for a in ap])

    es = ExitStack()
    with es:
        P = Prog(nc, es)

        _uid = [0]

        def uq(name):
            _uid[0] += 1
            return "%s_%d" % (name, _uid[0])

        def sb(name, shape, dt=BF16, stack=es):
            return stack.enter_context(nc.sbuf_tensor(uq(name), list(shape), dt))

        def mm(out, lhsT, rhs, start, stop, r=(), w=()):
            P.op("pe", lambda e: e.matmul(out, lhsT=lhsT, rhs=rhs, start=start, stop=stop, skip_group_check=True), r, w)

        def tr(out, in_, ident, r=(), w=()):
            P.op("pe", lambda e: e.transpose(out, in_, ident), r, w)

        def act(out, in_, func, r=(), w=(), bias=None, scale=None, accum=None):
            kw = {}
            if bias is not None:
                kw["bias"] = bias
            if scale is not None:
                kw["scale"] = scale
            if accum is not None:
                kw["accum_out"] = accum
            P.op("act", lambda e: e.activation(out=out, in_=in_, func=func, **kw), r, w)

        def ts(eng, out, in0, s1, s2, op0, op1=None, r=(), w=(), accum=None):
            kw = {}
            if op1 is not None:
                kw["op1"] = op1
            if accum is not None:
                kw["accum_out"] = accum
            P.op(eng, lambda e: e.tensor_scalar(out=out, in0=in0, scalar1=s1, scalar2=s2, op0=op0, **kw), r, w)

        def tt(eng, out, in0, in1, op, r=(), w=()):
            P.op(eng, lambda e: e.tensor_tensor(out=out, in0=in0, in1=in1, op=op), r, w)

        def stt(out, in0, scalar, in1, op0, op1, r=(), w=()):
            P.op("dve", lambda e: e.scalar_tensor_tensor(out=out, in0=in0, scalar=scalar, in1=in1, op0=op0, op1=op1), r, w)

        def cp(eng, out, in_, r=(), w=()):
            if eng == "act":
                P.op("act", lambda e: e.activation(out=out, in_=in_, func=AF.Copy), r, w)
            else:
                P.op(eng, lambda e: e.tensor_copy(out=out, in_=in_), r, w)

        def red(out, in_, op, r=(), w=()):
            P.op("dve", lambda e: e.tensor_reduce(out=out, in_=in_, axis=AX.X, op=op), r, w)

        def recip(out, in_, r=(), w=()):
            P.op("dve", lambda e: e.reciprocal(out=out, in_=in_), r, w)

        def mset(eng, ap, val, r=(), w=()):
            P.op(eng, lambda e: e.memset(ap, val), r, w)

        def dma(q, out, in_, r=(), w=(), slow=False):
            if slow:
                P.op(q, lambda e: e.dma_start(out=out, in_=in_, allow_slow_non_contiguous=True), r, w, dma=True)
            else:
                P.op(q, lambda e: e.dma_start(out=out, in_=in_), r, w, dma=True)

        def rstd_from_ss(rs, ss, n, r, w):
            act(rs, ss, AF.Ln, r=r, w=w, scale=1.0 / n, bias=epsb[:, 0:1])
            act(rs, rs, AF.Exp, r=w, w=w, scale=-0.5)

        ident_f = sb("ident_f", [128, 128], F32)
        ident_b = sb("ident_b", [128, 128])
        J_b = sb("J_b", [128, 128])
        tri_f = sb("tri_f", [128, 128], F32)
        ones_b = sb("ones_b", [128, 128])
        epsb = sb("epsb", [128, 1], F32)
        pw2 = sb("pw2", [128, NBIS], F32)
        farb = sb("farb", [128, 12], F32)
        memhatT = sb("memhatT", [128, NSEQ, KC, MEM])
        B_const = Buf("const")
        B_memhat = Buf("memhat")


        with ExitStack() as ph:
            tmpf = sb("s0_tmpf", [128, 128], F32, ph)
            tbl = sb("s0_tbl", [34, 12], F32, ph)
            oh = sb("s0_oh", [34, NG], F32, ph)
            gv = sb("s0_gv", [12, NG], BF16, ph)
            ev = sb("s0_ev", [12, NG], BF16, ph)
            mt = sb("s0_mt", [128, D], F32, ph)
            mb = sb("s0_mb", [128, D], BF16, ph)
            sq = sb("s0_sq", [128, D], BF16, ph)
            ss = sb("s0_ss", [128, 1], F32, ph)
            rs = sb("s0_rs", [128, 1], F32, ph)
            ptr = ph.enter_context(nc.psum_tensor(uq("s0_ptr"), [128, 1024], BF16))
            ps0 = ph.enter_context(nc.psum_tensor(uq("s0_ps"), [128, 512], F32))
            Bps0 = Buf()
            Bt, Btbl, Boh, Bgv, Bmt, Bmb, Bss, Bptr = [Buf() for _ in range(8)]
            dma("sp", ident_f[:], c_ident.ap(), w=[B_const])
            cp("dve", ident_b[:], ident_f[:], r=[B_const], w=[B_const])
            dma("sp", tmpf[:], c_J.ap(), w=[Bt])
            cp("dve", J_b[:], tmpf[:], r=[Bt], w=[B_const])
            dma("sp", tri_f[:], c_tri.ap(), w=[B_const])
            dma("sp", farb[:], AP(rel_bias, 31 * 12, [[0, 128], [1, 12]]), w=[B_const])
            mset("dve", ones_b[:], 1.0, w=[B_const])
            mset("dve", epsb[:], EPS, w=[B_const])
            for k_ in range(NBIS):
                mset("dve", pw2[:, k_:k_ + 1], float(2.0 ** -(k_ + 1)), w=[B_const])
            dma("sp", tbl[0:32, :], rel_bias.ap(), w=[Btbl])
            dma("sp", tbl[32:34, :], c_sel.ap(), w=[Btbl])
            ts("dve", tbl[0:32, :], tbl[0:32, :], 8.0, None, ALU.mult, r=[Btbl], w=[Btbl])
            dma("sp", oh[:], c_onehot.ap(), w=[Boh])
            for c in range((NG + 511) // 512):
                n = min(512, NG - c * 512)
                mm(ps0[0:12, 0:n], tbl[:, :], oh[:, c * 512:c * 512 + n], True, True, r=[Btbl, Boh], w=[Bps0])
                cp("dve", gv[:, c * 512:c * 512 + n], ps0[0:12, 0:n], r=[Bps0], w=[Bgv])
                act(ev[:, c * 512:c * 512 + n], ps0[0:12, 0:n], AF.Exp, r=[Bps0], w=[Bgv], scale=0.125)
            B_gscr = Buf("gscr")
            dma("sp", GSCR.ap(), gv[:], r=[Bgv], w=[B_gscr])
            dma("sp", ESCR.ap(), ev[:], r=[Bgv], w=[B_gscr])
            for sq_i in range(NSEQ):
                for blk in range(MEM // 128):
                    dma("sp", mt[:], mem_in.ap()[sq_i * MEM + blk * 128: sq_i * MEM + (blk + 1) * 128, :], w=[Bmt])
                    act(sq[:], mt[:], AF.Square, r=[Bmt], w=[Bss], accum=ss[:, 0:1])
                    rstd_from_ss(rs[:, 0:1], ss[:, 0:1], D, r=[Bss], w=[Bss])
                    ts("dve", mb[:], mt[:], rs[:, 0:1], None, ALU.mult, r=[Bmt, Bss], w=[Bmb])
                    for kc in range(KC):
                        tr(ptr[:, kc * 128:(kc + 1) * 128], mb[:, kc * 128:(kc + 1) * 128], ident_b[:],
                           r=[Bmb, B_const], w=[Bptr])
                    cp("dve", memhatT[:, sq_i, :, blk * 128:(blk + 1) * 128],
                       ptr[:, :].rearrange("p (k t) -> p k t", k=KC), r=[Bptr], w=[B_memhat])
            P.flush()

        wl_cnt = [0]

        def load_weight(dst, src_t, src_off, nk, N, gain, stage, Bdst, row_stride=None):
            rs_ = N if row_stride is None else row_stride
            CH = 2048
            for kc in range(nk):
                for c0 in range(0, N, CH):
                    n = min(CH, N - c0)
                    st, Bst = stage.next()
                    dma("sp", st[:, 0:n], AP(src_t, src_off + kc * 128 * rs_ + c0, [[rs_, 128], [1, n]]), w=[Bst])
                    wl_cnt[0] += 1
                    on_act = (wl_cnt[0] % 2 == 0)
                    if gain is None:
                        cp("act" if on_act else "dve", dst[:, kc, c0:c0 + n], st[:, 0:n], r=[Bst], w=[Bdst])
                    elif on_act:
                        g, Bg = gain
                        act(dst[:, kc, c0:c0 + n], st[:, 0:n], AF.Identity, r=[Bst, Bg], w=[Bdst], scale=g[:, kc:kc + 1])
                    else:
                        g, Bg = gain
                        ts("dve", dst[:, kc, c0:c0 + n], st[:, 0:n], g[:, kc:kc + 1], None, ALU.mult,
                           r=[Bst, Bg], w=[Bdst])

        def load_gain(dst, src_t, off, n, Bd):
            dma("sp", dst[:, 0:n], AP(src_t, off, [[1, 128], [128, n]]), w=[Bd], slow=True)

        B_x = {}

        def xbuf(which, t):
            return B_x.setdefault((which, t), Buf())
        B_scr = {}

        def scr(name, *idx):
            return B_scr.setdefault((name,) + idx, Buf())

        for l in range(L):
            lam_init = 0.8 - 0.6 * math.exp(-0.3 * l)
            xsrc = (x_in, "x") if l == 0 else (XB, "xb")

            with ExitStack() as ph:
                wsb = sb("p1_w", [128, KC, NIN], BF16, ph)
                stage = Rot([sb("p1_st%d" % i, [128, 2048], F32, ph) for i in range(3)])
                gmix = sb("p1_g", [128, KC], F32, ph)
                xt = Rot([sb("p1_x%d" % i, [128, D], F32, ph) for i in range(3)])
                hb = Rot([sb("p1_hb%d" % i, [128, D], BF16, ph) for i in range(2)])
                sqj = sb("p1_sq", [128, D], BF16, ph)
                ssr = Rot([sb("p1_ss%d" % i, [128, 2], F32, ph) for i in range(3)])
                hT = Rot([sb("p1_hT%d" % i, [128, KC, 512], BF16, ph) for i in range(2)])
                stF = Rot([sb("p1_sf%d" % i, [128, 4, 512], BF16, ph) for i in range(3)])
                stT = Rot([sb("p1_stt%d" % i, [128, 512], BF16, ph) for i in range(3)])
                stW = Rot([sb("p1_sw%d" % i, [128, 8], F32, ph) for i in range(2)])
                ptr = ph.enter_context(nc.psum_tensor(uq("p1_ptr"), [128, 1024], BF16))
                psx = [ph.enter_context(nc.psum_tensor(uq("p1_ps%d" % i), [128, 512], F32)) for i in range(5)]
                Bpsx = [Buf() for _ in range(5)]
                Bptr, Bw, Bg, Bsq = Buf(), Buf(), Buf(), Buf()
                load_gain(gmix, norm_mix, l * D, KC, Bg)
                load_weight(wsb, w_in, l * D * NIN, KC, NIN, (gmix, Bg), stage, Bw)
                FG = [(0, 4, 128, QTA), (512, 4, 128, KTA), (1536, 2, 128, QTB), (1792, 2, 128, KTB),
                      (2304, 2, 128, QTC), (2560, 2, 128, KTC), (3072, 4, 128, IQT), (3584, 1, 64, IKT)]
                TG = [(1024, 512, VA), (2048, 256, VB), (2816, 256, VC)]
                pi = 0
                for st_i in range(NTOK // 512):
                    seq = (st_i * 512) // S
                    t0 = st_i * 512 - seq * S
                    hTt, BhT = hT.next()
                    for u in range(4):
                        row0 = st_i * 512 + u * 128
                        x_t, Bxt = xt.next()
                        h_b, Bhb = hb.next()
                        s_s, Bs = ssr.next()
                        dma("sp", x_t[:], xsrc[0].ap()[row0:row0 + 128, :], r=[xbuf(xsrc[1], row0 // 128)], w=[Bxt])
                        act(sqj[:], x_t[:], AF.Square, r=[Bxt], w=[Bs, Bsq], accum=s_s[:, 0:1])
                        rstd_from_ss(s_s[:, 1:2], s_s[:, 0:1], D, r=[Bs], w=[Bs])
                        ts("dve", h_b[:], x_t[:], s_s[:, 1:2], None, ALU.mult, r=[Bxt, Bs], w=[Bhb])
                        for kc in range(KC):
                            tr(ptr[:, kc * 128:(kc + 1) * 128], h_b[:, kc * 128:(kc + 1) * 128], ident_b[:],
                               r=[Bhb, B_const], w=[Bptr])
                        cp("act", hTt[:, :, u * 128:(u + 1) * 128], ptr[:, :].rearrange("p (k t) -> p k t", k=KC),
                           r=[Bptr], w=[BhT])
                        for (c0, ncol, dt_) in TG:
                            b = pi % 5; pi += 1
                            for kc in range(KC):
                                mm(psx[b][:, 0:ncol], hTt[:, kc, u * 128:(u + 1) * 128], wsb[:, kc, c0:c0 + ncol],
                                   kc == 0, kc == KC - 1, r=[BhT, Bw], w=[Bpsx[b]])
                            s_t, Bst_ = stT.next()
                            cp("dve", s_t[:, 0:ncol], psx[b][:, 0:ncol], r=[Bpsx[b]], w=[Bst_])
                            dma("pool", dt_.ap()[row0:row0 + 128, :], s_t[:, 0:ncol], r=[Bst_],
                                w=[scr(nm(dt_), seq, st_i)])
                        b = pi % 5; pi += 1
                        for kc in range(KC):
                            mm(psx[b][:, 0:8], hTt[:, kc, u * 128:(u + 1) * 128], wsb[:, kc, 3648:3656],
                               kc == 0, kc == KC - 1, r=[BhT, Bw], w=[Bpsx[b]])
                        s_w, Bsw = stW.next()
                        cp("dve", s_w[:, :], psx[b][:, 0:8], r=[Bpsx[b]], w=[Bsw])
                        dma("pool", IW.ap()[row0:row0 + 128, :], s_w[:, :], r=[Bsw], w=[scr("IW", seq, st_i)])
                    for gi, (c0, nblk, bs, dt_) in enumerate(FG):
                        s_f, Bsf = stF.next()
                        for bi in range(nblk):
                            b = pi % 5; pi += 1
                            cc = c0 + bi * bs
                            for kc in range(KC):
                                mm(psx[b][0:bs, :], wsb[:, kc, cc:cc + bs], hTt[:, kc, :], kc == 0, kc == KC - 1,
                                   r=[BhT, Bw], w=[Bpsx[b]])
                            cp("act" if (bi % 2 == 0) else "dve", s_f[0:bs, bi, :], psx[b][0:bs, :], r=[Bpsx[b]], w=[Bsf])
                        dst = AP(dt_, seq * S + t0, [[NSEQ * S, bs], [bs * NSEQ * S, nblk], [1, 512]])
                        dma("pool", dst, s_f[0:bs, 0:nblk, :], r=[Bsf], w=[scr(nm(dt_), seq, st_i)])
                P.flush()

            for seq in range(NSEQ):
                def scr_all(name):
                    return [scr(name, seq, st_i) for st_i in range(seq * NT, (seq + 1) * NT)]

                def att_common(ph, tag):
                    C = {}
                    C["strips"] = sb(tag + "strips", [128, 4, MS], BF16, ph)
                    C["Bstrips"] = Buf()
                    C["kT"] = [[sb(tag + "kT%d%d" % (s_, m), [128, S], BF16, ph) for m in range(2)] for s_ in range(2)]
                    C["qT"] = [sb(tag + "qT%d" % s_, [128, S], BF16, ph) for s_ in range(2)]
                    C["V"] = [sb(tag + "V%d" % s_, [128, NB, 130], BF16, ph) for s_ in range(2)]
                    C["oh"] = [sb(tag + "oh%d" % s_, [128, NB, 128], BF16, ph) for s_ in range(2)]
                    C["Bin"] = [Buf(), Buf()]
                    C["Boh"] = [Buf(), Buf()]
                    C["pT"] = Rot([sb(tag + "pT%d" % i, [128, 2, 512], BF16, ph) for i in range(4)])
                    C["pE"] = Rot([sb(tag + "pE%d" % i, [128, 2, 512], BF16, ph) for i in range(3)])
                    C["stmp"] = Rot([sb(tag + "stmp%d" % i, [128, MS], BF16, ph) for i in range(1)])
                    C["sm"] = Rot([sb(tag + "sm%d" % i, [128, 8], F32, ph) for i in range(16)])
                    C["t0"] = Rot([sb(tag + "t0%d" % i, [128, 128], F32, ph) for i in range(4)])
                    C["o"] = Rot([sb(tag + "o%d" % i, [128, 128], F32, ph) for i in range(4)])
                    C["sqj4"] = [sb(tag + "sqj4%d" % i, [128, 128], BF16, ph) for i in range(4)]
                    C["Bsqj4"] = [Buf() for _ in range(4)]
                    C["sqj"] = sb(tag + "sqj", [128, 128], BF16, ph)
                    C["Bsqj"] = Buf()
                    C["accb"] = [ph.enter_context(nc.psum_tensor(uq(tag + "acc%d" % k), [128, 512], F32)) for k in range(4)]
                    C["Baccb"] = [Buf() for _ in range(4)]
                    C["sc2"] = Rot([ph.enter_context(nc.psum_tensor(uq(tag + "sc%d" % i), [128, 2, 512], F32)) for i in range(2)])
                    C["set"] = 0
                    for s_ in range(2):
                        mset("pool", C["kT"][s_][0][64:128, :], 0.0, w=[C["Bin"][s_]])
                        mset("pool", C["kT"][s_][1][0:64, :], 0.0, w=[C["Bin"][s_]])
                    return C

                def load_strips(C, h0):
                    for hh in range(4):
                        tmp, Btmp = C["stmp"].next()
                        dma("sp", tmp[:, :], AP(ESCR, (h0 + hh) * NG, [[1, 128], [1, MS]]), r=[B_gscr], w=[Btmp])
                        for c in range(MS // 512):
                            ps, Bps = C["sc2"].next()
                            mm(ps[:, 0, :], J_b[:], tmp[:, c * 512:(c + 1) * 512], True, True, r=[Btmp, B_const], w=[Bps])
                            cp("dve" if c % 2 else "act", C["strips"][:, hh, c * 512:(c + 1) * 512], ps[:, 0, :],
                               r=[Bps], w=[C["Bstrips"]])

                def att_pair(C, kind, qt, kt, row0, vloads, sidx, scale, jmin_fn, evac, ocol, dst_cols,
                             mask=None, hidx=None):
                    s_ = C["set"]; C["set"] ^= 1
                    Bin = C["Bin"][s_]
                    kT0, kT1 = C["kT"][s_]
                    qT = C["qT"][s_]
                    V = C["V"][s_]
                    oh_t = C["oh"][s_]; Boh = C["Boh"][s_]
                    rq = scr_all(nm(qt)); rk = scr_all(nm(kt))
                    dma("sp", qT[:, :], AP(qt, row0 * NSEQ * S + seq * S, [[NSEQ * S, 128], [1, S]]), r=rq, w=[Bin])
                    dma("sp", kT0[0:64, :], AP(kt, row0 * NSEQ * S + seq * S, [[NSEQ * S, 64], [1, S]]), r=rk, w=[Bin])
                    dma("sp", kT1[64:128, :], AP(kt, (row0 + 64) * NSEQ * S + seq * S, [[NSEQ * S, 64], [1, S]]), r=rk, w=[Bin])
                    for (vt, vc0, vn, dcol) in vloads:
                        W_ = {"VA": 512, "VB": 256, "VC": 256}[nm(vt)]
                        dma("sp", V[:, :, dcol:dcol + vn],
                            AP(vt, seq * S * W_ + vc0, [[W_, 128], [128 * W_, NB], [1, vn]]), r=scr_all(nm(vt)), w=[Bin])
                        mset("pool", V[:, :, dcol + vn:dcol + vn + 1], 1.0, w=[Bin])
                    kTs = (kT0, kT1)
                    nv, vcol = (129, (0, 0)) if kind == "A" else (65, (0, 65))
                    def accv(i_, m_, u, w_):
                        if kind == "A":
                            k_ = 2 * m_ + u // 2
                            return C["accb"][k_][:, (u % 2) * 256:(u % 2) * 256 + w_], C["Baccb"][k_], (u % 2 == 0)
                        k_ = 2 * (i_ % 2) + m_
                        return C["accb"][k_][:, u * 128:u * 128 + w_], C["Baccb"][k_], (u == 0)

                    def pv(job):
                        (i_, j_, c0_, j0_, pT_, BpT_, _last) = job
                        for m_ in range(2):
                            for u in range(c0_ // 128, 4):
                                a_, Ba_, first = accv(i_, m_, u, nv)
                                mm(a_, pT_[:, m_, u * 128:(u + 1) * 128],
                                   V[:, j_, vcol[m_]:vcol[m_] + nv], (j_ == j0_ and first), (j_ == 4 * i_ + u),
                                   r=[BpT_, Bin], w=[Ba_])
                    C["accv"] = accv

                    pending = []

                    due = []

                    def drain(limit):
                        while len(pending) > limit:
                            job = pending.pop(0)
                            pv(job)
                            for d_ in due:
                                d_[1] -= 1
                            while due and due[0][1] <= 0:
                                evac(C, due.pop(0)[0], oh_t, Boh)
                            if job[6]:
                                if kind == "A":
                                    evac(C, job[0], oh_t, Boh)
                                else:
                                    due.append([job[0], 2])

                    for i in range(NT):
                        if mask is not None:
                            mk, Bmk = mask(i)
                        j0 = jmin_fn(i)
                        for j in range(j0, 4 * i + 4):
                            c0 = max(0, j - 4 * i) * 128
                            D0 = 512 * i - 128 * j
                            off = min(D0, DFAR) + 384
                            far = (hidx is not None) and D0 >= 1664
                            ps, Bps = C["sc2"].next()
                            for m in range(2):
                                mm(ps[:, m, c0:512], kTs[m][:, j * 128:(j + 1) * 128], qT[:, i * 512 + c0:(i + 1) * 512],
                                   True, mask is None, r=[Bin], w=[Bps])
                                if mask is not None:
                                    for u in range(c0 // 128, 4):
                                        mm(ps[:, m, u * 128:(u + 1) * 128], mk[:, u, j * 128:(j + 1) * 128], ident_b[:],
                                           False, u == 3, r=[Bmk, B_const], w=[Bps])
                            pT, BpT = C["pT"].next()
                            if far and hidx[0] == hidx[1]:
                                act(pT[:, :, c0:512], ps[:, :, c0:512], AF.Exp, r=[Bps, B_const], w=[BpT], scale=scale,
                                    bias=farb[:, hidx[0]:hidx[0] + 1])
                            elif far:
                                for m in range(2):
                                    act(pT[:, m, c0:512], ps[:, m, c0:512], AF.Exp, r=[Bps, B_const], w=[BpT], scale=scale,
                                        bias=farb[:, hidx[m]:hidx[m] + 1])
                            else:
                                pR, BpR = C["pE"].next()
                                act(pR[:, :, c0:512], ps[:, :, c0:512], AF.Exp, r=[Bps], w=[BpR], scale=scale)
                                for m in range(2):
                                    tt("dve", pT[:, m, c0:512], pR[:, m, c0:512],
                                       C["strips"][:, sidx[m], off + c0:off + 512], ALU.mult,
                                       r=[BpR, C["Bstrips"]], w=[BpT])
                            pending.append((i, j, c0, j0, pT, BpT, j == 4 * i + 3))
                            drain(2)
                    drain(0)
                    while due:
                        evac(C, due.pop(0)[0], oh_t, Boh)
                    W_ = D
                    dma("pool", AP(OS, seq * S * W_ + ocol, [[W_, 128], [128 * W_, NB], [1, dst_cols]]),
                        oh_t[:, :, 0:dst_cols], r=[Boh], w=[scr("OS", seq, ocol)])

                def evac_bc(C, i, oh_t, Boh):
                    U = []
                    for m in range(2):
                        for u in range(4):
                            sm, Bsm = C["sm"].next()
                            a_, Ba_, _ = C["accv"](i, m, u, 65)
                            U.append((m, u, sm, Bsm, a_, Ba_))
                    for (m, u, sm, Bsm, a, Ba_) in U:
                        recip(sm[:, 0:1], a[:, 64:65], r=[Ba_], w=[Bsm])
                    for (m, u, sm, Bsm, a, Ba_) in U:
                        ts("dve", oh_t[:, 4 * i + u, m * 64:(m + 1) * 64], a[:, 0:64], sm[:, 0:1], None,
                           ALU.mult, r=[Ba_, Bsm], w=[Boh])

                with ExitStack() as ph:
                    C = att_common(ph, "ab_")
                    lamv = sb("ab_lam", [128, 4, 64], F32, ph)
                    lsm = sb("ab_lsm", [128, 8], F32, ph)
                    prod = sb("ab_prod", [128, 64], F32, ph)
                    gs = sb("ab_gs", [128, 128], F32, ph)
                    Blam, Bgs = Buf(), Buf()
                    for qi, t_ in enumerate((lam_q1, lam_k1, lam_q2, lam_k2)):
                        dma("sp", lamv[:, qi, :], AP(t_, l * 64, [[0, 128], [1, 64]]), w=[Blam])
                    for qi in range(2):
                        tt("dve", prod[:], lamv[:, 2 * qi, :], lamv[:, 2 * qi + 1, :], ALU.mult, r=[Blam], w=[Blam])
                        red(lsm[:, qi:qi + 1], prod[:], ALU.add, r=[Blam], w=[Blam])
                        act(lsm[:, qi:qi + 1], lsm[:, qi:qi + 1], AF.Exp, r=[Blam], w=[Blam])
                    tt("dve", lsm[:, 2:3], lsm[:, 1:2], lsm[:, 0:1], ALU.subtract, r=[Blam], w=[Blam])
                    ts("dve", lsm[:, 2:3], lsm[:, 2:3], -lam_init, None, ALU.add, r=[Blam], w=[Blam])
                    dma("sp", gs[:], AP(subln, l * 128, [[0, 128], [1, 128]]), w=[Bgs])
                    ts("dve", gs[:], gs[:], 1.0 - lam_init, None, ALU.mult, r=[Bgs], w=[Bgs])
                    neglam = lsm[:, 2:3]

                    def evac_a(C, i, oh_t, Boh):
                        U = []
                        for u in range(4):
                            sm, Bsm = C["sm"].next()
                            t0, Bt0 = C["t0"].next()
                            o_, Bo = C["o"].next()
                            a0_, B0_, _ = C["accv"](i, 0, u, 129)
                            a1_, B1_, _ = C["accv"](i, 1, u, 129)
                            U.append((u, sm, Bsm, t0, Bt0, o_, Bo, a0_, a1_, B0_, B1_))
                        for (u, sm, Bsm, t0, Bt0, o_, Bo, a0, a1, B0, B1) in U:
                            recip(sm[:, 0:1], a0[:, 128:129], r=[B0], w=[Bsm])
                        for (u, sm, Bsm, t0, Bt0, o_, Bo, a0, a1, B0, B1) in U:
                            recip(sm[:, 1:2], a1[:, 128:129], r=[B1], w=[Bsm])
                        for (u, sm, Bsm, t0, Bt0, o_, Bo, a0, a1, B0, B1) in U:
                            tt("dve", sm[:, 1:2], sm[:, 1:2], neglam, ALU.mult, r=[Blam], w=[Bsm])
                        for (u, sm, Bsm, t0, Bt0, o_, Bo, a0, a1, B0, B1) in U:
                            act(t0[:], a0[:, 0:128], AF.Identity, r=[B0, Bsm], w=[Bt0], scale=sm[:, 0:1])
                        for (u, sm, Bsm, t0, Bt0, o_, Bo, a0, a1, B0, B1) in U:
                            stt(o_[:], a1[:, 0:128], sm[:, 1:2], t0[:], ALU.mult, ALU.add, r=[B1, Bsm, Bt0], w=[Bo])
                        for (u, sm, Bsm, t0, Bt0, o_, Bo, a0, a1, B0, B1) in U:
                            act(C["sqj4"][u][:], o_[:], AF.Square, r=[Bo], w=[C["Bsqj4"][u], Bsm], accum=sm[:, 2:3])
                        for (u, sm, Bsm, t0, Bt0, o_, Bo, a0, a1, B0, B1) in U:
                            act(sm[:, 3:4], sm[:, 2:3], AF.Ln, r=[Bsm], w=[Bsm], scale=1.0 / 128, bias=epsb[:, 0:1])
                        for (u, sm, Bsm, t0, Bt0, o_, Bo, a0, a1, B0, B1) in U:
                            act(sm[:, 3:4], sm[:, 3:4], AF.Exp, r=[Bsm], w=[Bsm], scale=-0.5)
                        for (u, sm, Bsm, t0, Bt0, o_, Bo, a0, a1, B0, B1) in U:
                            stt(oh_t[:, 4 * i + u, :], o_[:], sm[:, 3:4], gs[:], ALU.mult, ALU.mult,
                                r=[Bo, Bsm, Bgs], w=[Boh])

                    load_strips(C, 0)
                    for h in range(4):
                        att_pair(C, "A", QTA, KTA, h * 128, [(VA, h * 128, 128, 0)], (h, h), 0.125,
                                 lambda i: 0, evac_a, h * 128, 128, hidx=(h, h))
                    load_strips(C, 4)
                    for pr in range(2):
                        att_pair(C, "B", QTB, KTB, pr * 128,
                                 [(VB, pr * 128, 64, 0), (VB, pr * 128 + 64, 64, 65)], (2 * pr, 2 * pr + 1), 0.125,
                                 lambda i: max(0, 4 * i - 17), evac_bc, 512 + pr * 128, 128)
                    P.flush()

                with ExitStack() as ph:
                    iqT = sb("ix_iqT", [128, 4, S], BF16, ph)
                    ik = [sb("ix_ik%d" % m, [128, S], BF16, ph) for m in range(2)]
                    iw_sb = sb("ix_iw", [128, NB, 8], F32, ph)
                    score = Rot([sb("ix_sc%d" % i, [128, S], F32, ph) for i in range(6)])
                    junk = [sb("ix_junk%d" % i, [128, S], mybir.dt.int8, ph) for i in range(3)]
                    Bjunk = [Buf(), Buf(), Buf()]
                    rt = Rot([sb("ix_r%d" % i, [128, 2, 512], BF16, ph) for i in range(3)])
                    dgw = Rot([sb("ix_dg%d" % i, [128, 8, 128], BF16, ph) for i in range(2)])
                    mo = Rot([sb("ix_mo%d" % i, [128, S], BF16, ph) for i in range(2)])
                    smr = Rot([sb("ix_sm%d" % i, [128, 8 + 2 * NBIS], F32, ph) for i in range(6)])
                    lg = Rot([ph.enter_context(nc.psum_tensor(uq("ix_lg%d" % i), [128, 2, 512], F32)) for i in range(2)])
                    scp = Rot([ph.enter_context(nc.psum_tensor(uq("ix_scp%d" % i), [128, 512], F32)) for i in range(2)])
                    Bin = Buf()
                    dma("sp", iqT[:, :, :], AP(IQT, seq * S, [[NSEQ * S, 128], [128 * NSEQ * S, 4], [1, S]]),
                        r=scr_all("IQT"), w=[Bin])
                    mset("pool", ik[0][64:128, :], 0.0, w=[Bin])
                    mset("pool", ik[1][0:64, :], 0.0, w=[Bin])
                    dma("sp", ik[0][0:64, :], AP(IKT, seq * S, [[NSEQ * S, 64], [1, S]]), r=scr_all("IKT"), w=[Bin])
                    dma("sp", ik[1][64:128, :], AP(IKT, seq * S, [[NSEQ * S, 64], [1, S]]), r=scr_all("IKT"), w=[Bin])
                    dma("sp", iw_sb[:, :, :], AP(IW, seq * S * 8, [[8, 128], [128 * 8, NB], [1, 8]]),
                        r=scr_all("IW"), w=[Bin])
                    def scores(b, res):
                        dg, Bdg = dgw.next()
                        for hh in range(8):
                            ts("pool", dg[:, hh, :], ident_b[:], iw_sb[:, b, hh:hh + 1], 0.0, ALU.mult, ALU.add,
                               r=[Bin, B_const], w=[Bdg])
                        sc_t, Bsc = score.next()
                        res.append((sc_t, Bsc))
                        pend = None
                        cur = {}

                        def fin(p):
                            (c_, hp_, ncol_, r__, Br_) = p
                            if hp_ == 0:
                                cur["sp"] = scp.next()
                            sp_, Bsp = cur["sp"]
                            for k_ in range(2):
                                hh_ = 2 * hp_ + k_
                                mm(sp_[:, 0:ncol_], dg[:, hh_, :], r__[:, k_, 0:ncol_], hh_ == 0, hh_ == 7, r=[Bdg, Br_], w=[Bsp])
                            if hp_ == 3:
                                cp("act", sc_t[:, c_ * 512:c_ * 512 + ncol_], sp_[:, 0:ncol_], r=[Bsp], w=[Bsc])
                        for c in range(b // 4 + 1):
                            ncol = 512 if c < b // 4 else ((b % 4) + 1) * 128
                            for hp in range(4):
                                lg_, Blg = lg.next()
                                for k_ in range(2):
                                    mm(lg_[:, k_, 0:ncol], iqT[:, hp, b * 128:(b + 1) * 128],
                                       ik[k_][:, c * 512:c * 512 + ncol], True, True, r=[Bin], w=[Blg])
                                r_, Br = rt.next()
                                act(r_[:, :, 0:ncol], lg_[:, :, 0:ncol], AF.Relu, r=[Blg], w=[Br])
                                if pend is not None:
                                    fin(pend)
                                pend = (c, hp, ncol, r_, Br)
                                yield
                        fin(pend)

                    SN = 8 + NBIS

                    def bisect(b, sc_t, Bsc, jk, Bjk, on_act):
                        Sc = (b + 1) * 128
                        sm, Bsm = smr.next()
                        red(sm[:, 1:2], sc_t[:, 0:Sc], ALU.max, r=[Bsc], w=[Bsm]); yield
                        red(sm[:, 0:1], sc_t[:, 0:Sc], ALU.min, r=[Bsc], w=[Bsm]); yield
                        tt("dve", sm[:, 1:2], sm[:, 1:2], sm[:, 0:1], ALU.subtract, r=[Bsm], w=[Bsm]); yield
                        tt("dve", sc_t[:, b * 128:Sc], sc_t[:, b * 128:Sc], tri_f[:], ALU.add, r=[B_const], w=[Bsc]); yield
                        ts("dve", sm[:, 8:8 + NBIS], pw2[:, 0:NBIS], sm[:, 1:2], None, ALU.mult, r=[Bsm, B_const], w=[Bsm]); yield
                        if not on_act:
                            tt("dve", sm[:, 2:3], sm[:, 0:1], sm[:, 8:9], ALU.add, r=[Bsm], w=[Bsm]); yield
                            for it in range(NBIS):
                                ts("dve", jk[:, 0:Sc], sc_t[:, 0:Sc], sm[:, 2:3], None, ALU.is_ge, ALU.add,
                                   r=[Bsc, Bsm], w=[Bjk, Bsm], accum=sm[:, 3:4]); yield
                                ts("dve", sm[:, 4:5], sm[:, 3:4], float(TOPK) - 0.5, -0.5, ALU.is_ge, ALU.add, r=[Bsm], w=[Bsm]); yield
                                stt(sm[:, 2:3], sm[:, 4:5], sm[:, 8 + it:9 + it], sm[:, 2:3], ALU.mult, ALU.add, r=[Bsm], w=[Bsm]); yield
                            stt(sm[:, 0:1], sm[:, 8 + NBIS - 1:8 + NBIS], -1.0, sm[:, 2:3], ALU.mult, ALU.add, r=[Bsm], w=[Bsm]); yield
                        else:
                            ts("dve", sm[:, SN:SN + NBIS], sm[:, 8:8 + NBIS], -1.0, None, ALU.mult, r=[Bsm], w=[Bsm]); yield
                            stt(sm[:, 2:3], sm[:, 0:1], -1.0, sm[:, 8:9], ALU.mult, ALU.subtract, r=[Bsm], w=[Bsm]); yield
                            for it in range(NBIS):
                                act(jk[:, 0:Sc], sc_t[:, 0:Sc], AF.Sign, r=[Bsc, Bsm], w=[Bjk, Bsm], bias=sm[:, 2:3],
                                    scale=1.0, accum=sm[:, 3:4]); yield; yield
                                ts("pool", sm[:, 4:5], sm[:, 3:4], float(2 * TOPK - Sc) - 0.5, -0.5, ALU.is_ge, ALU.add,
                                   r=[Bsm], w=[Bsm]); yield
                                ts("pool", sm[:, 2:3], sm[:, 4:5], sm[:, SN + it:SN + it + 1], sm[:, 2:3], ALU.mult, ALU.add,
                                   r=[Bsm], w=[Bsm]); yield
                            stt(sm[:, 0:1], sm[:, 2:3], -1.0, sm[:, 8 + NBIS - 1:8 + NBIS], ALU.mult, ALU.subtract, r=[Bsm], w=[Bsm]); yield
                        mo_t, Bmo = mo.next()
                        ts("dve", mo_t[:, 0:Sc], sc_t[:, 0:Sc], sm[:, 0:1], NEGM, ALU.is_lt, ALU.mult, r=[Bsc, Bsm], w=[Bmo]); yield
                        dma("sp", MASK.ap()[b * 128:(b + 1) * 128, 0:Sc], mo_t[:, 0:Sc], r=[Bmo], w=[scr("MASK", b)])

                    def run_rr(gens):
                        alive = list(gens)
                        while alive:
                            for g_ in list(alive):
                                try:
                                    next(g_)
                                except StopIteration:
                                    alive.remove(g_)

                    def chain(gl):
                        for g_ in gl:
                            yield from g_

                    groups = [list(range(b0, min(b0 + 3, NB))) for b0 in range(0, NB, 3)]
                    prev = []
                    for grp in groups:
                        res = []
                        run_rr([chain([scores(b_, res) for b_ in grp])] + prev)
                        prev = []
                        for gi, b_ in enumerate(grp):
                            on_act = (gi == 2)
                            prev.append(bisect(b_, res[gi][0], res[gi][1], junk[gi], Bjunk[gi], on_act))
                    run_rr(prev)
                    P.flush()

                with ExitStack() as ph:
                    C = att_common(ph, "c_")
                    mkr = Rot([sb("c_mk%d" % i, [128, 4, S], BF16, ph) for i in range(2)])

                    def load_mask(i):
                        mk, Bmk = mkr.next()
                        for u in range(4):
                            b = 4 * i + u
                            dma("sp", mk[:, u, 0:(b + 1) * 128], MASK.ap()[b * 128:(b + 1) * 128, 0:(b + 1) * 128],
                                r=[scr("MASK", b)], w=[Bmk])
                        return mk, Bmk
                    load_strips(C, 8)
                    for pr in range(2):
                        att_pair(C, "C", QTC, KTC, pr * 128,
                                 [(VC, pr * 128, 64, 0), (VC, pr * 128 + 64, 64, 65)], (2 * pr, 2 * pr + 1), 0.125,
                                 lambda i: 0, evac_bc, 768 + pr * 128, 128, mask=load_mask, hidx=(8 + 2 * pr, 9 + 2 * pr))
                    P.flush()

            with ExitStack() as ph:
                KTm = sb("d1_KTm", [128, NSEQ, 4, 2, MEM], BF16, ph)
                Vm = sb("d1_Vm", [128, NSEQ, 2, 4, 256], BF16, ph)
                Bkv = Buf()
                NPS = 5
                psx = [ph.enter_context(nc.psum_tensor(uq("d1_ps%d" % i), [128, 512], F32)) for i in range(NPS)]
                Bpsx = [Buf() for _ in range(NPS)]
                ptrs = [(ph.enter_context(nc.psum_tensor(uq("d1_ptr%d" % i), [128, 1024], BF16)), Buf()) for i in range(2)]
                lps = ph.enter_context(nc.psum_tensor(uq("d1_lps"), [128, 16], F32))
                Blps = Buf()
                pi = 0
                with ExitStack() as ph2:
                    wkv = sb("mk_w", [128, KC, 2 * D], BF16, ph2)
                    stage = Rot([sb("mk_st%d" % i, [128, 2048], F32, ph2) for i in range(3)])
                    gkv = sb("mk_g", [128, KC], F32, ph2)
                    Bw, Bg = Buf(), Buf()
                    load_gain(gkv, norm_memkv, l * D, KC, Bg)
                    load_weight(wkv, w_mkv, l * D * 2 * D, KC, 2 * D, (gkv, Bg), stage, Bw)
                    for sq_i in range(NSEQ):
                        for h in range(4):
                            for dc in range(2):
                                n0 = h * 256 + dc * 128
                                b = pi % NPS; pi += 1
                                for kc in range(KC):
                                    mm(psx[b][:, 0:MEM], wkv[:, kc, n0:n0 + 128], memhatT[:, sq_i, kc, :],
                                       kc == 0, kc == KC - 1, r=[Bw, B_memhat], w=[Bpsx[b]])
                                cp("act", KTm[:, sq_i, h, dc, :], psx[b][:, 0:MEM], r=[Bpsx[b]], w=[Bkv])
                        for blk in range(2):
                            for nch in range(2):
                                b = pi % NPS; pi += 1
                                for kc in range(KC):
                                    mm(psx[b][:, :], memhatT[:, sq_i, kc, blk * 128:(blk + 1) * 128],
                                       wkv[:, kc, D + nch * 512:D + (nch + 1) * 512], kc == 0, kc == KC - 1,
                                       r=[Bw, B_memhat], w=[Bpsx[b]])
                                cp("dve", Vm[:, sq_i, blk, 2 * nch:2 * nch + 2, :],
                                   psx[b][:, :].rearrange("p (h d) -> p h d", h=2), r=[Bpsx[b]], w=[Bkv])
                    P.flush()
                wo = sb("d1_wo", [128, KC, D], BF16, ph)
                wq = sb("d1_wq", [128, KC, D], BF16, ph)
                wm = sb("d1_wm", [128, KC, D], BF16, ph)
                gq = sb("d1_gq", [128, KC], F32, ph)
                stage = Rot([sb("d1_st%d" % i, [128, 2048], F32, ph) for i in range(3)])
                xt = Rot([sb("d1_x%d" % i, [128, D], F32, ph) for i in range(6)])
                ob = Rot([sb("d1_ob%d" % i, [128, D], BF16, ph) for i in range(2)])
                hb = Rot([sb("d1_hb%d" % i, [128, D], BF16, ph) for i in range(4)])
                sqr = Rot([sb("d1_sq%d" % i, [128, D], BF16, ph) for i in range(2)])
                ssr = Rot([sb("d1_ss%d" % i, [128, 2], F32, ph) for i in range(8)])
                oT = sb("d1_oT", [128, KC, 512], BF16, ph); BoT = Buf()
                hT = sb("d1_hT", [128, KC, 512], BF16, ph); BhT = Buf()
                qmT = sb("d1_qmT", [128, 8, 512], BF16, ph); BqmT = Buf()
                pTm = Rot([sb("d1_pT%d" % i, [128, 2, 512], BF16, ph) for i in range(4)])
                omT = sb("d1_omT", [128, 4, 2, 512], BF16, ph); BomT = Buf()
                rl = sb("d1_rl", [128, 16], F32, ph); Brl = Buf()
                Bwo, Bwq, Bwm, Bgq = Buf(), Buf(), Buf(), Buf()
                load_gain(gq, norm_mem, l * D, KC, Bgq)
                load_weight(wo, w_out, l * D * D, KC, D, None, stage, Bwo)
                load_weight(wq, w_mq, l * D * D, KC, D, (gq, Bgq), stage, Bwq)
                load_weight(wm, w_mo, l * D * D, KC, D, None, stage, Bwm)
                for st_i in range(NTOK // 512):
                    seq = (st_i * 512) // S
                    xs = []
                    for u in range(4):
                        row0 = st_i * 512 + u * 128
                        x_t, Bxt = xt.next()
                        o_b, Bob = ob.next()
                        xs.append((x_t, Bxt))
                        dma("sp", x_t[:], xsrc[0].ap()[row0:row0 + 128, :], r=[xbuf(xsrc[1], row0 // 128)], w=[Bxt])
                        dma("sp", o_b[:], OS.ap()[row0:row0 + 128, :],
                            r=[scr("OS", seq, oc) for oc in (0, 128, 256, 384, 512, 640, 768, 896)], w=[Bob])
                        pt_, Bpt_ = ptrs[u % 2]
                        for kc in range(KC):
                            tr(pt_[:, kc * 128:(kc + 1) * 128], o_b[:, kc * 128:(kc + 1) * 128], ident_b[:],
                               r=[Bob, B_const], w=[Bpt_])
                        cp("act", oT[:, :, u * 128:(u + 1) * 128], pt_[:, :].rearrange("p (k t) -> p k t", k=KC),
                           r=[Bpt_], w=[BoT])
                    for u in range(4):
                        x_t, Bxt = xs[u]
                        for nch in range(2):
                            b = pi % NPS; pi += 1
                            for kc in range(KC):
                                mm(psx[b][:, :], oT[:, kc, u * 128:(u + 1) * 128], wo[:, kc, nch * 512:(nch + 1) * 512],
                                   kc == 0, kc == KC - 1, r=[BoT, Bwo], w=[Bpsx[b]])
                            tt("dve", x_t[:, nch * 512:(nch + 1) * 512], psx[b][:, :], x_t[:, nch * 512:(nch + 1) * 512],
                               ALU.add, r=[Bpsx[b]], w=[Bxt])
                    hbs = [hb.next() for _ in range(4)]
                    sss = [ssr.next() for _ in range(4)]
                    for u in range(4):
                        sq_t, Bsq_ = sqr.next()
                        act(sq_t[:], xs[u][0][:], AF.Square, r=[xs[u][1]], w=[sss[u][1], Bsq_], accum=sss[u][0][:, 0:1])
                    for u in range(4):
                        act(sss[u][0][:, 1:2], sss[u][0][:, 0:1], AF.Ln, r=[sss[u][1]], w=[sss[u][1]], scale=1.0 / D,
                            bias=epsb[:, 0:1])
                    for u in range(4):
                        act(sss[u][0][:, 1:2], sss[u][0][:, 1:2], AF.Exp, r=[sss[u][1]], w=[sss[u][1]], scale=-0.5)
                    for u in range(4):
                        ts("dve", hbs[u][0][:], xs[u][0][:], sss[u][0][:, 1:2], None, ALU.mult,
                           r=[xs[u][1], sss[u][1]], w=[hbs[u][1]])
                    for u in range(4):
                        h_b, Bhb = hbs[u]
                        pt_, Bpt_ = ptrs[u % 2]
                        for kc in range(KC):
                            tr(pt_[:, kc * 128:(kc + 1) * 128], h_b[:, kc * 128:(kc + 1) * 128], ident_b[:],
                               r=[Bhb, B_const], w=[Bpt_])
                        cp("act", hT[:, :, u * 128:(u + 1) * 128], pt_[:, :].rearrange("p (k t) -> p k t", k=KC),
                           r=[Bpt_], w=[BhT])
                    for nb in range(8):
                        b = pi % NPS; pi += 1
                        for kc in range(KC):
                            mm(psx[b][:, :], wq[:, kc, nb * 128:(nb + 1) * 128], hT[:, kc, :], kc == 0, kc == KC - 1,
                               r=[BhT, Bwq], w=[Bpsx[b]])
                        cp("act" if nb % 2 else "dve", qmT[:, nb, :], psx[b][:, :], r=[Bpsx[b]], w=[BqmT])
                    pts = [pTm.next() for _ in range(4)]
                    for h in range(4):
                        p_t, Bp = pts[h]
                        for blk in range(2):
                            b = pi % NPS; pi += 1
                            for dc in range(2):
                                mm(psx[b][:, :], KTm[:, seq, h, dc, blk * 128:(blk + 1) * 128], qmT[:, 2 * h + dc, :],
                                   dc == 0, dc == 1, r=[Bkv, BqmT], w=[Bpsx[b]])
                            act(p_t[:, blk, :], psx[b][:, :], AF.Exp, r=[Bpsx[b]], w=[Bp], scale=1.0 / 16.0)
                    for h in range(4):
                        p_t, Bp = pts[h]
                        for u in range(4):
                            for blk in range(2):
                                mm(lps[:, u * 4 + h:u * 4 + h + 1], p_t[:, blk, u * 128:(u + 1) * 128], ones_b[:, 0:1],
                                   blk == 0, blk == 1, r=[Bp, B_const], w=[Blps])
                        for dc in range(2):
                            b = pi % NPS; pi += 1
                            for blk in range(2):
                                mm(psx[b][:, :], Vm[:, seq, blk, h, dc * 128:(dc + 1) * 128], p_t[:, blk, :],
                                   blk == 0, blk == 1, r=[Bkv, Bp], w=[Bpsx[b]])
                            cp("act" if dc else "dve", omT[:, h, dc, :], psx[b][:, :], r=[Bpsx[b]], w=[BomT])
                    recip(rl[:, :], lps[:, :], r=[Blps], w=[Brl])
                    for u in range(4):
                        x_t, Bxt = xs[u]
                        row0 = st_i * 512 + u * 128
                        for h in range(4):
                            for nch in range(2):
                                b = pi % NPS; pi += 1
                                for dc in range(2):
                                    mm(psx[b][:, :], omT[:, h, dc, u * 128:(u + 1) * 128],
                                       wm[:, 2 * h + dc, nch * 512:(nch + 1) * 512], dc == 0, dc == 1,
                                       r=[BomT, Bwm], w=[Bpsx[b]])
                                stt(x_t[:, nch * 512:(nch + 1) * 512], psx[b][:, :], rl[:, u * 4 + h:u * 4 + h + 1],
                                    x_t[:, nch * 512:(nch + 1) * 512], ALU.mult, ALU.add, r=[Bpsx[b], Brl], w=[Bxt])
                        dma("pool", XA.ap()[row0:row0 + 128, :], x_t[:], r=[Bxt], w=[xbuf("xa", row0 // 128)])
                P.flush()

            with ExitStack() as ph:
                T2 = 256
                wu = sb("d2_wu", [128, KC, 2 * FF], BF16, ph)
                wd = sb("d2_wd", [128, NFB, D], BF16, ph)
                gf = sb("d2_gf", [128, KC], F32, ph)
                cw = sb("d2_cw", [128, 3, 44], F32, ph)
                cb = sb("d2_cb", [128, 44], F32, ph)
                Bwu, Bwd, Bgf, Bcw = Buf(), Buf(), Buf(), Buf()
                with ExitStack() as ph2:
                    stage = Rot([sb("d2_st%d" % i, [128, 2048], F32, ph2) for i in range(3)])
                    load_gain(gf, norm_ffn, l * D, KC, Bgf)
                    for j in range(3):
                        load_gain(cw[:, j, :], conv_w, (l * 3 + j) * 2 * FF, 44, Bcw)
                    load_gain(cb, conv_b, l * 2 * FF, 44, Bcw)
                    load_weight(wu, w_up, l * D * 2 * FF, KC, 2 * FF, (gf, Bgf), stage, Bwu)
                    load_weight(wd, w_down, l * FF * D, NFB, D, None, stage, Bwd)
                    P.flush()
                xt = Rot([sb("d2_x%d" % i, [128, D], F32, ph) for i in range(4)])
                hb = Rot([sb("d2_hb%d" % i, [128, D], BF16, ph) for i in range(2)])
                sqj = sb("d2_sq", [128, D], BF16, ph); Bsq = Buf()
                ssr = Rot([sb("d2_ss%d" % i, [128, 2], F32, ph) for i in range(3)])
                hT = Rot([sb("d2_hT%d" % i, [128, KC, T2], BF16, ph) for i in range(2)])
                ubuf = Rot([sb("d2_u%d" % i, [128, T2 + 2], F32, ph) for i in range(5)])
                ubuf.t = [(t_, (Buf(), Buf())) for (t_, _) in ubuf.t]
                cbuf = Rot([sb("d2_c%d" % i, [128, T2], F32, ph) for i in range(5)])
                gbuf = Rot([sb("d2_gt%d" % i, [128, T2], BF16, ph) for i in range(3)])
                aT = Rot([sb("d2_aT%d" % i, [128, NFB, T2], BF16, ph) for i in range(2)])
                carry = sb("d2_carry", [128, 44, 2], F32, ph); Bcarry = [Buf() for _ in range(44)]
                ptr = ph.enter_context(nc.psum_tensor(uq("d2_ptr"), [128, 1024], BF16)); Bptr = Buf()
                psx = [ph.enter_context(nc.psum_tensor(uq("d2_ps%d" % i), [128, 512], F32)) for i in range(6)]
                Bpsx = [Buf() for _ in range(6)]
                pi = 0
                last = (l == L - 1)
                pend_st = []
                for st_i in range(NTOK // T2):
                    seq = (st_i * T2) // S
                    if (st_i * T2) % S == 0:
                        mset("pool", carry[:, :, :], 0.0, w=Bcarry)
                    hTt, BhT = hT.next()
                    xs = []
                    for u in range(T2 // 128):
                        row0 = st_i * T2 + u * 128
                        x_t, Bxt = xt.next()
                        xs.append((x_t, Bxt))
                        h_b, Bhb = hb.next()
                        s_s, Bs = ssr.next()
                        dma("sp", x_t[:], XA.ap()[row0:row0 + 128, :], r=[xbuf("xa", row0 // 128)], w=[Bxt])
                        act(sqj[:], x_t[:], AF.Square, r=[Bxt], w=[Bs, Bsq], accum=s_s[:, 0:1])
                        rstd_from_ss(s_s[:, 1:2], s_s[:, 0:1], D, r=[Bs], w=[Bs])
                        ts("dve", h_b[:], x_t[:], s_s[:, 1:2], None, ALU.mult, r=[Bxt, Bs], w=[Bhb])
                        for kc in range(KC):
                            tr(ptr[:, kc * 128:(kc + 1) * 128], h_b[:, kc * 128:(kc + 1) * 128], ident_b[:],
                               r=[Bhb, B_const], w=[Bptr])
                        cp("act", hTt[:, :, u * 128:(u + 1) * 128], ptr[:, :].rearrange("p (k t) -> p k t", k=KC),
                           r=[Bptr], w=[BhT])
                    for (x_p, Bx_p, r_p) in pend_st:
                        dma("sp", XB.ap()[r_p:r_p + 128, :], x_p[:], r=[Bx_p], w=[xbuf("xb", r_p // 128)])
                    pend_st = []
                    a_t, Ba = aT.next()
                    pend_g = None

                    def gate_mul(a_t_, Ba_, fb_, cg, Bcg, cv_, Bcv):
                        g_t, Bgt = gbuf.next()
                        act(g_t[:], cg[:], AF.Silu, r=[Bcg], w=[Bgt])
                        tt("dve", a_t_[:, fb_, :], cv_[:], g_t[:], ALU.mult, r=[Bcv, Bgt], w=[Ba_])
                    for fb in range(NFB):
                        cs = []
                        for which, nb in enumerate((fb, fb + NFB)):
                            b = pi % 6; pi += 1
                            for kc in range(KC):
                                mm(psx[b][:, 0:T2], wu[:, kc, nb * 128:(nb + 1) * 128], hTt[:, kc, :], kc == 0, kc == KC - 1,
                                   r=[BhT, Bwu], w=[Bpsx[b]])
                            u_t, (Bu, Buh) = ubuf.next()
                            c_t, Bc = cbuf.next()
                            cp("pool", u_t[:, 0:2], carry[:, nb, :], r=[Bcarry[nb]], w=[Buh])
                            cp("act", u_t[:, 2:T2 + 2], psx[b][:, 0:T2], r=[Bpsx[b]], w=[Bu])
                            act(c_t[:], psx[b][:, 0:T2], AF.Identity, r=[Bpsx[b], Bcw], w=[Bc],
                                scale=cw[:, 2, nb:nb + 1], bias=cb[:, nb:nb + 1])
                            cp("pool", carry[:, nb, :], u_t[:, T2:T2 + 2], r=[Bu], w=[Bcarry[nb]])
                            cs.append((c_t, Bc, u_t, Bu, Buh, nb))
                        for tap in (1, 0):
                            for (c_t, Bc, u_t, Bu, Buh, nb) in cs:
                                stt(c_t[:], u_t[:, tap:T2 + tap], cw[:, tap, nb:nb + 1], c_t[:], ALU.mult, ALU.add,
                                    r=[Bu, Buh, Bcw], w=[Bc])
                        if pend_g is not None:
                            gate_mul(*pend_g)
                        pend_g = (a_t, Ba, fb, cs[0][0], cs[0][1], cs[1][0], cs[1][1])
                    gate_mul(*pend_g)
                    pend_g = None
                    for u in range(T2 // 128):
                        x_t, Bxt = xs[u]
                        row0 = st_i * T2 + u * 128
                        for nch in range(2):
                            b = pi % 6; pi += 1
                            for fb in range(NFB):
                                mm(psx[b][:, :], a_t[:, fb, u * 128:(u + 1) * 128], wd[:, fb, nch * 512:(nch + 1) * 512],
                                   fb == 0, fb == NFB - 1, r=[Ba, Bwd], w=[Bpsx[b]])
                            tt("dve", x_t[:, nch * 512:(nch + 1) * 512], psx[b][:, :], x_t[:, nch * 512:(nch + 1) * 512],
                               ALU.add, r=[Bpsx[b]], w=[Bxt])
                        pend_st.append((x_t, Bxt, row0))
                for (x_p, Bx_p, r_p) in pend_st:
                    dma("sp", XB.ap()[r_p:r_p + 128, :], x_p[:], r=[Bx_p], w=[xbuf("xb", r_p // 128)])
                P.flush()

        with ExitStack() as ph:
            gfin = sb("fn_g", [128, D], F32, ph); Bg = Buf()
            xt = Rot([sb("fn_x%d" % i, [128, D], F32, ph) for i in range(3)])
            yt = Rot([sb("fn_y%d" % i, [128, D], F32, ph) for i in range(3)])
            sqj = sb("fn_sq", [128, D], BF16, ph); Bsq = Buf()
            ssr = Rot([sb("fn_ss%d" % i, [128, 2], F32, ph) for i in range(3)])
            dma("sp", gfin[:], AP(norm_final, 0, [[0, 128], [1, D]]), w=[Bg])
            for t in range(NTOK // 128):
                x_t, Bxt = xt.next()
                y_t, Byt = yt.next()
                s_s, Bs = ssr.next()
                dma("sp", x_t[:], XB.ap()[t * 128:(t + 1) * 128, :], r=[xbuf("xb", t)], w=[Bxt])
                act(sqj[:], x_t[:], AF.Square, r=[Bxt], w=[Bs, Bsq], accum=s_s[:, 0:1])
                rstd_from_ss(s_s[:, 1:2], s_s[:, 0:1], D, r=[Bs], w=[Bs])
                stt(y_t[:], x_t[:], s_s[:, 1:2], gfin[:], ALU.mult, ALU.mult, r=[Bxt, Bs, Bg], w=[Byt])
                dma("pool", y_out.ap()[t * 128:(t + 1) * 128, :], y_t[:], r=[Byt], w=[Buf()])
            P.flush()
        print("ops", P.nops, "waits", P.nwaits, flush=True)
    return nc


S_FULL = 4096
NSEQ_FULL = 2
L_FULL = 4
NCORES = 8
_CACHE = {}


def kernel(**inputs):
    f32 = lambda a: np.ascontiguousarray(np.asarray(a, dtype=np.float32))
    x = f32(inputs["x"])
    mem = f32(inputs["mem"])
    B = x.shape[0]
    per = B // NCORES
    consts = host_consts()
    shared = {k: f32(inputs[k]) for k in ("rel_bias", "norm_mix", "w_in", "lam_q1", "lam_k1", "lam_q2", "lam_k2",
                                          "subln", "w_out", "norm_mem", "norm_memkv", "w_mq", "w_mkv", "w_mo",
                                          "norm_ffn", "w_up", "conv_w", "conv_b", "w_down")}
    shared["norm_final"] = f32(inputs["norm_final"]).reshape(1, D)
    shared.update(consts)
    key = "full"
    if key not in _CACHE:
        _CACHE[key] = build(S_FULL, per, L_FULL, 256)
    nc = _CACHE[key]
    in_maps = []
    for c in range(NCORES):
        m = dict(shared)
        m["x"] = np.ascontiguousarray(x[c * per:(c + 1) * per].reshape(per * S_FULL, D))
        m["mem"] = np.ascontiguousarray(mem[c * per:(c + 1) * per].reshape(per * MEM, D))
        in_maps.append(m)
    res = run_bass_kernel_spmd(nc, in_maps, core_ids=list(range(NCORES)))
    out = np.concatenate([np.asarray(r["y"]).reshape(per, S_FULL, D) for r in res.results], axis=0)
    return out.astype(np.float32)
```
